# Optimizing a Trainium2 kernel written in Bass

```python
import jax, jax.numpy as jnp
from jax import lax
import numpy as np

D_MODEL = 1024
BATCH = 2
SEQ = 8192
DEPTH = 1

D_CONV = D_MODEL
CONV_A_WIDTH = 3
GDN_HEADS = D_MODEL // 128
HEAD_K = 128
HEAD_V = 128
KEY_DIM = GDN_HEADS * HEAD_K
VAL_DIM = GDN_HEADS * HEAD_V
CONV_QKV_WIDTH = 5
N_DIR = 2
CHUNK = 64
NORM_EPS = 1e-6
L2_EPS = 1e-6
SPLITS = (D_CONV, D_CONV, D_CONV, D_CONV,
          KEY_DIM, KEY_DIM, VAL_DIM, VAL_DIM,
          N_DIR * GDN_HEADS, N_DIR * GDN_HEADS,
          D_MODEL, D_MODEL)
N_IN = sum(SPLITS)

kernel_name = "hybrid_conv_gdn_bidir_adaln"


def rmsnorm(x, w):
    xf = x.astype(jnp.float32)
    y = xf * lax.rsqrt(jnp.mean(xf * xf, axis=-1, keepdims=True) + NORM_EPS)
    return (y * w.astype(jnp.float32)).astype(x.dtype)


def l2norm(x):
    return x * lax.rsqrt(jnp.sum(x * x, axis=-1, keepdims=True) + L2_EPS)


def dwconv_centred(x, w):
    k = w.shape[0]
    return lax.conv_general_dilated(
        x, w[:, None, :].astype(x.dtype), window_strides=(1,),
        padding=[(k // 2, k // 2)], dimension_numbers=('NWC', 'WIO', 'NWC'),
        feature_group_count=x.shape[-1])


def gated_delta_chunked(q, k, v, g, beta):
    bsz, nh, slen, dk = q.shape
    dv = v.shape[-1]
    nc = slen // CHUNK
    q = q * (dk ** -0.5)
    q = q.reshape(bsz, nh, nc, CHUNK, dk)
    k = k.reshape(bsz, nh, nc, CHUNK, dk)
    v = v.reshape(bsz, nh, nc, CHUNK, dv)
    beta = beta.reshape(bsz, nh, nc, CHUNK)
    g = jnp.cumsum(g.reshape(bsz, nh, nc, CHUNK), axis=-1)
    idx = jnp.arange(CHUNK)
    incl = idx[:, None] >= idx[None, :]
    strict = idx[:, None] > idx[None, :]
    diff = g[..., :, None] - g[..., None, :]
    decay = jnp.where(incl, jnp.exp(jnp.where(incl, diff, 0.0)), 0.0)
    kk = jnp.einsum('bhnid,bhnjd->bhnij', k, k)
    lower = jnp.where(strict, beta[..., :, None] * kk * decay, 0.0)
    eye = jnp.eye(CHUNK, dtype=q.dtype)
    tmat = lax.linalg.triangular_solve(eye + lower, jnp.broadcast_to(eye, lower.shape),
                                       left_side=True, lower=True, unit_diagonal=True)
    eg = jnp.exp(g)
    u = jnp.einsum('bhnij,bhnjd->bhnid', tmat, v * beta[..., None])
    w = jnp.einsum('bhnij,bhnjd->bhnid', tmat, k * (beta * eg)[..., None])
    a_qk = jnp.einsum('bhnid,bhnjd->bhnij', q, k) * decay
    q_dec = q * eg[..., None]
    g_last = g[..., -1]
    k_dec = k * jnp.exp(g_last[..., None] - g)[..., None]
    gl = jnp.exp(g_last)
    xs = tuple(jnp.moveaxis(t, 2, 0) for t in (q_dec, k_dec, u, w, a_qk, gl))

    def step(state, inp):
        qd, kd, uu, ww, aqk, gll = inp
        v_new = uu - jnp.einsum('bhcd,bhde->bhce', ww, state)
        o = jnp.einsum('bhcd,bhde->bhce', qd, state) + jnp.einsum('bhij,bhje->bhie', aqk, v_new)
        state = state * gll[..., None, None] + jnp.einsum('bhcd,bhce->bhde', kd, v_new)
        return state, o

    state0 = jnp.zeros((bsz, nh, dk, dv), jnp.float32)
    _, o = lax.scan(step, state0, xs)
    return jnp.moveaxis(o, 0, 2).reshape(bsz, nh, slen, dv)


def setup_inputs(seed: int = 0) -> dict:
    key = jax.random.key(seed)
    ks = jax.random.split(key, 16)
    f32 = jnp.float32
    x = jax.random.normal(ks[0], (BATCH, SEQ, D_MODEL), f32)
    c = jax.random.normal(ks[1], (BATCH, D_MODEL), f32)
    w_ada = jax.random.normal(ks[2], (DEPTH, D_MODEL, 3 * D_MODEL), f32) * D_MODEL ** -0.5
    b_ada = jax.random.normal(ks[3], (DEPTH, 3 * D_MODEL), f32) * 0.02
    norm_w = 1.0 + 0.01 * jax.random.normal(ks[4], (DEPTH, D_MODEL), f32)
    w_in = jax.random.normal(ks[5], (DEPTH, D_MODEL, N_IN), f32) * D_MODEL ** -0.5
    conv_a_w = jax.random.normal(ks[6], (DEPTH, CONV_A_WIDTH, D_CONV), f32) * CONV_A_WIDTH ** -0.5
    conv_qkv_w = jax.random.normal(ks[7], (DEPTH, CONV_QKV_WIDTH, 2 * KEY_DIM + VAL_DIM), f32) * CONV_QKV_WIDTH ** -0.5
    a_log = jnp.log(jax.random.uniform(ks[8], (DEPTH, N_DIR, GDN_HEADS), f32, 1.0, 16.0))
    dt = jnp.exp(jax.random.uniform(ks[9], (DEPTH, N_DIR, GDN_HEADS), f32,
                                    float(np.log(1e-3)), float(np.log(1e-1))))
    dt_bias = dt + jnp.log(-jnp.expm1(-dt))
    gdn_norm_w = 1.0 + 0.01 * jax.random.normal(ks[10], (DEPTH, HEAD_V), f32)
    w_pa = jax.random.normal(ks[11], (DEPTH, D_CONV, D_MODEL), f32) * D_CONV ** -0.5
    w_pb = jax.random.normal(ks[12], (DEPTH, VAL_DIM, D_MODEL), f32) * VAL_DIM ** -0.5
    w_o = jax.random.normal(ks[13], (DEPTH, D_MODEL, D_MODEL), f32) * D_MODEL ** -0.5
    final_norm_w = 1.0 + 0.01 * jax.random.normal(ks[14], (D_MODEL,), f32)
    return {"x": x, "c": c, "w_ada": w_ada, "b_ada": b_ada, "norm_w": norm_w, "w_in": w_in,
            "conv_a_w": conv_a_w, "conv_qkv_w": conv_qkv_w, "a_log": a_log, "dt_bias": dt_bias,
            "gdn_norm_w": gdn_norm_w, "w_pa": w_pa, "w_pb": w_pb, "w_o": w_o,
            "final_norm_w": final_norm_w}


def reference(x, c, w_ada, b_ada, norm_w, w_in, conv_a_w, conv_qkv_w, a_log, dt_bias,
              gdn_norm_w, w_pa, w_pb, w_o, final_norm_w):
    bsz, slen, _ = x.shape
    dt_ = x.dtype
    split_pts = [int(p) for p in np.cumsum(SPLITS)[:-1]]
    for l in range(DEPTH):
        mod = jax.nn.silu(c) @ w_ada[l] + b_ada[l]
        shift, scale, gate = jnp.split(mod, 3, axis=-1)
        h = rmsnorm(x, norm_w[l]) * (1.0 + scale[:, None, :]) + shift[:, None, :]

        proj = h @ w_in[l]
        (a_bg, a_cg, a_x, a_z, q, k, v, z_b, a_raw, b_raw,
         gate_a_raw, gate_b_raw) = jnp.split(proj, split_pts, axis=-1)

        ya = a_cg * dwconv_centred(a_bg * a_x, conv_a_w[l])
        ya = ya * jax.nn.silu(a_z)
        ya = ya @ w_pa[l]

        qkv = jax.nn.silu(dwconv_centred(jnp.concatenate([q, k, v], axis=-1), conv_qkv_w[l]))
        q, k, v = jnp.split(qkv, [KEY_DIM, 2 * KEY_DIM], axis=-1)
        q = l2norm(q.astype(jnp.float32).reshape(bsz, slen, GDN_HEADS, HEAD_K))
        k = l2norm(k.astype(jnp.float32).reshape(bsz, slen, GDN_HEADS, HEAD_K))
        v = v.astype(jnp.float32).reshape(bsz, slen, GDN_HEADS, HEAD_V)
        a_raw = a_raw.astype(jnp.float32).reshape(bsz, slen, N_DIR, GDN_HEADS)
        b_raw = b_raw.astype(jnp.float32).reshape(bsz, slen, N_DIR, GDN_HEADS)
        g = -jnp.exp(a_log[l].astype(jnp.float32)) * jax.nn.softplus(a_raw + dt_bias[l].astype(jnp.float32))
        beta = jax.nn.sigmoid(b_raw)
        q2 = jnp.concatenate([q, jnp.flip(q, 1)], axis=2)
        k2 = jnp.concatenate([k, jnp.flip(k, 1)], axis=2)
        v2 = jnp.concatenate([v, jnp.flip(v, 1)], axis=2)
        g2 = jnp.concatenate([g[:, :, 0], jnp.flip(g[:, :, 1], 1)], axis=2)
        b2 = jnp.concatenate([beta[:, :, 0], jnp.flip(beta[:, :, 1], 1)], axis=2)
        o = gated_delta_chunked(jnp.swapaxes(q2, 1, 2), jnp.swapaxes(k2, 1, 2),
                                jnp.swapaxes(v2, 1, 2), jnp.swapaxes(g2, 1, 2),
                                jnp.swapaxes(b2, 1, 2))
        o = o[:, :GDN_HEADS] + jnp.flip(o[:, GDN_HEADS:], axis=2)
        o = jnp.swapaxes(o, 1, 2)
        o = o * lax.rsqrt(jnp.mean(o * o, axis=-1, keepdims=True) + NORM_EPS) * gdn_norm_w[l].astype(jnp.float32)
        z = z_b.astype(jnp.float32).reshape(bsz, slen, GDN_HEADS, HEAD_V)
        yb = (o * jax.nn.silu(z)).reshape(bsz, slen, VAL_DIM).astype(dt_)
        yb = yb @ w_pb[l]

        merged = jax.nn.sigmoid(gate_a_raw) * ya + jax.nn.sigmoid(gate_b_raw) * yb
        x = x + gate[:, None, :] * (merged @ w_o[l])
    return rmsnorm(x, final_norm_w)
```

```python
import bisect
import contextlib

import numpy as np

import concourse.bass as bass
import concourse.mybir as mybir
from concourse.bass_utils import run_bass_kernel_spmd

F32 = mybir.dt.float32
BF16 = mybir.dt.bfloat16
ALU = mybir.AluOpType
AF = mybir.ActivationFunctionType
AX = mybir.AxisListType

ENGS = ("pe", "dve", "act", "pool", "sp")
SEM_EPOCH = 30000
NEG = -1.0e9
NCONST = 20
C_ID, C_ONE, C_MLOW, C_NEG, C_STRICT, C_J, C_MM, C_MMT = 0, 1, 2, 3, 4, 5, 6, 13
LEVELS = (1, 2, 4, 8, 16, 32, 64)


class Cfg:
    def __init__(self, DM=1024, SEG=2048):
        self.DM = DM
        self.NH = DM // 128
        self.KD = DM // 128
        self.SEG = SEG
        self.NT = SEG // 128
        self.NG = self.NT // 4
        self.SEQ = SEG * 4
        self.NIN = 8 * DM + 4 * self.NH + 2 * DM
        self.C_BG, self.C_CG, self.C_AX, self.C_AZ = 0, DM, 2 * DM, 3 * DM
        self.C_Q, self.C_K, self.C_V, self.C_Z = 4 * DM, 5 * DM, 6 * DM, 7 * DM
        self.C_A = 8 * DM
        self.C_B = 8 * DM + 2 * self.NH
        self.C_GA = 8 * DM + 4 * self.NH
        self.C_GB = self.C_GA + DM


class _Rec:
    def __getattr__(self, name):
        return lambda *a, **k: (name, a, k)


class Prog:
    def __init__(self, nc):
        self.nc = nc
        self.ops = []
        self.lastw = {}
        self.rd_eng = {}
        self.rd_dma = {}
        self.bar_idx = None

    def op(self, eng, fn, reads=(), writes=(), dma=None):
        idx = len(self.ops)
        pr = [k for k in reads if k.startswith("pb")]
        if pr:
            reads = [k for k in reads if not k.startswith("pb")]
            writes = list(writes) + [k for k in pr if k not in writes]
        deps = set()
        for k in reads:
            if k in self.lastw:
                deps.add(self.lastw[k])
        for k in writes:
            if k in self.lastw:
                deps.add(self.lastw[k])
            deps.update(self.rd_eng.get(k, ()))
            deps.update(self.rd_dma.get(k, ()))
        for k in writes:
            self.lastw[k] = idx
            self.rd_eng[k] = []
            self.rd_dma[k] = []
        for k in reads:
            if k in writes:
                continue
            if dma is not None:
                self.rd_dma.setdefault(k, []).append(idx)
            else:
                self.rd_eng.setdefault(k, []).append(idx)
        if self.bar_idx is not None:
            deps.add(self.bar_idx)
        deps.discard(idx)
        self.ops.append(dict(eng=eng, call=fn(_Rec()), deps=deps, dma=dma, needed=False))
        return idx

    def barrier(self, bar_t, cf, pbank, dram_src):
        allk = [k for k in self.lastw.keys()]
        self.bar_idx = None
        self.bar_idx = self.op("dve", lambda e: e.memset(bar_t[:, 0:1], 0.0), writes=allk + ["bar_dve"])
        self.op("act", lambda e: e.memzero(bar_t[:, 1:2]), reads=["bar_dve"], writes=["bar_act"])
        self.op("pool", lambda e: e.memset(bar_t[:, 2:3], 0.0), reads=["bar_dve"], writes=["bar_pool"])
        self.op("pe", lambda e: e.matmul(out=pbank[0:1, 0:1], lhsT=cf[0:1, 1, 0:1], rhs=cf[0:1, 1, 0:1], start=True, stop=True),
                reads=["bar_dve", "cf"], writes=["pbR2"])
        self.op("sp", lambda e: e.dma_start(out=bar_t[0:1, 4:8], in_=dram_src), reads=["bar_dve"], writes=["bar_sp"], dma="bar")

    def schedule(self, final_wait_ops):
        import heapq
        ops = self.ops
        n = len(ops)

        def cost(o):
            name, a, k = o["call"]
            out = k.get("out", a[0] if a else None)
            try:
                shp = out.shape
                free = 1
                for d in shp[1:]:
                    free *= d
                nbytes = free * shp[0] * (2 if out.dtype == BF16 else 4)
            except Exception:
                free, nbytes = 128, 65536
            if o["dma"] is not None:
                return 2000.0 + nbytes / 150.0
            e = o["eng"]
            aps = [v for v in list(a) + list(k.values()) if hasattr(v, "dtype") and hasattr(v, "shape")]
            in_psum = lambda v: str(v.name).startswith("pb")
            if e == "pe":
                f32 = any(v.dtype == F32 for v in aps if not in_psum(v))
                base = 91.0 if free <= 128 else 64.0 + 0.40 * free
                return base * (2.0 if f32 else 1.0)
            if e == "dve":
                slow = any((v.dtype == F32) or in_psum(v) for v in aps)
                return (100.0 + 1.0 * free) if slow else (80.0 + 0.5 * free)
            if e == "act":
                return 200.0 + 0.7 * free
            return 150.0 + 2.0 * free

        children = [[] for _ in range(n)]
        indeg = [0] * n
        for i, o in enumerate(ops):
            indeg[i] = len(o["deps"])
            for d in o["deps"]:
                children[d].append(i)
        LAT = 250.0
        costs = [cost(o) for o in ops]
        bl = [0.0] * n
        for i in range(n - 1, -1, -1):
            m = 0.0
            for ch in children[i]:
                if bl[ch] > m:
                    m = bl[ch]
            bl[i] = costs[i] + m
        prio = [(-bl[i], i) for i in range(n)]
        ready_t = [0.0] * n
        finish = [0.0] * n
        start = [0.0] * n
        free_at = {e: 0.0 for e in ENGS}
        pending = {e: [] for e in ENGS}
        avail = {e: [] for e in ENGS}
        for i, o in enumerate(ops):
            if indeg[i] == 0:
                heapq.heappush(pending[o["eng"]], (0.0, i))
        done = 0
        while done < n:
            best = None
            for e in ENGS:
                pe_, av_ = pending[e], avail[e]
                while pe_ and pe_[0][0] <= free_at[e]:
                    heapq.heappush(av_, prio[heapq.heappop(pe_)[1]])
                if av_:
                    cand = (free_at[e], av_[0][1], e, True)
                elif pe_:
                    cand = (pe_[0][0], pe_[0][1], e, False)
                else:
                    continue
                if best is None or cand[:2] < best[:2]:
                    best = cand
            assert best is not None, "scheduler deadlock"
            t0, i, e, from_av = best
            if from_av:
                heapq.heappop(avail[e])
            else:
                heapq.heappop(pending[e])
            o = ops[i]
            c = costs[i]
            start[i] = t0
            if o["dma"] is not None:
                free_at[e] = t0 + 70.0
            else:
                free_at[e] = t0 + c
            finish[i] = t0 + c
            done += 1
            for ch in children[i]:
                rt = finish[i] + (0.0 if (ops[ch]["eng"] == e and o["dma"] is None) else LAT)
                if rt > ready_t[ch]:
                    ready_t[ch] = rt
                indeg[ch] -= 1
                if indeg[ch] == 0:
                    heapq.heappush(pending[ops[ch]["eng"]], (ready_t[ch], ch))
        order = sorted(range(n), key=lambda i: (start[i], i))
        remap = {old: new for new, old in enumerate(order)}
        new_ops = []
        for old in order:
            o = ops[old]
            o["deps"] = set(remap[d] for d in o["deps"])
            new_ops.append(o)
        self.ops = new_ops
        self.est_ns = max(finish) if n else 0.0
        return [remap[d] for d in final_wait_ops]

    def emit(self, final_wait_ops=(), reorder=True):
        nc = self.nc
        if reorder:
            final_wait_ops = self.schedule(list(final_wait_ops))
        ops = self.ops
        for o in ops:
            best = {}
            keep = set()
            for d in o["deps"]:
                od = ops[d]
                if od["dma"] is not None:
                    keep.add(d)
                elif d > best.get(od["eng"], -1):
                    best[od["eng"]] = d
            keep.update(best.values())
            o["deps"] = keep

        def pe_pe(o, od):
            return o["eng"] == "pe" and od["eng"] == "pe" and od["dma"] is None and o["dma"] is None

        for o in ops:
            for d in o["deps"]:
                if not pe_pe(o, ops[d]):
                    ops[d]["needed"] = True
        for d in final_wait_ops:
            ops[d]["needed"] = True
        with contextlib.ExitStack() as st:
            eng_sems, eng_cnt = {}, {}
            dma_sems, dma_cnt, dma_order = {}, {}, {}
            for i, o in enumerate(ops):
                if o["dma"] is not None:
                    tag = o["dma"]
                    if tag not in dma_sems:
                        dma_sems[tag] = st.enter_context(nc.semaphore("d_" + tag))
                        dma_cnt[tag] = 0
                        dma_order[tag] = []
                    dma_cnt[tag] += 1
                    dma_order[tag].append(i)
                    o["ev"] = (dma_sems[tag], 16 * dma_cnt[tag], tag)
                elif o["needed"]:
                    e = o["eng"]
                    if e not in eng_sems or eng_cnt[e] >= SEM_EPOCH:
                        eng_sems[e] = st.enter_context(nc.semaphore("e_%s_%d" % (e, i)))
                        eng_cnt[e] = 0
                    eng_cnt[e] += 1
                    o["ev"] = (eng_sems[e], eng_cnt[e], None)
                else:
                    o["ev"] = None
            per_eng = {e: [] for e in ENGS}
            for i, o in enumerate(ops):
                per_eng[o["eng"]].append(i)

            def dep_event(i_waiter, d):
                sem, val, tag = ops[d]["ev"]
                return sem, val

            def run_engine(ename, engobj, extra_final=None):
                waited = {}
                for i in per_eng[ename]:
                    o = ops[i]
                    for d in sorted(o["deps"]):
                        if pe_pe(o, ops[d]):
                            continue
                        sem, val = dep_event(i, d)
                        if waited.get(id(sem), 0) >= val:
                            continue
                        waited[id(sem)] = val
                        engobj.wait_ge(sem, val)
                    name, a, k = o["call"]
                    inst = getattr(engobj, name)(*a, **k)
                    if o["ev"] is not None:
                        sem, val, tag = o["ev"]
                        inst.then_inc(sem, 16 if tag is not None else 1)
                if extra_final:
                    for d in extra_final:
                        sem, val = dep_event(len(ops), d)
                        engobj.wait_ge(sem, val)

            with nc.Block() as block:
                @block.tensor
                def _(e):
                    run_engine("pe", e)

                @block.vector
                def _(e):
                    run_engine("dve", e)

                @block.scalar
                def _(e):
                    run_engine("act", e)

                @block.gpsimd
                def _(e):
                    run_engine("pool", e)

                @block.sync
                def _(e):
                    run_engine("sp", e, extra_final=final_wait_ops)


def blocks(n, step=512):
    return [(a, min(a + step, n)) for a in range(0, n, step)]


def interleave(gens):
    gens = list(gens)
    while gens:
        nxt = []
        for g in gens:
            try:
                next(g)
                nxt.append(g)
            except StopIteration:
                pass
        gens = nxt


def build_program(cfg, debug=False):
    DM, NH, KD, SEG, NT, NG = cfg.DM, cfg.NH, cfg.KD, cfg.SEG, cfg.NT, cfg.NG
    SP4 = SEG + 4
    nc = bass.Bass("TRN2", target_bir_lowering=False)

    def din(name, shape, dt=F32):
        return nc.dram_tensor(name, list(shape), dt, kind="ExternalInput").ap()

    xs = din("xs", [5, SP4, DM])
    c_fm = din("c_fm", [128, KD])
    normw_fm = din("normw_fm", [128, KD])
    w_ada = din("w_ada", [DM, 3 * DM])
    b_ada = din("b_ada", [1, 3 * DM])
    w_in = din("w_in", [DM, cfg.NIN])
    w_ab = din("w_ab", [5, DM, 2 * NH])
    cw_fm = din("cw_fm", [128, 5 * 3 * NH * 5])
    cwa_fm = din("cwa_fm", [128, KD * 3])
    alog_bc = din("alog_bc", [128, 5 * NH])
    dtb_bc = din("dtb_bc", [128, 5 * NH])
    flags_bc = din("flags_bc", [128, 32])
    constm = din("constm", [128, NCONST * 128])
    gnw_bc = din("gnw_bc", [128, 128])
    fnw_bc = din("fnw_bc", [128, DM])
    w_pa = din("w_pa", [DM, DM])
    w_pb = din("w_pb", [DM, DM])
    w_o = din("w_o", [DM, DM])
    y = nc.dram_tensor("y", [SEG, DM], F32, kind="ExternalOutput").ap()
    oscr = nc.dram_tensor("oscr", [2, SEG, DM], F32).ap()
    dbg = None
    if debug:
        dbg = nc.dram_tensor("dbg", [128, 4 * 128], F32, kind="ExternalOutput").ap()

    P = Prog(nc)
    F_LO = lambda p: 2 * p
    F_HI = lambda p: 2 * p + 1
    F_CARRY, F_F, F_B = 10, 13, 16

    with contextlib.ExitStack() as st0:
        def sb(st, name, shape, dt):
            return st.enter_context(nc.sbuf_tensor(name, list(shape), dt))

        pb = {}
        for nm in ("A", "B", "G", "K", "Y", "R1", "R2"):
            pb[nm] = st0.enter_context(nc.psum_tensor("pb" + nm, [128, 512], F32))
        pbT = st0.enter_context(nc.psum_tensor("pbT", [128, 8, 128], BF16))

        cf = sb(st0, "cf", [128, 6, 128], F32)
        cb = sb(st0, "cb", [128, NCONST, 128], BF16)
        flg = sb(st0, "flg", [128, 32], F32)
        epsc = sb(st0, "epsc", [128, 1], F32)
        bar_t = sb(st0, "bar_t", [128, 8], F32)
        hT = sb(st0, "hT", [128, KD, SP4], BF16)
        shsc = sb(st0, "shsc", [128, 2 * KD], F32)
        scfm = sb(st0, "scfm", [128, KD], F32)
        gate_bc = sb(st0, "gate_bc", [128, DM], F32)
        xt = sb(st0, "xt", [128, DM], F32)
        xn = sb(st0, "xn", [128, DM], BF16)
        junk = xn
        ss = sb(st0, "ss", [128, 1], F32)
        rstd = sb(st0, "rstd", [128, 1], F32)
        wst = [sb(st0, "wst%d" % i, [128, KD, 128], F32) for i in range(2)]
        wbf = [sb(st0, "wbf%d" % i, [128, KD, 128], BF16) for i in range(4)]

        ident_bf = cb[:, C_ID, :]
        ones_bf = cb[:, C_ONE, :]
        ones_f = cf[:, C_ONE, :]

        def bc4(ap2d):
            return ap2d.unsqueeze(1).to_broadcast([128, 4, 128])

        P.op("sp", lambda e: e.dma_start(out=cf[:, :, :].rearrange("p a b -> p (a b)"), in_=constm[:, 0:6 * 128]),
             writes=["cf"], dma="cf")
        P.op("sp", lambda e: e.dma_start(out=flg[:, :], in_=flags_bc[:, :]), writes=["flg"], dma="flg")
        P.op("dve", lambda e: e.memset(epsc[:, :], 1e-6), writes=["epsc"])

        with contextlib.ExitStack() as st:
            cfull = sb(st, "cfull", [128, NCONST, 128], F32)
            P.op("sp", lambda e: e.dma_start(out=cfull[:, :, :].rearrange("p a b -> p (a b)"), in_=constm[:, :]),
                 writes=["cfull"], dma="cfull")
            P.op("dve", lambda e: e.tensor_copy(out=cb[:, :, :], in_=cfull[:, :, :]), reads=["cfull"], writes=["cb"])
            cT = sb(st, "cT", [128, KD], F32)
            scs = sb(st, "scs", [128, KD], F32)
            nwf = sb(st, "nwf", [128, KD], F32)
            wa = sb(st, "wa", [128, 3 * DM], F32)
            bada = sb(st, "bada", [1, 3 * DM], F32)
            modrow = sb(st, "modrow", [1, 3 * DM], F32)
            P.op("sp", lambda e: e.dma_start(out=cT[:, :], in_=c_fm[:, :]), writes=["cT"], dma="cT")
            P.op("sp", lambda e: e.dma_start(out=nwf[:, :], in_=normw_fm[:, :]), writes=["nwf"], dma="nwf")
            P.op("sp", lambda e: e.dma_start(out=bada[:, :], in_=b_ada[:, :]), writes=["bada"], dma="bada")
            P.op("act", lambda e: e.activation(out=scs[:, :], in_=cT[:, :], func=AF.Silu), reads=["cT"], writes=["scs"])
            cblocks = blocks(3 * DM)
            banks = ["A", "B", "G", "K", "Y", "R1"]
            assert len(cblocks) <= len(banks)
            for d in range(KD):
                P.op("sp", lambda e, d=d: e.dma_start(out=wa[:, :], in_=w_ada[d * 128:(d + 1) * 128, :]),
                     writes=["wa"], dma="wa")
                for i, (c0, c1) in enumerate(cblocks):
                    P.op("pe", lambda e, d=d, i=i, c0=c0, c1=c1: e.matmul(
                        out=pb[banks[i]][0:1, 0:c1 - c0], lhsT=scs[:, d:d + 1], rhs=wa[:, c0:c1],
                        start=(d == 0), stop=(d == KD - 1)), reads=["scs", "wa"], writes=["pb" + banks[i]])
            for i, (c0, c1) in enumerate(cblocks):
                P.op("dve", lambda e, i=i, c0=c0, c1=c1: e.tensor_tensor(
                    out=modrow[0:1, c0:c1], in0=pb[banks[i]][0:1, 0:c1 - c0], in1=bada[0:1, c0:c1], op=ALU.add),
                    reads=["pb" + banks[i], "bada"], writes=["modrow"])
            for i in range(2 * KD):
                P.op("pe", lambda e, i=i: e.matmul(out=pb["A"][:, i:i + 1], lhsT=modrow[0:1, i * 128:(i + 1) * 128],
                                                   rhs=cf[0:1, C_ONE, 0:1], start=True, stop=True),
                     reads=["modrow", "cf"], writes=["pbA"])
            P.op("dve", lambda e: e.tensor_copy(out=shsc[:, :], in_=pb["A"][:, 0:2 * KD]), reads=["pbA"], writes=["shsc"])
            P.op("dve", lambda e: e.scalar_tensor_tensor(out=scfm[:, :], in0=shsc[:, KD:2 * KD], scalar=1.0, in1=nwf[:, :],
                                                         op0=ALU.add, op1=ALU.mult),
                 reads=["shsc", "nwf"], writes=["scfm"])
            for i, (c0, c1) in enumerate(blocks(DM)):
                bk = banks[1 + (i % 2)]
                P.op("pe", lambda e, bk=bk, c0=c0, c1=c1: e.matmul(
                    out=pb[bk][:, 0:c1 - c0], lhsT=cf[0:1, C_ONE, :], rhs=modrow[0:1, 2 * DM + c0:2 * DM + c1],
                    start=True, stop=True), reads=["modrow", "cf"], writes=["pb" + bk])
                P.op("act", lambda e, bk=bk, c0=c0, c1=c1: e.copy(out=gate_bc[:, c0:c1], in_=pb[bk][:, 0:c1 - c0]),
                     reads=["pb" + bk], writes=["gate_bc"])
            P.barrier(bar_t, cf, pb["R2"], flags_bc[0:1, 0:4])

        def load_w_chunk(slot, src_ap, q="sp", cast_eng="dve"):
            ws_ = slot % 2
            P.op(q, lambda e: e.dma_start(out=wst[ws_][:, :, :], in_=src_ap.rearrange("(k p) c -> p k c", p=128)),
                 writes=["wst%d" % ws_], dma="wst%d" % ws_)
            P.op(cast_eng, lambda e: (e.copy(out=wbf[slot][:, :, :], in_=wst[ws_][:, :, :]) if cast_eng == "act"
                                      else e.tensor_copy(out=wbf[slot][:, :, :], in_=wst[ws_][:, :, :])),
                 reads=["wst%d" % ws_], writes=["wbf%d" % slot])

        def stage_a(p):
            for t in range(NT + 1):
                rows = 128 if t < NT else 4
                P.op("sp", lambda e, t=t, rows=rows: e.dma_start(out=xt[0:rows, :], in_=xs[p, t * 128:t * 128 + rows, :]),
                     writes=["xt"], dma="xt")
                P.op("act", lambda e, rows=rows: e.activation(out=junk[0:rows, :], in_=xt[0:rows, :], func=AF.Square,
                                                              accum_out=ss[0:rows, :]), reads=["xt"], writes=["xn", "ss"])
                P.op("dve", lambda e, rows=rows: e.tensor_scalar(out=rstd[0:rows, :], in0=ss[0:rows, :], scalar1=1.0 / DM,
                                                                 scalar2=1e-6, op0=ALU.mult, op1=ALU.add),
                     reads=["ss"], writes=["rstd"])
                P.op("act", lambda e, rows=rows: e.sqrt(out=rstd[0:rows, :], in_=rstd[0:rows, :]), reads=["rstd"], writes=["rstd"])
                P.op("dve", lambda e, rows=rows: e.reciprocal(out=rstd[0:rows, :], in_=rstd[0:rows, :]), reads=["rstd"], writes=["rstd"])
                P.op("dve", lambda e, rows=rows: e.tensor_scalar(out=xn[0:rows, :], in0=xt[0:rows, :], scalar1=rstd[0:rows, 0:1],
                                                                 scalar2=None, op0=ALU.mult), reads=["xt", "rstd"], writes=["xn"])
                for d in range(KD):
                    P.op("pe", lambda e, d=d, rows=rows: e.transpose(out=pbT[:, d, 0:rows], in_=xn[0:rows, d * 128:(d + 1) * 128],
                                                                     identity=ident_bf[0:rows, 0:rows]),
                         reads=["xn", "cb"], writes=["pbT"])
                for d in range(KD):
                    P.op("act", lambda e, d=d, t=t, rows=rows: e.activation(
                        out=hT[:, d, t * 128:t * 128 + rows], in_=pbT[:, d, 0:rows], func=AF.Identity,
                        scale=scfm[:, d:d + 1], bias=shsc[:, d:d + 1]), reads=["pbT", "scfm", "shsc"], writes=["hT"])
                yield

        with contextlib.ExitStack() as st:
            cwt = sb(st, "cwt", [128, 5, 3 * NH, 5], F32)
            alg = sb(st, "alg", [128, 5, NH], F32)
            dtb = sb(st, "dtb", [128, 5, NH], F32)
            wabf = sb(st, "wabf", [128, KD, 2 * NH], F32)
            wabb = sb(st, "wabb", [128, KD, 2 * NH], BF16)
            xa = sb(st, "xa", [128, NT, NH], F32)
            aexp = sb(st, "aexp", [128, NH], F32)
            gall2 = [sb(st, "gall%d" % i, [128, NT, NH], F32) for i in range(2)]
            ball2 = [sb(st, "ball%d" % i, [128, NT, NH], F32) for i in range(2)]
            Sst = sb(st, "Sst", [128, NH, 128], F32)
            Sf = sb(st, "Sf", [128, NH, 128], F32)
            Sb = sb(st, "Sb", [128, NH, 128], F32)
            Sbf = [sb(st, "Sbf%d" % i, [128, 128], BF16) for i in range(2)]
            pre = sb(st, "pre", [128, SP4], BF16)
            cTk2 = [sb(st, "cTk%d" % i, [128, SEG], BF16) for i in range(2)]
            cTv2 = [sb(st, "cTv%d" % i, [128, SEG], BF16) for i in range(2)]
            cTq2 = [sb(st, "cTq%d" % i, [128, SEG], BF16) for i in range(2)]
            dg = sb(st, "dg", [128, 3, 5, 128], BF16)
            sqt = sb(st, "sqt", [128, 512], BF16)
            rkT = sb(st, "rkT", [128, 512], F32)
            g4 = lambda n, dt: sb(st, n, [128, 4, 128], dt)
            kn_tok, v_tok = g4("kn_tok", BF16), g4("v_tok", BF16)
            rhs_g, GcM = g4("rhs_g", F32), g4("GcM", F32)
            dm = rhs_g
            Dm, eGR, bM, Dp, Am = g4("Dm", BF16), g4("eGR", BF16), g4("bM", BF16), g4("Dp", BF16), g4("Am", BF16)
            Rv, Rk = g4("Rv", BF16), g4("Rk", BF16)
            Lo = [g4("Lo%d" % G, BF16) for G in range(NG)]
            LoT = [g4("LoT%d" % G, BF16) for G in range(NG)]
            Lm = [g4("Lm%d" % G, BF16) for G in range(NG)]
            Um = [g4("Um%d" % G, BF16) for G in range(NG)]
            Yn = [g4("Yn%d" % G, BF16) for G in range(NG)]
            Ynp = [g4("Ynp%d" % G, BF16) for G in range(NG)]
            Xb = [[g4("X%d_0" % G, BF16)] * 2 for G in range(NG)]
            XTb = [[g4("XT%d_0" % G, BF16)] * 2 for G in range(NG)]
            Gc = [sb(st, "Gc%d" % G, [128, 4], F32) for G in range(NG)]
            eGc = [sb(st, "eGc%d" % G, [128, 4], F32) for G in range(NG)]
            kdf = [sb(st, "kdf%d" % G, [128, 4], F32) for G in range(NG)]
            bE = [sb(st, "bE%d" % G, [128, 4], F32) for G in range(NG)]
            hp = lambda n, dt: [sb(st, "%s%d" % (n, i), [128, NT, 128], dt) for i in range(2)]
            wT_all, u_all, kd_all, qdT_all, AT_all = hp("wTa", BF16), hp("ua", BF16), hp("kda", BF16), hp("qdTa", BF16), hp("ATa", BF16)
            gl_all = [sb(st, "gla%d" % i, [128, NT], F32) for i in range(2)]
            vnew = sb(st, "vnew", [128, 128], BF16)
            osb = [sb(st, "osb%d" % i, [128, 128], F32) for i in range(2)]

            if debug:
                print("GDN-phase SBUF bytes remaining/partition:", nc.sbuf_bytes_remaining)
            P.op("sp", lambda e: e.dma_start(out=cwt[:, :, :, :].rearrange("p a b c -> p (a b c)"), in_=cw_fm[:, :]),
                 writes=["cwt"], dma="cwt")
            P.op("sp", lambda e: e.dma_start(out=alg[:, :, :].rearrange("p a b -> p (a b)"), in_=alog_bc[:, :]),
                 writes=["alg"], dma="alg")
            P.op("sp", lambda e: e.dma_start(out=dtb[:, :, :].rearrange("p a b -> p (a b)"), in_=dtb_bc[:, :]),
                 writes=["dtb"], dma="dtb")
            P.op("dve", lambda e: e.memset(Sst[:, :, :], 0.0), writes=["S%d" % h for h in range(NH)])
            P.op("dve", lambda e: e.memset(Sf[:, :, :], 0.0), writes=["Sf%d" % h for h in range(NH)])
            P.op("dve", lambda e: e.memset(Sb[:, :, :], 0.0), writes=["Sb%d" % h for h in range(NH)])

            def pass_setup(p):
                gall, ball, kga, kba = gall2[p % 2], ball2[p % 2], "gall%d" % (p % 2), "ball%d" % (p % 2)
                yield from stage_a(p)
                P.op("sp", lambda e: e.dma_start(out=wabf[:, :, :], in_=w_ab[p].rearrange("(k p) c -> p k c", p=128)),
                     writes=["wabf"], dma="wabf")
                P.op("dve", lambda e: e.tensor_copy(out=wabb[:, :, :], in_=wabf[:, :, :]), reads=["wabf"], writes=["wabb"])
                for t in range(NT):
                    for d in range(KD):
                        P.op("pe", lambda e, t=t, d=d: e.matmul(
                            out=pb["K"][:, t * 2 * NH:(t + 1) * 2 * NH], lhsT=hT[:, d, 2 + 128 * t:2 + 128 * t + 128],
                            rhs=wabb[:, d, :], start=(d == 0), stop=(d == KD - 1)), reads=["hT", "wabb"], writes=["pbK"])
                abv = pb["K"][:, 0:NT * 2 * NH].rearrange("p (t c) -> p t c", c=2 * NH)
                P.op("dve", lambda e: e.tensor_tensor(out=xa[:, :, :], in0=abv[:, :, 0:NH],
                                                      in1=dtb[:, p:p + 1, :].to_broadcast([128, NT, NH]), op=ALU.add),
                     reads=["pbK", "dtb"], writes=["xa"])
                P.op("act", lambda e: e.activation(out=ball[:, :, :], in_=abv[:, :, NH:2 * NH], func=AF.Sigmoid),
                     reads=["pbK"], writes=[kba])
                P.op("dve", lambda e: e.tensor_scalar_min(out=xa[:, :, :], in0=xa[:, :, :], scalar1=30.0), reads=["xa"], writes=["xa"])
                P.op("act", lambda e: e.activation(out=xa[:, :, :], in_=xa[:, :, :], func=AF.Exp), reads=["xa"], writes=["xa"])
                P.op("act", lambda e: e.activation(out=xa[:, :, :], in_=xa[:, :, :], func=AF.Ln, bias=cf[:, C_ONE, 0:1], scale=1.0),
                     reads=["xa", "cf"], writes=["xa"])
                P.op("act", lambda e: e.activation(out=aexp[:, :], in_=alg[:, p, :], func=AF.Exp), reads=["alg"], writes=["aexp"])
                P.op("dve", lambda e: e.scalar_tensor_tensor(out=gall[:, :, :], in0=xa[:, :, :], scalar=-1.0,
                                                             in1=aexp[:, :].unsqueeze(1).to_broadcast([128, NT, NH]),
                                                             op0=ALU.mult, op1=ALU.mult), reads=["xa", "aexp"], writes=[kga])
                yield

            def prepA(p, h, par, out):
                cTk, cTv, cTq = cTk2[par], cTv2[par], cTq2[par]
                names = [("k", cfg.C_K, cTk), ("v", cfg.C_V, cTv)] + ([("q", cfg.C_Q, cTq)] if out else [])
                for wi, (nm, coff, cT_) in enumerate(names):
                    load_w_chunk(wi, w_in[:, coff + h * 128:coff + (h + 1) * 128])
                    chunk = {"k": NH + h, "v": 2 * NH + h, "q": h}[nm]
                    for tap in range(5):
                        P.op("dve", lambda e, wi=wi, tap=tap, chunk=chunk: e.tensor_scalar(
                            out=dg[:, wi, tap, :], in0=ident_bf, scalar1=cwt[:, p, chunk, tap:tap + 1], scalar2=None,
                            op0=ALU.mult), reads=["cb", "cwt"], writes=["dg%d" % wi])
                yield
                for wi, (nm, coff, cT_) in enumerate(names):
                    for bi, (c0, c1) in enumerate(blocks(SP4)):
                        for d in range(KD):
                            P.op("pe", lambda e, wi=wi, d=d, c0=c0, c1=c1: e.matmul(
                                out=pb["A"][:, 0:c1 - c0], lhsT=wbf[wi][:, d, :], rhs=hT[:, d, c0:c1],
                                start=(d == 0), stop=(d == KD - 1)), reads=["wbf%d" % wi, "hT"], writes=["pbA"])
                        if bi % 2 == 0:
                            P.op("act", lambda e, c0=c0, c1=c1: e.copy(out=pre[:, c0:c1], in_=pb["A"][:, 0:c1 - c0]),
                                 reads=["pbA"], writes=["pre"])
                        else:
                            P.op("dve", lambda e, c0=c0, c1=c1: e.tensor_copy(out=pre[:, c0:c1], in_=pb["A"][:, 0:c1 - c0]),
                                 reads=["pbA"], writes=["pre"])
                        yield
                    P.op("dve", lambda e: e.tensor_scalar(out=pre[:, 0:2], in0=pre[:, 0:2], scalar1=flg[:, F_LO(p):F_LO(p) + 1],
                                                          scalar2=None, op0=ALU.mult), reads=["pre", "flg"], writes=["pre"])
                    P.op("dve", lambda e: e.tensor_scalar(out=pre[:, SEG + 2:SP4], in0=pre[:, SEG + 2:SP4],
                                                          scalar1=flg[:, F_HI(p):F_HI(p) + 1], scalar2=None, op0=ALU.mult),
                         reads=["pre", "flg"], writes=["pre"])
                    for (o0, o1) in blocks(SEG):
                        for tap in range(5):
                            P.op("pe", lambda e, wi=wi, tap=tap, o0=o0, o1=o1: e.matmul(
                                out=pb["R2"][:, 0:o1 - o0], lhsT=dg[:, wi, tap, :], rhs=pre[:, o0 + tap:o1 + tap],
                                start=(tap == 0), stop=(tap == 4)), reads=["dg%d" % wi, "pre"], writes=["pbR2"])
                        P.op("act", lambda e, cT_=cT_, o0=o0, o1=o1: e.activation(out=cT_[:, o0:o1], in_=pb["R2"][:, 0:o1 - o0],
                                                                                   func=AF.Silu), reads=["pbR2"], writes=["cT%s%d" % (nm, par)])
                        yield
                for nm, cT_ in ([("k", cTk)] + ([("q", cTq)] if out else [])):
                    for (o0, o1) in blocks(SEG):
                        n = o1 - o0
                        P.op("act", lambda e, cT_=cT_, o0=o0, o1=o1, n=n: e.activation(out=sqt[:, 0:n], in_=cT_[:, o0:o1], func=AF.Square),
                             reads=["cT%s%d" % (nm, par)], writes=["sqt"])
                        P.op("pe", lambda e, n=n: e.matmul(out=pb["R2"][:, 0:n], lhsT=ones_bf, rhs=sqt[:, 0:n], start=True, stop=True),
                             reads=["sqt", "cb"], writes=["pbR2"])
                        P.op("act", lambda e, n=n: e.activation(out=rkT[:, 0:n], in_=pb["R2"][:, 0:n], func=AF.Ln,
                                                                bias=epsc[:, 0:1], scale=1.0), reads=["pbR2", "epsc"], writes=["rkT"])
                        P.op("act", lambda e, n=n: e.activation(out=rkT[:, 0:n], in_=rkT[:, 0:n], func=AF.Exp, scale=-0.5),
                             reads=["rkT"], writes=["rkT"])
                        if nm == "k":
                            P.op("dve", lambda e, cT_=cT_, o0=o0, o1=o1, n=n: e.tensor_tensor(
                                out=cT_[:, o0:o1], in0=cT_[:, o0:o1], in1=rkT[:, 0:n], op=ALU.mult),
                                reads=["cT%s%d" % (nm, par), "rkT"], writes=["cT%s%d" % (nm, par)])
                        else:
                            P.op("dve", lambda e, cT_=cT_, o0=o0, o1=o1, n=n: e.scalar_tensor_tensor(
                                out=cT_[:, o0:o1], in0=cT_[:, o0:o1], scalar=128.0 ** -0.5, in1=rkT[:, 0:n],
                                op0=ALU.mult, op1=ALU.mult), reads=["cT%s%d" % (nm, par), "rkT"], writes=["cT%s%d" % (nm, par)])
                        yield
            def prepB(p, h, par, out):
                cTk, cTv, cTq = cTk2[par], cTv2[par], cTq2[par]
                kck, kcv, kcq = "cTk%d" % par, "cTv%d" % par, "cTq%d" % par
                gall, ball, kga, kba = gall2[p % 2], ball2[p % 2], "gall%d" % (p % 2), "ball%d" % (p % 2)
                v4 = lambda ps: ps[:, :].rearrange("p (g f) -> p g f", f=128)
                PK, PG = pb["K"], pb["G"]

                def chain(G):
                    T0 = 4 * G
                    tc = lambda g: slice((T0 + g) * 128, (T0 + g + 1) * 128)
                    kG = "_%d" % G
                    gsl = gall[:, T0:T0 + 4, h:h + 1].to_broadcast([128, 4, 128])
                    bsl = ball[:, T0:T0 + 4, h:h + 1].to_broadcast([128, 4, 128])
                    P.op("dve", lambda e: e.tensor_tensor(out=rhs_g[:, :, :], in0=bc4(cf[:, C_MLOW, :]), in1=gsl, op=ALU.mult),
                         reads=["cf", kga], writes=["rhs_g"])
                    for g in range(4):
                        P.op("pe", lambda e, g=g: e.matmul(out=PG[:, g * 128:(g + 1) * 128], lhsT=ones_f, rhs=rhs_g[:, g, :],
                                                           start=True, stop=True), reads=["cf", "rhs_g"], writes=["pbG"])
                    for g in range(4):
                        P.op("pe", lambda e, g=g: e.matmul(out=PK[:, g:g + 1], lhsT=rhs_g[:, g, :], rhs=cf[:, C_ONE, 0:1],
                                                           start=True, stop=True), reads=["cf", "rhs_g"], writes=["pbK"])
                    P.op("dve", lambda e: e.tensor_copy(out=Gc[G][:, :], in_=PK[:, 0:4]), reads=["pbK"], writes=["Gc" + kG])
                    P.op("dve", lambda e: e.tensor_tensor(out=GcM[:, :, :], in0=bc4(cf[:, C_NEG, :]),
                                                          in1=Gc[G][:, :].unsqueeze(2).to_broadcast([128, 4, 128]), op=ALU.add),
                         reads=["cf", "Gc" + kG], writes=["GcM"])
                    P.op("dve", lambda e: e.scalar_tensor_tensor(out=dm[:, :, :], in0=v4(PG), scalar=-1.0, in1=GcM[:, :, :],
                                                                 op0=ALU.mult, op1=ALU.add), reads=["pbG", "GcM"], writes=["rhs_g"])
                    P.op("act", lambda e: e.activation(out=Dm[:, :, :], in_=dm[:, :, :], func=AF.Exp), reads=["rhs_g"], writes=["Dm"])
                    if out:
                        P.op("act", lambda e: e.activation(out=eGR[:, :, :], in_=v4(PG), func=AF.Exp), reads=["pbG"], writes=["eGR"])
                    P.op("act", lambda e: e.activation(out=eGc[G][:, :], in_=Gc[G][:, :], func=AF.Exp), reads=["Gc" + kG], writes=["eGc" + kG])
                    P.op("dve", lambda e: e.tensor_tensor(out=kdf[G][:, :], in0=v4(PG)[:, :, 127], in1=Gc[G][:, :], op=ALU.subtract),
                         reads=["pbG", "Gc" + kG], writes=["kdf" + kG])
                    P.op("act", lambda e: e.activation(out=kdf[G][:, :], in_=kdf[G][:, :], func=AF.Exp), reads=["kdf" + kG], writes=["kdf" + kG])
                    P.op("act", lambda e: e.activation(out=gl_all[par][:, T0:T0 + 4], in_=v4(PG)[:, :, 127], func=AF.Exp),
                         reads=["pbG"], writes=["gl%d" % par])
                    P.op("dve", lambda e: e.tensor_tensor(out=bM[:, :, :], in0=bc4(cb[:, C_STRICT, :]), in1=bsl, op=ALU.mult),
                         reads=["cb", kba], writes=["bM"])
                    P.op("dve", lambda e: e.tensor_tensor(out=Dp[:, :, :], in0=Dm[:, :, :], in1=bM[:, :, :], op=ALU.mult),
                         reads=["Dm", "bM"], writes=["Dp"])
                    for g in range(4):
                        P.op("pe", lambda e, g=g: e.matmul(out=PK[:, g * 128:(g + 1) * 128], lhsT=cTk[:, tc(g)], rhs=cTk[:, tc(g)],
                                                           start=True, stop=True), reads=[kck], writes=["pbK"])
                    P.op("dve", lambda e: e.tensor_tensor(out=Lm[G][:, :, :], in0=v4(PK), in1=Dp[:, :, :], op=ALU.mult),
                         reads=["pbK", "Dp"], writes=["Lm" + kG])
                    for g in range(4):
                        P.op("pe", lambda e, g=g: e.transpose(out=pbT[:, g, :], in_=Lm[G][:, g, :], identity=ident_bf),
                             reads=["Lm" + kG, "cb"], writes=["pbT"])
                    P.op("act", lambda e: e.copy(out=Um[G][:, :, :], in_=pbT[:, 0:4, :]), reads=["pbT"], writes=["Um" + kG])
                    if out:
                        for g in range(4):
                            P.op("pe", lambda e, g=g: e.matmul(out=PK[:, g * 128:(g + 1) * 128], lhsT=cTq[:, tc(g)], rhs=cTk[:, tc(g)],
                                                               start=True, stop=True), reads=[kcq, kck], writes=["pbK"])
                        P.op("dve", lambda e: e.tensor_tensor(out=Am[:, :, :], in0=v4(PK), in1=Dm[:, :, :], op=ALU.mult),
                             reads=["pbK", "Dm"], writes=["Am"])
                        for g in range(4):
                            P.op("pe", lambda e, g=g: e.transpose(out=pbT[:, 4 + g, :], in_=Am[:, g, :], identity=ident_bf),
                                 reads=["Am", "cb"], writes=["pbT"])
                        P.op("act", lambda e: e.copy(out=AT_all[par][:, T0:T0 + 4, :], in_=pbT[:, 4:8, :]),
                             reads=["pbT"], writes=["AT%d" % par])
                        P.op("dve", lambda e: e.tensor_tensor(
                            out=qdT_all[par][:, T0:T0 + 4, :], in0=cTq[:, T0 * 128:(T0 + 4) * 128].rearrange("p (g f) -> p g f", f=128),
                            in1=eGR[:, :, :], op=ALU.mult), reads=[kcq, "eGR"], writes=["qdT%d" % par])
                        yield
                    b1, b2 = (pb["Y"], pb["B"]) if G % 2 == 0 else (pb["G"], pb["K"])
                    n1, n2 = ("pbY", "pbB") if G % 2 == 0 else ("pbG", "pbK")
                    X, XT = Xb[G][0], XTb[G][0]
                    kx, kxt = "X%d_0" % G, "XT%d_0" % G
                    for li, s in enumerate(LEVELS):
                        P.op("dve", lambda e, li=li: e.tensor_tensor(out=Lo[G][:, :, :], in0=Lm[G][:, :, :], in1=bc4(cb[:, C_MM + li, :]), op=ALU.mult),
                             reads=["Lm" + kG, "cb"], writes=["Lo" + kG])
                        P.op("dve", lambda e, li=li: e.tensor_tensor(out=LoT[G][:, :, :], in0=Um[G][:, :, :], in1=bc4(cb[:, C_MMT + li, :]), op=ALU.mult),
                             reads=["Um" + kG, "cb"], writes=["LoT" + kG])
                        if s == 1:
                            P.op("dve", lambda e: e.scalar_tensor_tensor(out=X[:, :, :], in0=Lo[G][:, :, :], scalar=-1.0, in1=bc4(ident_bf),
                                                                         op0=ALU.mult, op1=ALU.add), reads=["Lo" + kG, "cb"], writes=[kx])
                            P.op("dve", lambda e: e.scalar_tensor_tensor(out=XT[:, :, :], in0=LoT[G][:, :, :], scalar=-1.0, in1=bc4(ident_bf),
                                                                         op0=ALU.mult, op1=ALU.add), reads=["LoT" + kG, "cb"], writes=[kxt])
                            yield
                            continue
                        for g in range(4):
                            P.op("pe", lambda e, g=g: e.matmul(out=b1[:, g * 128:(g + 1) * 128], lhsT=LoT[G][:, g, :], rhs=X[:, g, :],
                                                               start=True, stop=True), reads=["LoT" + kG, kx], writes=[n1])
                        P.op("act", lambda e: e.mul(out=Yn[G][:, :, :], in_=v4(b1), mul=-1.0), reads=[n1], writes=["Yn" + kG])
                        for g in range(4):
                            P.op("pe", lambda e, g=g: e.matmul(out=b2[:, g * 128:(g + 1) * 128], lhsT=Lo[G][:, g, :], rhs=XT[:, g, :],
                                                               start=True, stop=True), reads=["Lo" + kG, kxt], writes=[n2])
                        P.op("act", lambda e: e.mul(out=Ynp[G][:, :, :], in_=v4(b2), mul=-1.0), reads=[n2], writes=["Ynp" + kG])
                        yield
                        xdve = True
                        for g in range(4):
                            if not xdve:
                                P.op("pe", lambda e, g=g: e.matmul(out=b1[:, g * 128:(g + 1) * 128], lhsT=ident_bf, rhs=X[:, g, :],
                                                                   start=True, stop=False), reads=["cb", kx], writes=[n1])
                            P.op("pe", lambda e, g=g: e.matmul(out=b1[:, g * 128:(g + 1) * 128], lhsT=XT[:, g, :], rhs=Yn[G][:, g, :],
                                                               start=xdve, stop=True), reads=[kxt, "Yn" + kG], writes=[n1])
                        for g in range(4):
                            P.op("pe", lambda e, g=g: e.matmul(out=b2[:, g * 128:(g + 1) * 128], lhsT=X[:, g, :], rhs=Ynp[G][:, g, :],
                                                               start=True, stop=True), reads=[kx, "Ynp" + kG], writes=[n2])
                        if xdve:
                            P.op("dve", lambda e: e.tensor_tensor(out=X[:, :, :], in0=v4(b1), in1=X[:, :, :], op=ALU.add), reads=[n1, kx], writes=[kx])
                        else:
                            P.op("act", lambda e: e.copy(out=X[:, :, :], in_=v4(b1)), reads=[n1], writes=[kx])
                        P.op("dve", lambda e: e.tensor_tensor(out=XT[:, :, :], in0=v4(b2), in1=XT[:, :, :], op=ALU.add), reads=[n2, kxt], writes=[kxt])
                        yield
                    for g in range(4):
                        P.op("pe", lambda e, g=g: e.transpose(out=pbT[:, g, :], in_=cTk[:, tc(g)], identity=ident_bf),
                             reads=[kck, "cb"], writes=["pbT"])
                    P.op("act", lambda e: e.copy(out=kn_tok[:, :, :], in_=pbT[:, 0:4, :]), reads=["pbT"], writes=["kn_tok"])
                    for g in range(4):
                        P.op("pe", lambda e, g=g: e.transpose(out=pbT[:, 4 + g, :], in_=cTv[:, tc(g)], identity=ident_bf),
                             reads=[kcv, "cb"], writes=["pbT"])
                    P.op("act", lambda e: e.copy(out=v_tok[:, :, :], in_=pbT[:, 4:8, :]), reads=["pbT"], writes=["v_tok"])
                    P.op("dve", lambda e: e.tensor_tensor(out=Rv[:, :, :], in0=v_tok[:, :, :], in1=bsl, op=ALU.mult),
                         reads=["v_tok", kba], writes=["Rv"])
                    P.op("dve", lambda e: e.tensor_tensor(out=bE[G][:, :], in0=ball[:, T0:T0 + 4, h], in1=eGc[G][:, :], op=ALU.mult),
                         reads=[kba, "eGc" + kG], writes=["bE" + kG])
                    P.op("dve", lambda e: e.tensor_tensor(out=Rk[:, :, :], in0=kn_tok[:, :, :],
                                                          in1=bE[G][:, :].unsqueeze(2).to_broadcast([128, 4, 128]), op=ALU.mult),
                         reads=["kn_tok", "bE" + kG], writes=["Rk"])
                    for g in range(4):
                        P.op("pe", lambda e, g=g, XT=XT: e.matmul(out=b1[:, g * 128:(g + 1) * 128], lhsT=XT[:, g, :], rhs=Rv[:, g, :],
                                                                  start=True, stop=True), reads=[kxt, "Rv"], writes=[n1])
                    P.op("act", lambda e: e.copy(out=u_all[par][:, T0:T0 + 4, :], in_=v4(b1)), reads=[n1], writes=["u%d" % par])
                    for g in range(4):
                        P.op("pe", lambda e, g=g, XT=XT: e.matmul(out=b2[:, g * 128:(g + 1) * 128], lhsT=Rk[:, g, :], rhs=XT[:, g, :],
                                                                  start=True, stop=True), reads=[kxt, "Rk"], writes=[n2])
                    P.op("act", lambda e: e.copy(out=wT_all[par][:, T0:T0 + 4, :], in_=v4(b2)), reads=[n2], writes=["wT%d" % par])
                    P.op("dve", lambda e: e.tensor_tensor(out=kd_all[par][:, T0:T0 + 4, :], in0=kn_tok[:, :, :],
                                                           in1=kdf[G][:, :].unsqueeze(2).to_broadcast([128, 4, 128]), op=ALU.mult),
                         reads=["kn_tok", "kdf" + kG], writes=["kd%d" % par])

                    yield

                chains = [chain(G) for G in range(NG)]
                while chains:
                    alive = []
                    for c in chains:
                        try:
                            next(c)
                            alive.append(c)
                        except StopIteration:
                            pass
                    chains = alive
                    yield

            def rec(p, h, par, out):
                S_h, Sf_h, Sb_h = Sst[:, h, :], Sf[:, h, :], Sb[:, h, :]
                kS, kSf, kSb = "S%d" % h, "Sf%d" % h, "Sb%d" % h
                sbf = Sbf[par]
                ksbf = "Sbf%d" % par
                if p in (1, 2):
                    P.op("dve", lambda e: e.tensor_scalar(out=S_h, in0=S_h, scalar1=flg[:, F_CARRY + p:F_CARRY + p + 1], scalar2=None,
                                                          op0=ALU.mult), reads=[kS, "flg"], writes=[kS])
                elif p == 3:
                    P.op("dve", lambda e: e.tensor_copy(out=S_h, in_=Sb_h), reads=[kSb], writes=[kS])
                elif p == 4:
                    P.op("dve", lambda e: e.tensor_copy(out=S_h, in_=Sf_h), reads=[kSf], writes=[kS])
                P.op("act", lambda e: e.copy(out=sbf[:, :], in_=S_h), reads=[kS], writes=[ksbf])
                for t in range(NT):
                    P.op("pe", lambda e, t=t: e.matmul(out=pb["R1"][:, 0:128], lhsT=wT_all[par][:, t, :], rhs=sbf[:, :], start=True, stop=True),
                         reads=["wT%d" % par, ksbf], writes=["pbR1"])
                    P.op("dve", lambda e, t=t: e.tensor_tensor(out=vnew[:, :], in0=u_all[par][:, t, :], in1=pb["R1"][:, 0:128], op=ALU.subtract),
                         reads=["u%d" % par, "pbR1"], writes=["vnew"])
                    if out:
                        ob = osb[t % 2]
                        P.op("pe", lambda e, t=t: e.matmul(out=pb["R1"][:, 256:384], lhsT=qdT_all[par][:, t, :], rhs=sbf[:, :], start=True, stop=False),
                             reads=["qdT%d" % par, ksbf], writes=["pbR1"])
                        P.op("pe", lambda e, t=t: e.matmul(out=pb["R1"][:, 256:384], lhsT=AT_all[par][:, t, :], rhs=vnew[:, :], start=False, stop=True),
                             reads=["AT%d" % par, "vnew"], writes=["pbR1"])
                        P.op("act", lambda e, ob=ob: e.copy(out=ob[:, :], in_=pb["R1"][:, 256:384]), reads=["pbR1"], writes=["osb%d" % (t % 2)])
                        P.op("sp", lambda e, ob=ob, t=t: e.dma_start(out=oscr[p - 3, t * 128:(t + 1) * 128, h * 128:(h + 1) * 128], in_=ob[:, :]),
                             reads=["osb%d" % (t % 2)], writes=["oscr%d_%d_%d" % (p - 3, t, h)], dma="osb%d" % (t % 2))
                    P.op("pe", lambda e, t=t: e.matmul(out=pb["R1"][:, 128:256], lhsT=kd_all[par][:, t, :], rhs=vnew[:, :], start=True, stop=True),
                         reads=["kd%d" % par, "vnew"], writes=["pbR1"])
                    P.op("dve", lambda e, t=t: e.scalar_tensor_tensor(out=S_h, in0=S_h, scalar=gl_all[par][:, t:t + 1], in1=pb["R1"][:, 128:256],
                                                                      op0=ALU.mult, op1=ALU.add), reads=[kS, "gl%d" % par, "pbR1"], writes=[kS])
                    P.op("act", lambda e: e.copy(out=sbf[:, :], in_=S_h), reads=[kS], writes=[ksbf])
                    yield
                if p < 3:
                    P.op("dve", lambda e: e.scalar_tensor_tensor(out=Sf_h, in0=S_h, scalar=flg[:, F_F + p:F_F + p + 1], in1=Sf_h,
                                                                 op0=ALU.mult, op1=ALU.add), reads=[kS, kSf, "flg"], writes=[kSf])
                    P.op("dve", lambda e: e.scalar_tensor_tensor(out=Sb_h, in0=S_h, scalar=flg[:, F_B + p:F_B + p + 1], in1=Sb_h,
                                                                 op0=ALU.mult, op1=ALU.add), reads=[kS, kSb, "flg"], writes=[kSb])
                yield

            def prepA_task(p, h, par, out):
                if h == 0:
                    yield from pass_setup(p)
                yield from prepA(p, h, par, out)

            seq = [(p, h, k % 2, p >= 3) for k, (p, h) in enumerate((p, h) for p in range(5) for h in range(NH))]
            n = len(seq)
            for step in range(n + 2):
                gens = []
                if step < n:
                    gens.append(prepA_task(*seq[step]))
                if 0 <= step - 1 < n:
                    gens.append(prepB(*seq[step - 1]))
                if 0 <= step - 2 < n:
                    gens.append(rec(*seq[step - 2]))
                interleave(gens)

            P.barrier(bar_t, cf, pb["R2"], flags_bc[0:1, 0:4])

        with contextlib.ExitStack() as st:
            stash1 = sb(st, "stash1", [128, KD, SEG], BF16)
            stash2 = sb(st, "stash2", [128, KD, SEG], BF16)
            wres_f = sb(st, "wres_f", [128, DM], F32)
            wres = sb(st, "wres", [128, KD, DM], BF16)
            cwa = sb(st, "cwa", [128, KD, 3], F32)
            dgA = sb(st, "dgA", [128, 3, 128], BF16)
            pA = sb(st, "pA", [128, SEG + 2], BF16)
            tf1 = sb(st, "tf1", [128, 512], F32)
            tf2 = sb(st, "tf2", [128, 512], F32)
            tf3 = sb(st, "tf3", [128, 512], F32)
            gnw = sb(st, "gnw", [128, 128], F32)
            fnw = sb(st, "fnw", [128, DM], F32)
            ofw = sb(st, "ofw", [128, DM], F32)
            obw = sb(st, "obw", [128, DM], F32)
            osum = sb(st, "osum", [128, DM], F32)
            sq = sb(st, "sq", [128, DM], F32)
            ssh = sb(st, "ssh", [128, NH], F32)
            ybp = sb(st, "ybp", [128, DM], BF16)
            xr = sb(st, "xr", [128, DM], F32)
            yo = ofw
            first_reads = []
            if debug:
                print("main-phase SBUF bytes remaining/partition:", nc.sbuf_bytes_remaining)
            P.op("sp", lambda e: e.dma_start(out=cwa[:, :, :].rearrange("p a b -> p (a b)"), in_=cwa_fm[:, :]), reads=first_reads, writes=["cwa"], dma="cwa")
            P.op("sp", lambda e: e.dma_start(out=gnw[:, :], in_=gnw_bc[:, :]), reads=first_reads, writes=["gnw"], dma="gnw")
            P.op("sp", lambda e: e.dma_start(out=fnw[:, :], in_=fnw_bc[:, :]), reads=first_reads, writes=["fnw"], dma="fnw")
            hs = lambda d, o0, o1: hT[:, d, 2 + o0:2 + o1]

            def proj(bank, slot, rhs_fn, n):
                for d in range(KD):
                    P.op("pe", lambda e, d=d: e.matmul(out=pb[bank][:, 0:n], lhsT=wbf[slot][:, d, :], rhs=rhs_fn(d),
                                                       start=(d == 0), stop=(d == KD - 1)), reads=["wbf%d" % slot, "hT"], writes=["pb" + bank])

            for fc in range(KD):
                for slot, coff in enumerate((cfg.C_BG, cfg.C_AX, cfg.C_CG, cfg.C_AZ)):
                    load_w_chunk(slot, w_in[:, coff + fc * 128:coff + (fc + 1) * 128], cast_eng="act" if slot % 2 else "dve")
                for tap in range(3):
                    P.op("dve", lambda e, tap=tap: e.tensor_scalar(out=dgA[:, tap, :], in0=ident_bf, scalar1=cwa[:, fc, tap:tap + 1], scalar2=None,
                                                                   op0=ALU.mult), reads=["cb", "cwa"], writes=["dgA"])
                for (c0, c1) in blocks(SEG + 2):
                    n = c1 - c0
                    proj("A", 0, lambda d, c0=c0, c1=c1: hT[:, d, 1 + c0:1 + c1], n)
                    proj("B", 1, lambda d, c0=c0, c1=c1: hT[:, d, 1 + c0:1 + c1], n)
                    P.op("act", lambda e, n=n: e.copy(out=tf1[:, 0:n], in_=pb["A"][:, 0:n]), reads=["pbA"], writes=["tf1"])
                    P.op("dve", lambda e, c0=c0, c1=c1, n=n: e.tensor_tensor(out=pA[:, c0:c1], in0=tf1[:, 0:n], in1=pb["B"][:, 0:n], op=ALU.mult),
                         reads=["tf1", "pbB"], writes=["pA"])
                P.op("dve", lambda e: e.tensor_scalar(out=pA[:, 0:1], in0=pA[:, 0:1], scalar1=flg[:, F_LO(4):F_LO(4) + 1], scalar2=None, op0=ALU.mult),
                     reads=["pA", "flg"], writes=["pA"])
                P.op("dve", lambda e: e.tensor_scalar(out=pA[:, SEG + 1:SEG + 2], in0=pA[:, SEG + 1:SEG + 2], scalar1=flg[:, F_HI(4):F_HI(4) + 1],
                                                      scalar2=None, op0=ALU.mult), reads=["pA", "flg"], writes=["pA"])
                for (o0, o1) in blocks(SEG):
                    n = o1 - o0
                    for tap in range(3):
                        P.op("pe", lambda e, tap=tap, o0=o0, o1=o1, n=n: e.matmul(out=pb["K"][:, 0:n], lhsT=dgA[:, tap, :], rhs=pA[:, o0 + tap:o1 + tap],
                                                                                 start=(tap == 0), stop=(tap == 2)), reads=["dgA", "pA"], writes=["pbK"])
                    proj("A", 2, lambda d, o0=o0, o1=o1: hs(d, o0, o1), n)
                    proj("B", 3, lambda d, o0=o0, o1=o1: hs(d, o0, o1), n)
                    P.op("act", lambda e, n=n: e.activation(out=tf2[:, 0:n], in_=pb["B"][:, 0:n], func=AF.Silu), reads=["pbB"], writes=["tf2"])
                    P.op("act", lambda e, n=n: e.copy(out=tf1[:, 0:n], in_=pb["A"][:, 0:n]), reads=["pbA"], writes=["tf1"])
                    P.op("dve", lambda e, n=n: e.tensor_tensor(out=tf3[:, 0:n], in0=tf1[:, 0:n], in1=pb["K"][:, 0:n], op=ALU.mult),
                         reads=["tf1", "pbK"], writes=["tf3"])
                    P.op("dve", lambda e, o0=o0, o1=o1, n=n: e.tensor_tensor(out=stash1[:, fc, o0:o1], in0=tf3[:, 0:n], in1=tf2[:, 0:n], op=ALU.mult),
                         reads=["tf3", "tf2"], writes=["stash1"])

            def branch_out(wmat, gate_off, accumulate):
                for oc in range(KD):
                    load_w_chunk(0, wmat[:, oc * 128:(oc + 1) * 128])
                    load_w_chunk(1, w_in[:, gate_off + oc * 128:gate_off + (oc + 1) * 128], cast_eng="act")
                    for (o0, o1) in blocks(SEG):
                        n = o1 - o0
                        for fc in range(KD):
                            P.op("pe", lambda e, fc=fc, o0=o0, o1=o1, n=n: e.matmul(out=pb["A"][:, 0:n], lhsT=wbf[0][:, fc, :], rhs=stash1[:, fc, o0:o1],
                                                                                   start=(fc == 0), stop=(fc == KD - 1)), reads=["wbf0", "stash1"], writes=["pbA"])
                        proj("B", 1, lambda d, o0=o0, o1=o1: hs(d, o0, o1), n)
                        P.op("act", lambda e, n=n: e.activation(out=tf2[:, 0:n], in_=pb["B"][:, 0:n], func=AF.Sigmoid), reads=["pbB"], writes=["tf2"])
                        if not accumulate:
                            P.op("dve", lambda e, oc=oc, o0=o0, o1=o1, n=n: e.tensor_tensor(out=stash2[:, oc, o0:o1], in0=tf2[:, 0:n], in1=pb["A"][:, 0:n], op=ALU.mult),
                                 reads=["tf2", "pbA"], writes=["stash2"])
                        else:
                            P.op("dve", lambda e, n=n: e.tensor_tensor(out=tf3[:, 0:n], in0=tf2[:, 0:n], in1=pb["A"][:, 0:n], op=ALU.mult),
                                 reads=["tf2", "pbA"], writes=["tf3"])
                            P.op("dve", lambda e, oc=oc, o0=o0, o1=o1, n=n: e.tensor_tensor(out=stash2[:, oc, o0:o1], in0=stash2[:, oc, o0:o1], in1=tf3[:, 0:n], op=ALU.add),
                                 reads=["tf3", "stash2"], writes=["stash2"])

            branch_out(w_pa, cfg.C_GA, False)

            def load_wres(src_ap):
                for d in range(KD):
                    P.op("sp", lambda e, d=d: e.dma_start(out=wres_f[:, :], in_=src_ap[d * 128:(d + 1) * 128, :]), writes=["wres_f"], dma="wres_f")
                    P.op("act" if d % 2 else "dve", lambda e, d=d: (e.copy(out=wres[:, d, :], in_=wres_f[:, :]) if d % 2 else e.tensor_copy(out=wres[:, d, :], in_=wres_f[:, :])), reads=["wres_f"], writes=["wres"])

            load_wres(w_in[:, cfg.C_Z:cfg.C_Z + DM])
            for t in range(NT):
                P.op("sp", lambda e, t=t: e.dma_start(out=ofw[:, :], in_=oscr[1, t * 128:(t + 1) * 128, :]), reads=["oscr1_%d_%d" % (t, hh) for hh in range(NH)], writes=["ofw"], dma="ofw")
                P.op("sp", lambda e, t=t: e.dma_start(out=obw[:, :], in_=oscr[0, (NT - 1 - t) * 128:(NT - t) * 128, :]), reads=["oscr0_%d_%d" % (NT - 1 - t, hh) for hh in range(NH)], writes=["obw"], dma="obw")
                for i, (c0, c1) in enumerate(blocks(DM)):
                    bk = "A" if i % 2 == 0 else "B"
                    P.op("pe", lambda e, bk=bk, c0=c0, c1=c1: e.matmul(out=pb[bk][:, 0:c1 - c0], lhsT=cf[:, C_J, :], rhs=obw[:, c0:c1], start=True, stop=True),
                         reads=["cf", "obw"], writes=["pb" + bk])
                    P.op("dve", lambda e, bk=bk, c0=c0, c1=c1: e.tensor_tensor(out=osum[:, c0:c1], in0=pb[bk][:, 0:c1 - c0], in1=ofw[:, c0:c1], op=ALU.add),
                         reads=["pb" + bk, "ofw"], writes=["osum"])
                P.op("act", lambda e: e.activation(out=sq[:, :], in_=osum[:, :], func=AF.Square), reads=["osum"], writes=["sq"])
                P.op("dve", lambda e: e.tensor_reduce(out=ssh[:, :], in_=sq[:, :].rearrange("p (h f) -> p h f", f=128), axis=AX.X, op=ALU.add),
                     reads=["sq"], writes=["ssh"])
                P.op("dve", lambda e: e.tensor_scalar(out=ssh[:, :], in0=ssh[:, :], scalar1=1.0 / 128, scalar2=1e-6, op0=ALU.mult, op1=ALU.add),
                     reads=["ssh"], writes=["ssh"])
                P.op("act", lambda e: e.sqrt(out=ssh[:, :], in_=ssh[:, :]), reads=["ssh"], writes=["ssh"])
                P.op("dve", lambda e: e.reciprocal(out=ssh[:, :], in_=ssh[:, :]), reads=["ssh"], writes=["ssh"])
                o3 = osum[:, :].rearrange("p (h f) -> p h f", f=128)
                P.op("dve", lambda e: e.tensor_tensor(out=o3, in0=o3, in1=ssh[:, :].unsqueeze(2).to_broadcast([128, NH, 128]), op=ALU.mult),
                     reads=["osum", "ssh"], writes=["osum"])
                P.op("dve", lambda e: e.tensor_tensor(out=o3, in0=o3, in1=gnw[:, :].unsqueeze(1).to_broadcast([128, NH, 128]), op=ALU.mult),
                     reads=["osum", "gnw"], writes=["osum"])
                for i, (c0, c1) in enumerate(blocks(DM)):
                    bk = "K" if i % 2 == 0 else "Y"
                    for d in range(KD):
                        P.op("pe", lambda e, bk=bk, d=d, c0=c0, c1=c1: e.matmul(out=pb[bk][:, 0:c1 - c0], lhsT=hs(d, t * 128, (t + 1) * 128), rhs=wres[:, d, c0:c1],
                                                                               start=(d == 0), stop=(d == KD - 1)), reads=["hT", "wres"], writes=["pb" + bk])
                    P.op("act", lambda e, bk=bk, c0=c0, c1=c1: e.activation(out=tf2[:, 0:c1 - c0], in_=pb[bk][:, 0:c1 - c0], func=AF.Silu), reads=["pb" + bk], writes=["tf2"])
                    P.op("dve", lambda e, c0=c0, c1=c1: e.tensor_tensor(out=ybp[:, c0:c1], in0=osum[:, c0:c1], in1=tf2[:, 0:c1 - c0], op=ALU.mult),
                         reads=["osum", "tf2"], writes=["ybp"])
                for fc in range(KD):
                    P.op("pe", lambda e, fc=fc: e.transpose(out=pbT[:, fc, :], in_=ybp[:, fc * 128:(fc + 1) * 128], identity=ident_bf),
                         reads=["ybp", "cb"], writes=["pbT"])
                P.op("act", lambda e, t=t: e.copy(out=stash1[:, :, t * 128:(t + 1) * 128], in_=pbT[:, 0:KD, :]), reads=["pbT"], writes=["stash1"])

            branch_out(w_pb, cfg.C_GB, True)

            load_wres(w_o[:, :])
            fin_ops = []
            for t in range(NT):
                P.op("sp", lambda e, t=t: e.dma_start(out=xr[:, :], in_=xs[4, 2 + t * 128:2 + (t + 1) * 128, :]), writes=["xr"], dma="xr")
                for i, (c0, c1) in enumerate(blocks(DM)):
                    bk = "A" if i % 2 == 0 else "B"
                    for fc in range(KD):
                        P.op("pe", lambda e, bk=bk, fc=fc, c0=c0, c1=c1: e.matmul(out=pb[bk][:, 0:c1 - c0], lhsT=stash2[:, fc, t * 128:(t + 1) * 128], rhs=wres[:, fc, c0:c1],
                                                                                 start=(fc == 0), stop=(fc == KD - 1)), reads=["stash2", "wres"], writes=["pb" + bk])
                    P.op("dve", lambda e, bk=bk, c0=c0, c1=c1: e.tensor_tensor(out=osum[:, c0:c1], in0=pb[bk][:, 0:c1 - c0], in1=gate_bc[:, c0:c1], op=ALU.mult),
                         reads=["pb" + bk, "gate_bc"], writes=["osum"])
                P.op("dve", lambda e: e.tensor_tensor(out=osum[:, :], in0=osum[:, :], in1=xr[:, :], op=ALU.add), reads=["osum", "xr"], writes=["osum"])
                P.op("act", lambda e: e.activation(out=sq[:, :], in_=osum[:, :], func=AF.Square, accum_out=ss[:, :]), reads=["osum"], writes=["sq", "ss"])
                P.op("dve", lambda e: e.tensor_scalar(out=rstd[:, :], in0=ss[:, :], scalar1=1.0 / DM, scalar2=1e-6, op0=ALU.mult, op1=ALU.add),
                     reads=["ss"], writes=["rstd"])
                P.op("act", lambda e: e.sqrt(out=rstd[:, :], in_=rstd[:, :]), reads=["rstd"], writes=["rstd"])
                P.op("dve", lambda e: e.reciprocal(out=rstd[:, :], in_=rstd[:, :]), reads=["rstd"], writes=["rstd"])
                P.op("dve", lambda e: e.scalar_tensor_tensor(out=yo[:, :], in0=osum[:, :], scalar=rstd[:, 0:1], in1=fnw[:, :], op0=ALU.mult, op1=ALU.mult),
                     reads=["osum", "rstd", "fnw"], writes=["ofw"])
                fin_ops.append(P.op("sp", lambda e, t=t: e.dma_start(out=y[t * 128:(t + 1) * 128, :], in_=yo[:, :]), reads=["ofw"], writes=["y%d" % t], dma="yo"))
            P.emit(final_wait_ops=fin_ops)
            if debug:
                print("scheduler estimate (us):", P.est_ns / 1e3)
    return nc


def make_consts():
    c = np.zeros((NCONST, 128, 128), np.float32)
    p = np.arange(128)[:, None]
    f = np.arange(128)[None, :]
    c[C_ID] = (p == f)
    c[C_ONE] = 1.0
    c[C_MLOW] = (p <= f)
    c[C_NEG] = np.where(p >= f, 0.0, NEG)
    c[C_STRICT] = (p > f)
    c[C_J] = (p + f == 127)
    for li, s in enumerate(LEVELS):
        m = ((p // (2 * s)) == (f // (2 * s))) & ((p % (2 * s)) >= s) & ((f % (2 * s)) < s)
        c[C_MM + li] = m
        c[C_MMT + li] = m.T
    return np.ascontiguousarray(c.transpose(1, 0, 2).reshape(128, NCONST * 128))


def fm(v, k):
    return np.ascontiguousarray(np.asarray(v, np.float32).reshape(k, 128).T)


def rep(v):
    v = np.asarray(v, np.float32).reshape(1, -1)
    return np.ascontiguousarray(np.broadcast_to(v, (128, v.shape[1])))


def host_prep(cfg, inputs, cores=None):
    DM, NH, KD, SEG, SEQ = cfg.DM, cfg.NH, cfg.KD, cfg.SEG, cfg.SEQ
    x = np.asarray(inputs["x"], np.float32)
    w_in = np.ascontiguousarray(np.asarray(inputs["w_in"], np.float32)[0])
    conv_qkv = np.asarray(inputs["conv_qkv_w"], np.float32)[0]
    a_log = np.asarray(inputs["a_log"], np.float32)[0]
    dt_bias = np.asarray(inputs["dt_bias"], np.float32)[0]
    consts = make_consts()
    common = {
        "normw_fm": fm(inputs["norm_w"][0], KD),
        "w_ada": np.ascontiguousarray(np.asarray(inputs["w_ada"], np.float32)[0]),
        "b_ada": np.ascontiguousarray(np.asarray(inputs["b_ada"], np.float32)[0].reshape(1, -1)),
        "w_in": w_in,
        "cwa_fm": np.ascontiguousarray(np.asarray(inputs["conv_a_w"], np.float32)[0].T.reshape(KD, 128, 3).transpose(1, 0, 2).reshape(128, KD * 3)),
        "constm": consts,
        "gnw_bc": rep(inputs["gdn_norm_w"][0]),
        "fnw_bc": rep(inputs["final_norm_w"]),
        "w_pa": np.ascontiguousarray(np.asarray(inputs["w_pa"], np.float32)[0]),
        "w_pb": np.ascontiguousarray(np.asarray(inputs["w_pb"], np.float32)[0]),
        "w_o": np.ascontiguousarray(np.asarray(inputs["w_o"], np.float32)[0]),
    }

    def seg_halo(b, i, rev):
        lo, hi = i * SEG - 2, (i + 1) * SEG + 2
        out = np.zeros((SEG + 4, DM), np.float32)
        a, bnd = max(lo, 0), min(hi, SEQ)
        out[a - lo:a - lo + (bnd - a)] = x[b, a:bnd]
        flo, fhi = float(i > 0), float(i < 3)
        if rev:
            out = out[::-1]
            flo, fhi = fhi, flo
        return out, flo, fhi

    in_maps = []
    cores = range(8) if cores is None else cores
    for core in cores:
        b, j = divmod(core, 4)
        plist = [(i, 0) for i in range(j)] + [(i, 1) for i in range(3, j, -1)] + [(j, 1), (j, 0)]
        xs = np.zeros((5, SEG + 4, DM), np.float32)
        flags = np.zeros((32,), np.float32)
        wab = np.zeros((5, DM, 2 * NH), np.float32)
        cw = np.zeros((5, 3 * NH, 128, 5), np.float32)
        alog = np.zeros((5, NH), np.float32)
        dtb = np.zeros((5, NH), np.float32)
        for p, (i, dr) in enumerate(plist):
            xs[p], flags[2 * p], flags[2 * p + 1] = seg_halo(b, i, dr == 1)
            wab[p, :, :NH] = w_in[:, cfg.C_A + dr * NH:cfg.C_A + (dr + 1) * NH]
            wab[p, :, NH:] = w_in[:, cfg.C_B + dr * NH:cfg.C_B + (dr + 1) * NH]
            taps = conv_qkv[::-1] if dr == 1 else conv_qkv
            cw[p] = taps.T.reshape(3 * NH, 128, 5)
            alog[p] = a_log[dr]
            dtb[p] = dt_bias[dr]
        for s in range(1, 3):
            flags[10 + s] = float(plist[s][1] == plist[s - 1][1])
        for s in range(3):
            nxt_dir = plist[s + 1][1] if s < 2 else None
            if plist[s][1] == 0 and (s == 2 or nxt_dir == 1):
                flags[13 + s] = 1.0
            if plist[s][1] == 1 and s == 2:
                flags[16 + s] = 1.0
        m = dict(common)
        m.update({
            "xs": xs,
            "c_fm": fm(inputs["c"][b], KD),
            "w_ab": wab,
            "cw_fm": np.ascontiguousarray(cw.transpose(2, 0, 1, 3).reshape(128, 5 * 3 * NH * 5)),
            "alog_bc": rep(alog.reshape(-1)),
            "dtb_bc": rep(dtb.reshape(-1)),
            "flags_bc": np.ascontiguousarray(np.broadcast_to(flags.reshape(1, 32), (128, 32))),
        })
        in_maps.append(m)
    return in_maps


_CACHE = {}


def kernel(**inputs):
    cfg = Cfg()
    if "nc" not in _CACHE:
        _CACHE["nc"] = build_program(cfg)
    nc = _CACHE["nc"]
    in_maps = host_prep(cfg, inputs)
    res = run_bass_kernel_spmd(nc, in_maps, core_ids=list(range(8)))
    out = np.zeros((2, cfg.SEQ, cfg.DM), np.float32)
    for core in range(8):
        b, j = divmod(core, 4)
        out[b, j * cfg.SEG:(j + 1) * cfg.SEG] = np.asarray(res.results[core]["y"], np.float32)
    return out
```

```python
import bisect
import contextlib

import numpy as np

import concourse.bass as bass
import concourse.mybir as mybir
from concourse.bass_utils import run_bass_kernel_spmd

F32 = mybir.dt.float32
BF16 = mybir.dt.bfloat16
ALU = mybir.AluOpType
AF = mybir.ActivationFunctionType
AX = mybir.AxisListType

ENGS = ("pe", "dve", "act", "pool", "sp")
SEM_EPOCH = 30000
NEG = -1.0e9
NCONST = 20
C_ID, C_ONE, C_MLOW, C_NEG, C_STRICT, C_J, C_MM, C_MMT = 0, 1, 2, 3, 4, 5, 6, 13
LEVELS = (1, 2, 4, 8, 16, 32, 64)


class Cfg:
    def __init__(self, DM=1024, SEG=2048):
        self.DM = DM
        self.NH = DM // 128
        self.KD = DM // 128
        self.SEG = SEG
        self.NT = SEG // 128
        self.NG = self.NT // 4
        self.SEQ = SEG * 4
        self.NIN = 8 * DM + 4 * self.NH + 2 * DM
        self.C_BG, self.C_CG, self.C_AX, self.C_AZ = 0, DM, 2 * DM, 3 * DM
        self.C_Q, self.C_K, self.C_V, self.C_Z = 4 * DM, 5 * DM, 6 * DM, 7 * DM
        self.C_A = 8 * DM
        self.C_B = 8 * DM + 2 * self.NH
        self.C_GA = 8 * DM + 4 * self.NH
        self.C_GB = self.C_GA + DM


class _Rec:
    def __getattr__(self, name):
        return lambda *a, **k: (name, a, k)


class Prog:
    def __init__(self, nc):
        self.nc = nc
        self.ops = []
        self.lastw = {}
        self.rd_eng = {}
        self.rd_dma = {}
        self.bar_idx = None

    def op(self, eng, fn, reads=(), writes=(), dma=None):
        idx = len(self.ops)
        pr = [k for k in reads if k.startswith("pb")]
        if pr:
            reads = [k for k in reads if not k.startswith("pb")]
            writes = list(writes) + [k for k in pr if k not in writes]
        deps = set()
        for k in reads:
            if k in self.lastw:
                deps.add(self.lastw[k])
        for k in writes:
            if k in self.lastw:
                deps.add(self.lastw[k])
            deps.update(self.rd_eng.get(k, ()))
            deps.update(self.rd_dma.get(k, ()))
        for k in writes:
            self.lastw[k] = idx
            self.rd_eng[k] = []
            self.rd_dma[k] = []
        for k in reads:
            if k in writes:
                continue
            if dma is not None:
                self.rd_dma.setdefault(k, []).append(idx)
            else:
                self.rd_eng.setdefault(k, []).append(idx)
        if self.bar_idx is not None:
            deps.add(self.bar_idx)
        deps.discard(idx)
        self.ops.append(dict(eng=eng, call=fn(_Rec()), deps=deps, dma=dma, needed=False))
        return idx

    def barrier(self, bar_t, cf, pbank, dram_src):
        allk = [k for k in self.lastw.keys()]
        self.bar_idx = None
        self.bar_idx = self.op("dve", lambda e: e.memset(bar_t[:, 0:1], 0.0), writes=allk + ["bar_dve"])
        self.op("act", lambda e: e.memzero(bar_t[:, 1:2]), reads=["bar_dve"], writes=["bar_act"])
        self.op("pool", lambda e: e.memset(bar_t[:, 2:3], 0.0), reads=["bar_dve"], writes=["bar_pool"])
        self.op("pe", lambda e: e.matmul(out=pbank[0:1, 0:1], lhsT=cf[0:1, 1, 0:1], rhs=cf[0:1, 1, 0:1], start=True, stop=True),
                reads=["bar_dve", "cf"], writes=["pbR2"])
        self.op("sp", lambda e: e.dma_start(out=bar_t[0:1, 4:8], in_=dram_src), reads=["bar_dve"], writes=["bar_sp"], dma="bar")

    def schedule(self, final_wait_ops):
        import heapq
        ops = self.ops
        n = len(ops)

        def cost(o):
            name, a, k = o["call"]
            out = k.get("out", a[0] if a else None)
            try:
                shp = out.shape
                free = 1
                for d in shp[1:]:
                    free *= d
                nbytes = free * shp[0] * (2 if out.dtype == BF16 else 4)
            except Exception:
                free, nbytes = 128, 65536
            if o["dma"] is not None:
                return 2000.0 + nbytes / 150.0
            e = o["eng"]
            aps = [v for v in list(a) + list(k.values()) if hasattr(v, "dtype") and hasattr(v, "shape")]
            in_psum = lambda v: str(v.name).startswith("pb")
            if e == "pe":
                f32 = any(v.dtype == F32 for v in aps if not in_psum(v))
                base = 91.0 if free <= 128 else 64.0 + 0.40 * free
                return base * (2.0 if f32 else 1.0)
            if e == "dve":
                slow = any((v.dtype == F32) or in_psum(v) for v in aps)
                return (100.0 + 1.0 * free) if slow else (80.0 + 0.5 * free)
            if e == "act":
                return 200.0 + 0.7 * free
            return 150.0 + 2.0 * free

        children = [[] for _ in range(n)]
        indeg = [0] * n
        for i, o in enumerate(ops):
            indeg[i] = len(o["deps"])
            for d in o["deps"]:
                children[d].append(i)
        LAT = 120.0
        costs = [cost(o) for o in ops]
        bl = [0.0] * n
        for i in range(n - 1, -1, -1):
            m = 0.0
            for ch in children[i]:
                if bl[ch] > m:
                    m = bl[ch]
            bl[i] = costs[i] + m
        prio = [(-bl[i], i) for i in range(n)]
        ready_t = [0.0] * n
        finish = [0.0] * n
        start = [0.0] * n
        free_at = {e: 0.0 for e in ENGS}
        pending = {e: [] for e in ENGS}
        avail = {e: [] for e in ENGS}
        for i, o in enumerate(ops):
            if indeg[i] == 0:
                heapq.heappush(pending[o["eng"]], (0.0, i))
        done = 0
        while done < n:
            best = None
            for e in ENGS:
                pe_, av_ = pending[e], avail[e]
                while pe_ and pe_[0][0] <= free_at[e]:
                    heapq.heappush(av_, prio[heapq.heappop(pe_)[1]])
                if av_:
                    cand = (free_at[e], av_[0][1], e, True)
                elif pe_:
                    cand = (pe_[0][0], pe_[0][1], e, False)
                else:
                    continue
                if best is None or cand[:2] < best[:2]:
                    best = cand
            assert best is not None, "scheduler deadlock"
            t0, i, e, from_av = best
            if from_av:
                heapq.heappop(avail[e])
            else:
                heapq.heappop(pending[e])
            o = ops[i]
            c = costs[i]
            start[i] = t0
            if o["dma"] is not None:
                free_at[e] = t0 + 70.0
            else:
                free_at[e] = t0 + c
            finish[i] = t0 + c
            done += 1
            for ch in children[i]:
                rt = finish[i] + (0.0 if (ops[ch]["eng"] == e and o["dma"] is None) else LAT)
                if rt > ready_t[ch]:
                    ready_t[ch] = rt
                indeg[ch] -= 1
                if indeg[ch] == 0:
                    heapq.heappush(pending[ops[ch]["eng"]], (ready_t[ch], ch))
        order = sorted(range(n), key=lambda i: (start[i], i))
        remap = {old: new for new, old in enumerate(order)}
        new_ops = []
        for old in order:
            o = ops[old]
            o["deps"] = set(remap[d] for d in o["deps"])
            new_ops.append(o)
        self.ops = new_ops
        self.est_ns = max(finish) if n else 0.0
        return [remap[d] for d in final_wait_ops]

    def emit(self, final_wait_ops=(), reorder=True):
        nc = self.nc
        if reorder:
            final_wait_ops = self.schedule(list(final_wait_ops))
        ops = self.ops
        for o in ops:
            best = {}
            keep = set()
            for d in o["deps"]:
                od = ops[d]
                if od["dma"] is not None:
                    keep.add(d)
                elif d > best.get(od["eng"], -1):
                    best[od["eng"]] = d
            keep.update(best.values())
            o["deps"] = keep

        def pe_pe(o, od):
            return o["eng"] == "pe" and od["eng"] == "pe" and od["dma"] is None and o["dma"] is None

        for o in ops:
            for d in o["deps"]:
                if not pe_pe(o, ops[d]):
                    ops[d]["needed"] = True
        for d in final_wait_ops:
            ops[d]["needed"] = True
        with contextlib.ExitStack() as st:
            eng_sems, eng_cnt = {}, {}
            dma_sems, dma_cnt, dma_order = {}, {}, {}
            for i, o in enumerate(ops):
                if o["dma"] is not None:
                    tag = o["dma"]
                    if tag not in dma_sems:
                        dma_sems[tag] = st.enter_context(nc.semaphore("d_" + tag))
                        dma_cnt[tag] = 0
                        dma_order[tag] = []
                    dma_cnt[tag] += 1
                    dma_order[tag].append(i)
                    o["ev"] = (dma_sems[tag], 16 * dma_cnt[tag], tag)
                elif o["needed"]:
                    e = o["eng"]
                    if e not in eng_sems or eng_cnt[e] >= SEM_EPOCH:
                        eng_sems[e] = st.enter_context(nc.semaphore("e_%s_%d" % (e, i)))
                        eng_cnt[e] = 0
                    eng_cnt[e] += 1
                    o["ev"] = (eng_sems[e], eng_cnt[e], None)
                else:
                    o["ev"] = None
            per_eng = {e: [] for e in ENGS}
            for i, o in enumerate(ops):
                per_eng[o["eng"]].append(i)

            def dep_event(i_waiter, d):
                sem, val, tag = ops[d]["ev"]
                return sem, val

            def run_engine(ename, engobj, extra_final=None):
                waited = {}
                for i in per_eng[ename]:
                    o = ops[i]
                    for d in sorted(o["deps"]):
                        if pe_pe(o, ops[d]):
                            continue
                        sem, val = dep_event(i, d)
                        if waited.get(id(sem), 0) >= val:
                            continue
                        waited[id(sem)] = val
                        engobj.wait_ge(sem, val)
                    name, a, k = o["call"]
                    inst = getattr(engobj, name)(*a, **k)
                    if o["ev"] is not None:
                        sem, val, tag = o["ev"]
                        inst.then_inc(sem, 16 if tag is not None else 1)
                if extra_final:
                    for d in extra_final:
                        sem, val = dep_event(len(ops), d)
                        engobj.wait_ge(sem, val)

            with nc.Block() as block:
                @block.tensor
                def _(e):
                    run_engine("pe", e)

                @block.vector
                def _(e):
                    run_engine("dve", e)

                @block.scalar
                def _(e):
                    run_engine("act", e)

                @block.gpsimd
                def _(e):
                    run_engine("pool", e)

                @block.sync
                def _(e):
                    run_engine("sp", e, extra_final=final_wait_ops)


def blocks(n, step=512):
    return [(a, min(a + step, n)) for a in range(0, n, step)]


def interleave(gens):
    gens = list(gens)
    while gens:
        nxt = []
        for g in gens:
            try:
                next(g)
                nxt.append(g)
            except StopIteration:
                pass
        gens = nxt


def build_program(cfg, debug=False):
    DM, NH, KD, SEG, NT, NG = cfg.DM, cfg.NH, cfg.KD, cfg.SEG, cfg.NT, cfg.NG
    SP4 = SEG + 4
    nc = bass.Bass("TRN2", target_bir_lowering=False)

    def din(name, shape, dt=F32):
        return nc.dram_tensor(name, list(shape), dt, kind="ExternalInput").ap()

    xs = din("xs", [5, SP4, DM])
    c_fm = din("c_fm", [128, KD])
    normw_fm = din("normw_fm", [128, KD])
    w_ada = din("w_ada", [DM, 3 * DM])
    b_ada = din("b_ada", [1, 3 * DM])
    w_in = din("w_in", [DM, cfg.NIN])
    w_ab = din("w_ab", [5, DM, 2 * NH])
    cw_fm = din("cw_fm", [128, 5 * 3 * NH * 5])
    cwa_fm = din("cwa_fm", [128, KD * 3])
    alog_bc = din("alog_bc", [128, 5 * NH])
    dtb_bc = din("dtb_bc", [128, 5 * NH])
    flags_bc = din("flags_bc", [128, 32])
    constm = din("constm", [128, NCONST * 128])
    gnw_bc = din("gnw_bc", [128, 128])
    fnw_bc = din("fnw_bc", [128, DM])
    w_pa = din("w_pa", [DM, DM])
    w_pb = din("w_pb", [DM, DM])
    w_o = din("w_o", [DM, DM])
    y = nc.dram_tensor("y", [SEG, DM], F32, kind="ExternalOutput").ap()
    oscr = nc.dram_tensor("oscr", [2, SEG, DM], F32).ap()
    dbg = None
    if debug:
        dbg = nc.dram_tensor("dbg", [128, 4 * 128], F32, kind="ExternalOutput").ap()

    P = Prog(nc)
    F_LO = lambda p: 2 * p
    F_HI = lambda p: 2 * p + 1
    F_CARRY, F_F, F_B = 10, 13, 16

    with contextlib.ExitStack() as st0:
        def sb(st, name, shape, dt):
            return st.enter_context(nc.sbuf_tensor(name, list(shape), dt))

        pb = {}
        for nm in ("A", "B", "G", "K", "Y", "R1", "R2"):
            pb[nm] = st0.enter_context(nc.psum_tensor("pb" + nm, [128, 512], F32))
        pbT = st0.enter_context(nc.psum_tensor("pbT", [128, 8, 128], BF16))

        cf = sb(st0, "cf", [128, 6, 128], F32)
        cb = sb(st0, "cb", [128, NCONST, 128], BF16)
        flg = sb(st0, "flg", [128, 32], F32)
        epsc = sb(st0, "epsc", [128, 1], F32)
        bar_t = sb(st0, "bar_t", [128, 8], F32)
        hT = sb(st0, "hT", [128, KD, SP4], BF16)
        shsc = sb(st0, "shsc", [128, 2 * KD], F32)
        scfm = sb(st0, "scfm", [128, KD], F32)
        gate_bc = sb(st0, "gate_bc", [128, DM], F32)
        xt = sb(st0, "xt", [128, DM], F32)
        xn = sb(st0, "xn", [128, DM], BF16)
        junk = xn
        ss = sb(st0, "ss", [128, 1], F32)
        rstd = sb(st0, "rstd", [128, 1], F32)
        wst = [sb(st0, "wst%d" % i, [128, KD, 128], F32) for i in range(2)]
        wbf = [sb(st0, "wbf%d" % i, [128, KD, 128], BF16) for i in range(4)]

        ident_bf = cb[:, C_ID, :]
        ones_bf = cb[:, C_ONE, :]
        ones_f = cf[:, C_ONE, :]

        def bc4(ap2d):
            return ap2d.unsqueeze(1).to_broadcast([128, 4, 128])

        P.op("sp", lambda e: e.dma_start(out=cf[:, :, :].rearrange("p a b -> p (a b)"), in_=constm[:, 0:6 * 128]),
             writes=["cf"], dma="cf")
        P.op("sp", lambda e: e.dma_start(out=flg[:, :], in_=flags_bc[:, :]), writes=["flg"], dma="flg")
        P.op("dve", lambda e: e.memset(epsc[:, :], 1e-6), writes=["epsc"])

        with contextlib.ExitStack() as st:
            cfull = sb(st, "cfull", [128, NCONST, 128], F32)
            P.op("sp", lambda e: e.dma_start(out=cfull[:, :, :].rearrange("p a b -> p (a b)"), in_=constm[:, :]),
                 writes=["cfull"], dma="cfull")
            P.op("dve", lambda e: e.tensor_copy(out=cb[:, :, :], in_=cfull[:, :, :]), reads=["cfull"], writes=["cb"])
            cT = sb(st, "cT", [128, KD], F32)
            scs = sb(st, "scs", [128, KD], F32)
            nwf = sb(st, "nwf", [128, KD], F32)
            wa = sb(st, "wa", [128, 3 * DM], F32)
            bada = sb(st, "bada", [1, 3 * DM], F32)
            modrow = sb(st, "modrow", [1, 3 * DM], F32)
            P.op("sp", lambda e: e.dma_start(out=cT[:, :], in_=c_fm[:, :]), writes=["cT"], dma="cT")
            P.op("sp", lambda e: e.dma_start(out=nwf[:, :], in_=normw_fm[:, :]), writes=["nwf"], dma="nwf")
            P.op("sp", lambda e: e.dma_start(out=bada[:, :], in_=b_ada[:, :]), writes=["bada"], dma="bada")
            P.op("act", lambda e: e.activation(out=scs[:, :], in_=cT[:, :], func=AF.Silu), reads=["cT"], writes=["scs"])
            cblocks = blocks(3 * DM)
            banks = ["A", "B", "G", "K", "Y", "R1"]
            assert len(cblocks) <= len(banks)
            for d in range(KD):
                P.op("sp", lambda e, d=d: e.dma_start(out=wa[:, :], in_=w_ada[d * 128:(d + 1) * 128, :]),
                     writes=["wa"], dma="wa")
                for i, (c0, c1) in enumerate(cblocks):
                    P.op("pe", lambda e, d=d, i=i, c0=c0, c1=c1: e.matmul(
                        out=pb[banks[i]][0:1, 0:c1 - c0], lhsT=scs[:, d:d + 1], rhs=wa[:, c0:c1],
                        start=(d == 0), stop=(d == KD - 1)), reads=["scs", "wa"], writes=["pb" + banks[i]])
            for i, (c0, c1) in enumerate(cblocks):
                P.op("dve", lambda e, i=i, c0=c0, c1=c1: e.tensor_tensor(
                    out=modrow[0:1, c0:c1], in0=pb[banks[i]][0:1, 0:c1 - c0], in1=bada[0:1, c0:c1], op=ALU.add),
                    reads=["pb" + banks[i], "bada"], writes=["modrow"])
            for i in range(2 * KD):
                P.op("pe", lambda e, i=i: e.matmul(out=pb["A"][:, i:i + 1], lhsT=modrow[0:1, i * 128:(i + 1) * 128],
                                                   rhs=cf[0:1, C_ONE, 0:1], start=True, stop=True),
                     reads=["modrow", "cf"], writes=["pbA"])
            P.op("dve", lambda e: e.tensor_copy(out=shsc[:, :], in_=pb["A"][:, 0:2 * KD]), reads=["pbA"], writes=["shsc"])
            P.op("dve", lambda e: e.scalar_tensor_tensor(out=scfm[:, :], in0=shsc[:, KD:2 * KD], scalar=1.0, in1=nwf[:, :],
                                                         op0=ALU.add, op1=ALU.mult),
                 reads=["shsc", "nwf"], writes=["scfm"])
            for i, (c0, c1) in enumerate(blocks(DM)):
                bk = banks[1 + (i % 2)]
                P.op("pe", lambda e, bk=bk, c0=c0, c1=c1: e.matmul(
                    out=pb[bk][:, 0:c1 - c0], lhsT=cf[0:1, C_ONE, :], rhs=modrow[0:1, 2 * DM + c0:2 * DM + c1],
                    start=True, stop=True), reads=["modrow", "cf"], writes=["pb" + bk])
                P.op("act", lambda e, bk=bk, c0=c0, c1=c1: e.copy(out=gate_bc[:, c0:c1], in_=pb[bk][:, 0:c1 - c0]),
                     reads=["pb" + bk], writes=["gate_bc"])
            P.barrier(bar_t, cf, pb["R2"], flags_bc[0:1, 0:4])

        def load_w_chunk(slot, src_ap, q="sp", cast_eng="dve"):
            ws_ = slot % 2
            P.op(q, lambda e: e.dma_start(out=wst[ws_][:, :, :], in_=src_ap.rearrange("(k p) c -> p k c", p=128)),
                 writes=["wst%d" % ws_], dma="wst%d" % ws_)
            P.op(cast_eng, lambda e: (e.copy(out=wbf[slot][:, :, :], in_=wst[ws_][:, :, :]) if cast_eng == "act"
                                      else e.tensor_copy(out=wbf[slot][:, :, :], in_=wst[ws_][:, :, :])),
                 reads=["wst%d" % ws_], writes=["wbf%d" % slot])

        def stage_a(p):
            for t in range(NT + 1):
                rows = 128 if t < NT else 4
                P.op("sp", lambda e, t=t, rows=rows: e.dma_start(out=xt[0:rows, :], in_=xs[p, t * 128:t * 128 + rows, :]),
                     writes=["xt"], dma="xt")
                P.op("act", lambda e, rows=rows: e.activation(out=junk[0:rows, :], in_=xt[0:rows, :], func=AF.Square,
                                                              accum_out=ss[0:rows, :]), reads=["xt"], writes=["xn", "ss"])
                P.op("dve", lambda e, rows=rows: e.tensor_scalar(out=rstd[0:rows, :], in0=ss[0:rows, :], scalar1=1.0 / DM,
                                                                 scalar2=1e-6, op0=ALU.mult, op1=ALU.add),
                     reads=["ss"], writes=["rstd"])
                P.op("act", lambda e, rows=rows: e.sqrt(out=rstd[0:rows, :], in_=rstd[0:rows, :]), reads=["rstd"], writes=["rstd"])
                P.op("dve", lambda e, rows=rows: e.reciprocal(out=rstd[0:rows, :], in_=rstd[0:rows, :]), reads=["rstd"], writes=["rstd"])
                P.op("dve", lambda e, rows=rows: e.tensor_scalar(out=xn[0:rows, :], in0=xt[0:rows, :], scalar1=rstd[0:rows, 0:1],
                                                                 scalar2=None, op0=ALU.mult), reads=["xt", "rstd"], writes=["xn"])
                for d in range(KD):
                    P.op("pe", lambda e, d=d, rows=rows: e.transpose(out=pbT[:, d, 0:rows], in_=xn[0:rows, d * 128:(d + 1) * 128],
                                                                     identity=ident_bf[0:rows, 0:rows]),
                         reads=["xn", "cb"], writes=["pbT"])
                for d in range(KD):
                    P.op("act", lambda e, d=d, t=t, rows=rows: e.activation(
                        out=hT[:, d, t * 128:t * 128 + rows], in_=pbT[:, d, 0:rows], func=AF.Identity,
                        scale=scfm[:, d:d + 1], bias=shsc[:, d:d + 1]), reads=["pbT", "scfm", "shsc"], writes=["hT"])
                yield

        with contextlib.ExitStack() as st:
            cwt = sb(st, "cwt", [128, 5, 3 * NH, 5], F32)
            alg = sb(st, "alg", [128, 5, NH], F32)
            dtb = sb(st, "dtb", [128, 5, NH], F32)
            wabf = sb(st, "wabf", [128, KD, 2 * NH], F32)
            wabb = sb(st, "wabb", [128, KD, 2 * NH], BF16)
            xa = sb(st, "xa", [128, NT, NH], F32)
            aexp = sb(st, "aexp", [128, NH], F32)
            gall2 = [sb(st, "gall%d" % i, [128, NT, NH], F32) for i in range(2)]
            ball2 = [sb(st, "ball%d" % i, [128, NT, NH], F32) for i in range(2)]
            Sst = sb(st, "Sst", [128, NH, 128], F32)
            Sf = sb(st, "Sf", [128, NH, 128], F32)
            Sb = sb(st, "Sb", [128, NH, 128], F32)
            Sbf = [sb(st, "Sbf%d" % i, [128, 128], BF16) for i in range(2)]
            pre = sb(st, "pre", [128, SP4], BF16)
            cTk2 = [sb(st, "cTk%d" % i, [128, SEG], BF16) for i in range(2)]
            cTv2 = [sb(st, "cTv%d" % i, [128, SEG], BF16) for i in range(2)]
            cTq2 = [sb(st, "cTq%d" % i, [128, SEG], BF16) for i in range(2)]
            dg = sb(st, "dg", [128, 3, 5, 128], BF16)
            sqt = sb(st, "sqt", [128, 512], BF16)
            rkT = sb(st, "rkT", [128, 512], F32)
            g4 = lambda n, dt: sb(st, n, [128, 4, 128], dt)
            kn_tok, v_tok = g4("kn_tok", BF16), g4("v_tok", BF16)
            rhs_g, GcM = g4("rhs_g", F32), g4("GcM", F32)
            dm = rhs_g
            Dm, eGR, bM, Dp, Am = g4("Dm", BF16), g4("eGR", BF16), g4("bM", BF16), g4("Dp", BF16), g4("Am", BF16)
            Rv, Rk = g4("Rv", BF16), g4("Rk", BF16)
            Lo = [g4("Lo%d" % G, BF16) for G in range(NG)]
            LoT = [g4("LoT%d" % G, BF16) for G in range(NG)]
            Lm = [g4("Lm%d" % G, BF16) for G in range(NG)]
            Um = [g4("Um%d" % G, BF16) for G in range(NG)]
            Yn = [g4("Yn%d" % G, BF16) for G in range(NG)]
            Ynp = [g4("Ynp%d" % G, BF16) for G in range(NG)]
            Xb = [[g4("X%d_0" % G, BF16)] * 2 for G in range(NG)]
            XTb = [[g4("XT%d_0" % G, BF16)] * 2 for G in range(NG)]
            Gc = [sb(st, "Gc%d" % G, [128, 4], F32) for G in range(NG)]
            eGc = [sb(st, "eGc%d" % G, [128, 4], F32) for G in range(NG)]
            kdf = [sb(st, "kdf%d" % G, [128, 4], F32) for G in range(NG)]
            bE = [sb(st, "bE%d" % G, [128, 4], F32) for G in range(NG)]
            hp = lambda n, dt: [sb(st, "%s%d" % (n, i), [128, NT, 128], dt) for i in range(2)]
            wT_all, u_all, kd_all, qdT_all, AT_all = hp("wTa", BF16), hp("ua", BF16), hp("kda", BF16), hp("qdTa", BF16), hp("ATa", BF16)
            gl_all = [sb(st, "gla%d" % i, [128, NT], F32) for i in range(2)]
            vnew = sb(st, "vnew", [128, 128], BF16)
            osb = [sb(st, "osb%d" % i, [128, 128], F32) for i in range(2)]

            if debug:
                print("GDN-phase SBUF bytes remaining/partition:", nc.sbuf_bytes_remaining)
            P.op("sp", lambda e: e.dma_start(out=cwt[:, :, :, :].rearrange("p a b c -> p (a b c)"), in_=cw_fm[:, :]),
                 writes=["cwt"], dma="cwt")
            P.op("sp", lambda e: e.dma_start(out=alg[:, :, :].rearrange("p a b -> p (a b)"), in_=alog_bc[:, :]),
                 writes=["alg"], dma="alg")
            P.op("sp", lambda e: e.dma_start(out=dtb[:, :, :].rearrange("p a b -> p (a b)"), in_=dtb_bc[:, :]),
                 writes=["dtb"], dma="dtb")
            P.op("dve", lambda e: e.memset(Sst[:, :, :], 0.0), writes=["S%d" % h for h in range(NH)])
            P.op("dve", lambda e: e.memset(Sf[:, :, :], 0.0), writes=["Sf%d" % h for h in range(NH)])
            P.op("dve", lambda e: e.memset(Sb[:, :, :], 0.0), writes=["Sb%d" % h for h in range(NH)])

            def pass_setup(p):
                gall, ball, kga, kba = gall2[p % 2], ball2[p % 2], "gall%d" % (p % 2), "ball%d" % (p % 2)
                yield from stage_a(p)
                P.op("sp", lambda e: e.dma_start(out=wabf[:, :, :], in_=w_ab[p].rearrange("(k p) c -> p k c", p=128)),
                     writes=["wabf"], dma="wabf")
                P.op("dve", lambda e: e.tensor_copy(out=wabb[:, :, :], in_=wabf[:, :, :]), reads=["wabf"], writes=["wabb"])
                for t in range(NT):
                    for d in range(KD):
                        P.op("pe", lambda e, t=t, d=d: e.matmul(
                            out=pb["K"][:, t * 2 * NH:(t + 1) * 2 * NH], lhsT=hT[:, d, 2 + 128 * t:2 + 128 * t + 128],
                            rhs=wabb[:, d, :], start=(d == 0), stop=(d == KD - 1)), reads=["hT", "wabb"], writes=["pbK"])
                abv = pb["K"][:, 0:NT * 2 * NH].rearrange("p (t c) -> p t c", c=2 * NH)
                P.op("dve", lambda e: e.tensor_tensor(out=xa[:, :, :], in0=abv[:, :, 0:NH],
                                                      in1=dtb[:, p:p + 1, :].to_broadcast([128, NT, NH]), op=ALU.add),
                     reads=["pbK", "dtb"], writes=["xa"])
                P.op("act", lambda e: e.activation(out=ball[:, :, :], in_=abv[:, :, NH:2 * NH], func=AF.Sigmoid),
                     reads=["pbK"], writes=[kba])
                P.op("dve", lambda e: e.tensor_scalar_min(out=xa[:, :, :], in0=xa[:, :, :], scalar1=30.0), reads=["xa"], writes=["xa"])
                P.op("act", lambda e: e.activation(out=xa[:, :, :], in_=xa[:, :, :], func=AF.Exp), reads=["xa"], writes=["xa"])
                P.op("act", lambda e: e.activation(out=xa[:, :, :], in_=xa[:, :, :], func=AF.Ln, bias=cf[:, C_ONE, 0:1], scale=1.0),
                     reads=["xa", "cf"], writes=["xa"])
                P.op("act", lambda e: e.activation(out=aexp[:, :], in_=alg[:, p, :], func=AF.Exp), reads=["alg"], writes=["aexp"])
                P.op("dve", lambda e: e.scalar_tensor_tensor(out=gall[:, :, :], in0=xa[:, :, :], scalar=-1.0,
                                                             in1=aexp[:, :].unsqueeze(1).to_broadcast([128, NT, NH]),
                                                             op0=ALU.mult, op1=ALU.mult), reads=["xa", "aexp"], writes=[kga])
                yield

            def prepA(p, h, par, out):
                cTk, cTv, cTq = cTk2[par], cTv2[par], cTq2[par]
                names = [("k", cfg.C_K, cTk), ("v", cfg.C_V, cTv)] + ([("q", cfg.C_Q, cTq)] if out else [])
                for wi, (nm, coff, cT_) in enumerate(names):
                    load_w_chunk(wi, w_in[:, coff + h * 128:coff + (h + 1) * 128])
                    chunk = {"k": NH + h, "v": 2 * NH + h, "q": h}[nm]
                    for tap in range(5):
                        P.op("dve", lambda e, wi=wi, tap=tap, chunk=chunk: e.tensor_scalar(
                            out=dg[:, wi, tap, :], in0=ident_bf, scalar1=cwt[:, p, chunk, tap:tap + 1], scalar2=None,
                            op0=ALU.mult), reads=["cb", "cwt"], writes=["dg%d" % wi])
                yield
                for wi, (nm, coff, cT_) in enumerate(names):
                    for bi, (c0, c1) in enumerate(blocks(SP4)):
                        for d in range(KD):
                            P.op("pe", lambda e, wi=wi, d=d, c0=c0, c1=c1: e.matmul(
                                out=pb["A"][:, 0:c1 - c0], lhsT=wbf[wi][:, d, :], rhs=hT[:, d, c0:c1],
                                start=(d == 0), stop=(d == KD - 1)), reads=["wbf%d" % wi, "hT"], writes=["pbA"])
                        if bi % 2 == 0:
                            P.op("act", lambda e, c0=c0, c1=c1: e.copy(out=pre[:, c0:c1], in_=pb["A"][:, 0:c1 - c0]),
                                 reads=["pbA"], writes=["pre"])
                        else:
                            P.op("dve", lambda e, c0=c0, c1=c1: e.tensor_copy(out=pre[:, c0:c1], in_=pb["A"][:, 0:c1 - c0]),
                                 reads=["pbA"], writes=["pre"])
                        yield
                    P.op("dve", lambda e: e.tensor_scalar(out=pre[:, 0:2], in0=pre[:, 0:2], scalar1=flg[:, F_LO(p):F_LO(p) + 1],
                                                          scalar2=None, op0=ALU.mult), reads=["pre", "flg"], writes=["pre"])
                    P.op("dve", lambda e: e.tensor_scalar(out=pre[:, SEG + 2:SP4], in0=pre[:, SEG + 2:SP4],
                                                          scalar1=flg[:, F_HI(p):F_HI(p) + 1], scalar2=None, op0=ALU.mult),
                         reads=["pre", "flg"], writes=["pre"])
                    for (o0, o1) in blocks(SEG):
                        for tap in range(5):
                            P.op("pe", lambda e, wi=wi, tap=tap, o0=o0, o1=o1: e.matmul(
                                out=pb["R2"][:, 0:o1 - o0], lhsT=dg[:, wi, tap, :], rhs=pre[:, o0 + tap:o1 + tap],
                                start=(tap == 0), stop=(tap == 4)), reads=["dg%d" % wi, "pre"], writes=["pbR2"])
                        P.op("act", lambda e, cT_=cT_, o0=o0, o1=o1: e.activation(out=cT_[:, o0:o1], in_=pb["R2"][:, 0:o1 - o0],
                                                                                   func=AF.Silu), reads=["pbR2"], writes=["cT%s%d" % (nm, par)])
                        yield
                for nm, cT_ in ([("k", cTk)] + ([("q", cTq)] if out else [])):
                    for (o0, o1) in blocks(SEG):
                        n = o1 - o0
                        P.op("act", lambda e, cT_=cT_, o0=o0, o1=o1, n=n: e.activation(out=sqt[:, 0:n], in_=cT_[:, o0:o1], func=AF.Square),
                             reads=["cT%s%d" % (nm, par)], writes=["sqt"])
                        P.op("pe", lambda e, n=n: e.matmul(out=pb["R2"][:, 0:n], lhsT=ones_bf, rhs=sqt[:, 0:n], start=True, stop=True),
                             reads=["sqt", "cb"], writes=["pbR2"])
                        P.op("act", lambda e, n=n: e.activation(out=rkT[:, 0:n], in_=pb["R2"][:, 0:n], func=AF.Ln,
                                                                bias=epsc[:, 0:1], scale=1.0), reads=["pbR2", "epsc"], writes=["rkT"])
                        P.op("act", lambda e, n=n: e.activation(out=rkT[:, 0:n], in_=rkT[:, 0:n], func=AF.Exp, scale=-0.5),
                             reads=["rkT"], writes=["rkT"])
                        if nm == "k":
                            P.op("dve", lambda e, cT_=cT_, o0=o0, o1=o1, n=n: e.tensor_tensor(
                                out=cT_[:, o0:o1], in0=cT_[:, o0:o1], in1=rkT[:, 0:n], op=ALU.mult),
                                reads=["cT%s%d" % (nm, par), "rkT"], writes=["cT%s%d" % (nm, par)])
                        else:
                            P.op("dve", lambda e, cT_=cT_, o0=o0, o1=o1, n=n: e.scalar_tensor_tensor(
                                out=cT_[:, o0:o1], in0=cT_[:, o0:o1], scalar=128.0 ** -0.5, in1=rkT[:, 0:n],
                                op0=ALU.mult, op1=ALU.mult), reads=["cT%s%d" % (nm, par), "rkT"], writes=["cT%s%d" % (nm, par)])
                        yield
            def prepB(p, h, par, out):
                cTk, cTv, cTq = cTk2[par], cTv2[par], cTq2[par]
                kck, kcv, kcq = "cTk%d" % par, "cTv%d" % par, "cTq%d" % par
                gall, ball, kga, kba = gall2[p % 2], ball2[p % 2], "gall%d" % (p % 2), "ball%d" % (p % 2)
                v4 = lambda ps: ps[:, :].rearrange("p (g f) -> p g f", f=128)
                PK, PG = pb["K"], pb["G"]

                def chain(G):
                    T0 = 4 * G
                    tc = lambda g: slice((T0 + g) * 128, (T0 + g + 1) * 128)
                    kG = "_%d" % G
                    gsl = gall[:, T0:T0 + 4, h:h + 1].to_broadcast([128, 4, 128])
                    bsl = ball[:, T0:T0 + 4, h:h + 1].to_broadcast([128, 4, 128])
                    P.op("dve", lambda e: e.tensor_tensor(out=rhs_g[:, :, :], in0=bc4(cf[:, C_MLOW, :]), in1=gsl, op=ALU.mult),
                         reads=["cf", kga], writes=["rhs_g"])
                    for g in range(4):
                        P.op("pe", lambda e, g=g: e.matmul(out=PG[:, g * 128:(g + 1) * 128], lhsT=ones_f, rhs=rhs_g[:, g, :],
                                                           start=True, stop=True), reads=["cf", "rhs_g"], writes=["pbG"])
                    for g in range(4):
                        P.op("pe", lambda e, g=g: e.matmul(out=PK[:, g:g + 1], lhsT=rhs_g[:, g, :], rhs=cf[:, C_ONE, 0:1],
                                                           start=True, stop=True), reads=["cf", "rhs_g"], writes=["pbK"])
                    P.op("dve", lambda e: e.tensor_copy(out=Gc[G][:, :], in_=PK[:, 0:4]), reads=["pbK"], writes=["Gc" + kG])
                    P.op("dve", lambda e: e.tensor_tensor(out=GcM[:, :, :], in0=bc4(cf[:, C_NEG, :]),
                                                          in1=Gc[G][:, :].unsqueeze(2).to_broadcast([128, 4, 128]), op=ALU.add),
                         reads=["cf", "Gc" + kG], writes=["GcM"])
                    P.op("dve", lambda e: e.scalar_tensor_tensor(out=dm[:, :, :], in0=v4(PG), scalar=-1.0, in1=GcM[:, :, :],
                                                                 op0=ALU.mult, op1=ALU.add), reads=["pbG", "GcM"], writes=["rhs_g"])
                    P.op("act", lambda e: e.activation(out=Dm[:, :, :], in_=dm[:, :, :], func=AF.Exp), reads=["rhs_g"], writes=["Dm"])
                    if out:
                        P.op("act", lambda e: e.activation(out=eGR[:, :, :], in_=v4(PG), func=AF.Exp), reads=["pbG"], writes=["eGR"])
                    P.op("act", lambda e: e.activation(out=eGc[G][:, :], in_=Gc[G][:, :], func=AF.Exp), reads=["Gc" + kG], writes=["eGc" + kG])
                    P.op("dve", lambda e: e.tensor_tensor(out=kdf[G][:, :], in0=v4(PG)[:, :, 127], in1=Gc[G][:, :], op=ALU.subtract),
                         reads=["pbG", "Gc" + kG], writes=["kdf" + kG])
                    P.op("act", lambda e: e.activation(out=kdf[G][:, :], in_=kdf[G][:, :], func=AF.Exp), reads=["kdf" + kG], writes=["kdf" + kG])
                    P.op("act", lambda e: e.activation(out=gl_all[par][:, T0:T0 + 4], in_=v4(PG)[:, :, 127], func=AF.Exp),
                         reads=["pbG"], writes=["gl%d" % par])
                    P.op("dve", lambda e: e.tensor_tensor(out=bM[:, :, :], in0=bc4(cb[:, C_STRICT, :]), in1=bsl, op=ALU.mult),
                         reads=["cb", kba], writes=["bM"])
                    P.op("dve", lambda e: e.tensor_tensor(out=Dp[:, :, :], in0=Dm[:, :, :], in1=bM[:, :, :], op=ALU.mult),
                         reads=["Dm", "bM"], writes=["Dp"])
                    for g in range(4):
                        P.op("pe", lambda e, g=g: e.matmul(out=PK[:, g * 128:(g + 1) * 128], lhsT=cTk[:, tc(g)], rhs=cTk[:, tc(g)],
                                                           start=True, stop=True), reads=[kck], writes=["pbK"])
                    P.op("dve", lambda e: e.tensor_tensor(out=Lm[G][:, :, :], in0=v4(PK), in1=Dp[:, :, :], op=ALU.mult),
                         reads=["pbK", "Dp"], writes=["Lm" + kG])
                    for g in range(4):
                        P.op("pe", lambda e, g=g: e.transpose(out=pbT[:, g, :], in_=Lm[G][:, g, :], identity=ident_bf),
                             reads=["Lm" + kG, "cb"], writes=["pbT"])
                    P.op("act", lambda e: e.copy(out=Um[G][:, :, :], in_=pbT[:, 0:4, :]), reads=["pbT"], writes=["Um" + kG])
                    if out:
                        for g in range(4):
                            P.op("pe", lambda e, g=g: e.matmul(out=PK[:, g * 128:(g + 1) * 128], lhsT=cTq[:, tc(g)], rhs=cTk[:, tc(g)],
                                                               start=True, stop=True), reads=[kcq, kck], writes=["pbK"])
                        P.op("dve", lambda e: e.tensor_tensor(out=Am[:, :, :], in0=v4(PK), in1=Dm[:, :, :], op=ALU.mult),
                             reads=["pbK", "Dm"], writes=["Am"])
                        for g in range(4):
                            P.op("pe", lambda e, g=g: e.transpose(out=pbT[:, 4 + g, :], in_=Am[:, g, :], identity=ident_bf),
                                 reads=["Am", "cb"], writes=["pbT"])
                        P.op("act", lambda e: e.copy(out=AT_all[par][:, T0:T0 + 4, :], in_=pbT[:, 4:8, :]),
                             reads=["pbT"], writes=["AT%d" % par])
                        P.op("dve", lambda e: e.tensor_tensor(
                            out=qdT_all[par][:, T0:T0 + 4, :], in0=cTq[:, T0 * 128:(T0 + 4) * 128].rearrange("p (g f) -> p g f", f=128),
                            in1=eGR[:, :, :], op=ALU.mult), reads=[kcq, "eGR"], writes=["qdT%d" % par])
                        yield
                    b1, b2 = (pb["Y"], pb["B"]) if G % 2 == 0 else (pb["G"], pb["K"])
                    n1, n2 = ("pbY", "pbB") if G % 2 == 0 else ("pbG", "pbK")
                    X, XT = Xb[G][0], XTb[G][0]
                    kx, kxt = "X%d_0" % G, "XT%d_0" % G
                    for li, s in enumerate(LEVELS):
                        P.op("dve", lambda e, li=li: e.tensor_tensor(out=Lo[G][:, :, :], in0=Lm[G][:, :, :], in1=bc4(cb[:, C_MM + li, :]), op=ALU.mult),
                             reads=["Lm" + kG, "cb"], writes=["Lo" + kG])
                        if s == 1:
                            P.op("dve", lambda e: e.tensor_tensor(out=LoT[G][:, :, :], in0=Um[G][:, :, :], in1=bc4(cb[:, C_MMT, :]), op=ALU.mult),
                                 reads=["Um" + kG, "cb"], writes=["LoT" + kG])
                            P.op("dve", lambda e: e.scalar_tensor_tensor(out=X[:, :, :], in0=Lo[G][:, :, :], scalar=-1.0, in1=bc4(ident_bf),
                                                                         op0=ALU.mult, op1=ALU.add), reads=["Lo" + kG, "cb"], writes=[kx])
                            P.op("dve", lambda e: e.scalar_tensor_tensor(out=XT[:, :, :], in0=LoT[G][:, :, :], scalar=-1.0, in1=bc4(ident_bf),
                                                                         op0=ALU.mult, op1=ALU.add), reads=["LoT" + kG, "cb"], writes=[kxt])
                            yield
                            continue
                        for g in range(4):
                            P.op("pe", lambda e, g=g: e.matmul(out=b2[:, g * 128:(g + 1) * 128], lhsT=Lo[G][:, g, :], rhs=XT[:, g, :],
                                                               start=True, stop=True), reads=["Lo" + kG, kxt], writes=[n2])
                        P.op("act", lambda e: e.mul(out=Ynp[G][:, :, :], in_=v4(b2), mul=-1.0), reads=[n2], writes=["Ynp" + kG])
                        yield
                        for g in range(4):
                            P.op("pe", lambda e, g=g: e.matmul(out=b1[:, g * 128:(g + 1) * 128], lhsT=X[:, g, :], rhs=Ynp[G][:, g, :],
                                                               start=True, stop=True), reads=[kx, "Ynp" + kG], writes=[n1])
                        P.op("dve", lambda e: e.tensor_tensor(out=XT[:, :, :], in0=v4(b1), in1=XT[:, :, :], op=ALU.add), reads=[n1, kxt], writes=[kxt])
                        if s != LEVELS[-1]:
                            hh = 4 * (G % 2)
                            for g in range(4):
                                P.op("pe", lambda e, g=g: e.transpose(out=pbT[:, hh + g, :], in_=XT[:, g, :], identity=ident_bf),
                                     reads=[kxt, "cb"], writes=["pbT"])
                            P.op("act", lambda e: e.copy(out=X[:, :, :], in_=pbT[:, hh:hh + 4, :]), reads=["pbT"], writes=[kx])
                        yield
                    for g in range(4):
                        P.op("pe", lambda e, g=g: e.transpose(out=pbT[:, g, :], in_=cTk[:, tc(g)], identity=ident_bf),
                             reads=[kck, "cb"], writes=["pbT"])
                    P.op("act", lambda e: e.copy(out=kn_tok[:, :, :], in_=pbT[:, 0:4, :]), reads=["pbT"], writes=["kn_tok"])
                    for g in range(4):
                        P.op("pe", lambda e, g=g: e.transpose(out=pbT[:, 4 + g, :], in_=cTv[:, tc(g)], identity=ident_bf),
                             reads=[kcv, "cb"], writes=["pbT"])
                    P.op("act", lambda e: e.copy(out=v_tok[:, :, :], in_=pbT[:, 4:8, :]), reads=["pbT"], writes=["v_tok"])
                    P.op("dve", lambda e: e.tensor_tensor(out=Rv[:, :, :], in0=v_tok[:, :, :], in1=bsl, op=ALU.mult),
                         reads=["v_tok", kba], writes=["Rv"])
                    P.op("dve", lambda e: e.tensor_tensor(out=bE[G][:, :], in0=ball[:, T0:T0 + 4, h], in1=eGc[G][:, :], op=ALU.mult),
                         reads=[kba, "eGc" + kG], writes=["bE" + kG])
                    P.op("dve", lambda e: e.tensor_tensor(out=Rk[:, :, :], in0=kn_tok[:, :, :],
                                                          in1=bE[G][:, :].unsqueeze(2).to_broadcast([128, 4, 128]), op=ALU.mult),
                         reads=["kn_tok", "bE" + kG], writes=["Rk"])
                    for g in range(4):
                        P.op("pe", lambda e, g=g, XT=XT: e.matmul(out=b1[:, g * 128:(g + 1) * 128], lhsT=XT[:, g, :], rhs=Rv[:, g, :],
                                                                  start=True, stop=True), reads=[kxt, "Rv"], writes=[n1])
                    P.op("act", lambda e: e.copy(out=u_all[par][:, T0:T0 + 4, :], in_=v4(b1)), reads=[n1], writes=["u%d" % par])
                    for g in range(4):
                        P.op("pe", lambda e, g=g, XT=XT: e.matmul(out=b2[:, g * 128:(g + 1) * 128], lhsT=Rk[:, g, :], rhs=XT[:, g, :],
                                                                  start=True, stop=True), reads=[kxt, "Rk"], writes=[n2])
                    P.op("act", lambda e: e.copy(out=wT_all[par][:, T0:T0 + 4, :], in_=v4(b2)), reads=[n2], writes=["wT%d" % par])
                    P.op("dve", lambda e: e.tensor_tensor(out=kd_all[par][:, T0:T0 + 4, :], in0=kn_tok[:, :, :],
                                                           in1=kdf[G][:, :].unsqueeze(2).to_broadcast([128, 4, 128]), op=ALU.mult),
                         reads=["kn_tok", "kdf" + kG], writes=["kd%d" % par])

                    yield

                chains = [chain(G) for G in range(NG)]
                while chains:
                    alive = []
                    for c in chains:
                        try:
                            next(c)
                            alive.append(c)
                        except StopIteration:
                            pass
                    chains = alive
                    yield

            def rec(p, h, par, out):
                S_h, Sf_h, Sb_h = Sst[:, h, :], Sf[:, h, :], Sb[:, h, :]
                kS, kSf, kSb = "S%d" % h, "Sf%d" % h, "Sb%d" % h
                sbf = Sbf[par]
                ksbf = "Sbf%d" % par
                if p in (1, 2):
                    P.op("dve", lambda e: e.tensor_scalar(out=S_h, in0=S_h, scalar1=flg[:, F_CARRY + p:F_CARRY + p + 1], scalar2=None,
                                                          op0=ALU.mult), reads=[kS, "flg"], writes=[kS])
                elif p == 3:
                    P.op("dve", lambda e: e.tensor_copy(out=S_h, in_=Sb_h), reads=[kSb], writes=[kS])
                elif p == 4:
                    P.op("dve", lambda e: e.tensor_copy(out=S_h, in_=Sf_h), reads=[kSf], writes=[kS])
                P.op("act", lambda e: e.copy(out=sbf[:, :], in_=S_h), reads=[kS], writes=[ksbf])
                for t in range(NT):
                    P.op("pe", lambda e, t=t: e.matmul(out=pb["R1"][:, 0:128], lhsT=wT_all[par][:, t, :], rhs=sbf[:, :], start=True, stop=True),
                         reads=["wT%d" % par, ksbf], writes=["pbR1"])
                    P.op("dve", lambda e, t=t: e.tensor_tensor(out=vnew[:, :], in0=u_all[par][:, t, :], in1=pb["R1"][:, 0:128], op=ALU.subtract),
                         reads=["u%d" % par, "pbR1"], writes=["vnew"])
                    if out:
                        ob = osb[t % 2]
                        P.op("pe", lambda e, t=t: e.matmul(out=pb["R1"][:, 256:384], lhsT=qdT_all[par][:, t, :], rhs=sbf[:, :], start=True, stop=False),
                             reads=["qdT%d" % par, ksbf], writes=["pbR1"])
                        P.op("pe", lambda e, t=t: e.matmul(out=pb["R1"][:, 256:384], lhsT=AT_all[par][:, t, :], rhs=vnew[:, :], start=False, stop=True),
                             reads=["AT%d" % par, "vnew"], writes=["pbR1"])
                        P.op("act", lambda e, ob=ob: e.copy(out=ob[:, :], in_=pb["R1"][:, 256:384]), reads=["pbR1"], writes=["osb%d" % (t % 2)])
                        P.op("sp", lambda e, ob=ob, t=t: e.dma_start(out=oscr[p - 3, t * 128:(t + 1) * 128, h * 128:(h + 1) * 128], in_=ob[:, :]),
                             reads=["osb%d" % (t % 2)], writes=["oscr%d_%d_%d" % (p - 3, t, h)], dma="osb%d" % (t % 2))
                    P.op("pe", lambda e, t=t: e.matmul(out=pb["R1"][:, 128:256], lhsT=kd_all[par][:, t, :], rhs=vnew[:, :], start=True, stop=True),
                         reads=["kd%d" % par, "vnew"], writes=["pbR1"])
                    P.op("dve", lambda e, t=t: e.scalar_tensor_tensor(out=S_h, in0=S_h, scalar=gl_all[par][:, t:t + 1], in1=pb["R1"][:, 128:256],
                                                                      op0=ALU.mult, op1=ALU.add), reads=[kS, "gl%d" % par, "pbR1"], writes=[kS])
                    P.op("act", lambda e: e.copy(out=sbf[:, :], in_=S_h), reads=[kS], writes=[ksbf])
                    yield
                if p < 3:
                    P.op("dve", lambda e: e.scalar_tensor_tensor(out=Sf_h, in0=S_h, scalar=flg[:, F_F + p:F_F + p + 1], in1=Sf_h,
                                                                 op0=ALU.mult, op1=ALU.add), reads=[kS, kSf, "flg"], writes=[kSf])
                    P.op("dve", lambda e: e.scalar_tensor_tensor(out=Sb_h, in0=S_h, scalar=flg[:, F_B + p:F_B + p + 1], in1=Sb_h,
                                                                 op0=ALU.mult, op1=ALU.add), reads=[kS, kSb, "flg"], writes=[kSb])
                yield

            def prepA_task(p, h, par, out):
                if h == 0:
                    yield from pass_setup(p)
                yield from prepA(p, h, par, out)

            seq = [(p, h, k % 2, p >= 3) for k, (p, h) in enumerate((p, h) for p in range(5) for h in range(NH))]
            n = len(seq)
            for step in range(n + 2):
                gens = []
                if step < n:
                    gens.append(prepA_task(*seq[step]))
                if 0 <= step - 1 < n:
                    gens.append(prepB(*seq[step - 1]))
                if 0 <= step - 2 < n:
                    gens.append(rec(*seq[step - 2]))
                interleave(gens)

            P.barrier(bar_t, cf, pb["R2"], flags_bc[0:1, 0:4])

        with contextlib.ExitStack() as st:
            stash1 = sb(st, "stash1", [128, KD, SEG], BF16)
            stash2 = sb(st, "stash2", [128, KD, SEG], BF16)
            wres_f = sb(st, "wres_f", [128, DM], F32)
            wres = sb(st, "wres", [128, KD, DM], BF16)
            cwa = sb(st, "cwa", [128, KD, 3], F32)
            dgA = sb(st, "dgA", [128, 3, 128], BF16)
            pA = sb(st, "pA", [128, SEG + 2], BF16)
            tf1 = sb(st, "tf1", [128, 512], F32)
            tf2 = sb(st, "tf2", [128, 512], F32)
            tf3 = sb(st, "tf3", [128, 512], F32)
            gnw = sb(st, "gnw", [128, 128], F32)
            fnw = sb(st, "fnw", [128, DM], F32)
            ofw = sb(st, "ofw", [128, DM], F32)
            obw = sb(st, "obw", [128, DM], F32)
            osum = sb(st, "osum", [128, DM], F32)
            sq = sb(st, "sq", [128, DM], F32)
            ssh = sb(st, "ssh", [128, NH], F32)
            ybp = sb(st, "ybp", [128, DM], BF16)
            xr = sb(st, "xr", [128, DM], F32)
            yo = ofw
            first_reads = []
            if debug:
                print("main-phase SBUF bytes remaining/partition:", nc.sbuf_bytes_remaining)
            P.op("sp", lambda e: e.dma_start(out=cwa[:, :, :].rearrange("p a b -> p (a b)"), in_=cwa_fm[:, :]), reads=first_reads, writes=["cwa"], dma="cwa")
            P.op("sp", lambda e: e.dma_start(out=gnw[:, :], in_=gnw_bc[:, :]), reads=first_reads, writes=["gnw"], dma="gnw")
            P.op("sp", lambda e: e.dma_start(out=fnw[:, :], in_=fnw_bc[:, :]), reads=first_reads, writes=["fnw"], dma="fnw")
            hs = lambda d, o0, o1: hT[:, d, 2 + o0:2 + o1]

            def proj(bank, slot, rhs_fn, n):
                for d in range(KD):
                    P.op("pe", lambda e, d=d: e.matmul(out=pb[bank][:, 0:n], lhsT=wbf[slot][:, d, :], rhs=rhs_fn(d),
                                                       start=(d == 0), stop=(d == KD - 1)), reads=["wbf%d" % slot, "hT"], writes=["pb" + bank])

            for fc in range(KD):
                for slot, coff in enumerate((cfg.C_BG, cfg.C_AX, cfg.C_CG, cfg.C_AZ)):
                    load_w_chunk(slot, w_in[:, coff + fc * 128:coff + (fc + 1) * 128], cast_eng="act" if slot % 2 else "dve")
                for tap in range(3):
                    P.op("dve", lambda e, tap=tap: e.tensor_scalar(out=dgA[:, tap, :], in0=ident_bf, scalar1=cwa[:, fc, tap:tap + 1], scalar2=None,
                                                                   op0=ALU.mult), reads=["cb", "cwa"], writes=["dgA"])
                for (c0, c1) in blocks(SEG + 2):
                    n = c1 - c0
                    proj("A", 0, lambda d, c0=c0, c1=c1: hT[:, d, 1 + c0:1 + c1], n)
                    proj("B", 1, lambda d, c0=c0, c1=c1: hT[:, d, 1 + c0:1 + c1], n)
                    P.op("act", lambda e, n=n: e.copy(out=tf1[:, 0:n], in_=pb["A"][:, 0:n]), reads=["pbA"], writes=["tf1"])
                    P.op("dve", lambda e, c0=c0, c1=c1, n=n: e.tensor_tensor(out=pA[:, c0:c1], in0=tf1[:, 0:n], in1=pb["B"][:, 0:n], op=ALU.mult),
                         reads=["tf1", "pbB"], writes=["pA"])
                P.op("dve", lambda e: e.tensor_scalar(out=pA[:, 0:1], in0=pA[:, 0:1], scalar1=flg[:, F_LO(4):F_LO(4) + 1], scalar2=None, op0=ALU.mult),
                     reads=["pA", "flg"], writes=["pA"])
                P.op("dve", lambda e: e.tensor_scalar(out=pA[:, SEG + 1:SEG + 2], in0=pA[:, SEG + 1:SEG + 2], scalar1=flg[:, F_HI(4):F_HI(4) + 1],
                                                      scalar2=None, op0=ALU.mult), reads=["pA", "flg"], writes=["pA"])
                for (o0, o1) in blocks(SEG):
                    n = o1 - o0
                    for tap in range(3):
                        P.op("pe", lambda e, tap=tap, o0=o0, o1=o1, n=n: e.matmul(out=pb["K"][:, 0:n], lhsT=dgA[:, tap, :], rhs=pA[:, o0 + tap:o1 + tap],
                                                                                 start=(tap == 0), stop=(tap == 2)), reads=["dgA", "pA"], writes=["pbK"])
                    proj("A", 2, lambda d, o0=o0, o1=o1: hs(d, o0, o1), n)
                    proj("B", 3, lambda d, o0=o0, o1=o1: hs(d, o0, o1), n)
                    P.op("act", lambda e, n=n: e.activation(out=tf2[:, 0:n], in_=pb["B"][:, 0:n], func=AF.Silu), reads=["pbB"], writes=["tf2"])
                    P.op("act", lambda e, n=n: e.copy(out=tf1[:, 0:n], in_=pb["A"][:, 0:n]), reads=["pbA"], writes=["tf1"])
                    P.op("dve", lambda e, n=n: e.tensor_tensor(out=tf3[:, 0:n], in0=tf1[:, 0:n], in1=pb["K"][:, 0:n], op=ALU.mult),
                         reads=["tf1", "pbK"], writes=["tf3"])
                    P.op("dve", lambda e, o0=o0, o1=o1, n=n: e.tensor_tensor(out=stash1[:, fc, o0:o1], in0=tf3[:, 0:n], in1=tf2[:, 0:n], op=ALU.mult),
                         reads=["tf3", "tf2"], writes=["stash1"])

            def branch_out(wmat, gate_off, accumulate):
                for oc in range(KD):
                    load_w_chunk(0, wmat[:, oc * 128:(oc + 1) * 128])
                    load_w_chunk(1, w_in[:, gate_off + oc * 128:gate_off + (oc + 1) * 128], cast_eng="act")
                    for (o0, o1) in blocks(SEG):
                        n = o1 - o0
                        for fc in range(KD):
                            P.op("pe", lambda e, fc=fc, o0=o0, o1=o1, n=n: e.matmul(out=pb["A"][:, 0:n], lhsT=wbf[0][:, fc, :], rhs=stash1[:, fc, o0:o1],
                                                                                   start=(fc == 0), stop=(fc == KD - 1)), reads=["wbf0", "stash1"], writes=["pbA"])
                        proj("B", 1, lambda d, o0=o0, o1=o1: hs(d, o0, o1), n)
                        P.op("act", lambda e, n=n: e.activation(out=tf2[:, 0:n], in_=pb["B"][:, 0:n], func=AF.Sigmoid), reads=["pbB"], writes=["tf2"])
                        if not accumulate:
                            P.op("dve", lambda e, oc=oc, o0=o0, o1=o1, n=n: e.tensor_tensor(out=stash2[:, oc, o0:o1], in0=tf2[:, 0:n], in1=pb["A"][:, 0:n], op=ALU.mult),
                                 reads=["tf2", "pbA"], writes=["stash2"])
                        else:
                            P.op("dve", lambda e, n=n: e.tensor_tensor(out=tf3[:, 0:n], in0=tf2[:, 0:n], in1=pb["A"][:, 0:n], op=ALU.mult),
                                 reads=["tf2", "pbA"], writes=["tf3"])
                            P.op("dve", lambda e, oc=oc, o0=o0, o1=o1, n=n: e.tensor_tensor(out=stash2[:, oc, o0:o1], in0=stash2[:, oc, o0:o1], in1=tf3[:, 0:n], op=ALU.add),
                                 reads=["tf3", "stash2"], writes=["stash2"])

            branch_out(w_pa, cfg.C_GA, False)

            def load_wres(src_ap):
                for d in range(KD):
                    P.op("sp", lambda e, d=d: e.dma_start(out=wres_f[:, :], in_=src_ap[d * 128:(d + 1) * 128, :]), writes=["wres_f"], dma="wres_f")
                    P.op("act" if d % 2 else "dve", lambda e, d=d: (e.copy(out=wres[:, d, :], in_=wres_f[:, :]) if d % 2 else e.tensor_copy(out=wres[:, d, :], in_=wres_f[:, :])), reads=["wres_f"], writes=["wres"])

            load_wres(w_in[:, cfg.C_Z:cfg.C_Z + DM])
            for t in range(NT):
                P.op("sp", lambda e, t=t: e.dma_start(out=ofw[:, :], in_=oscr[1, t * 128:(t + 1) * 128, :]), reads=["oscr1_%d_%d" % (t, hh) for hh in range(NH)], writes=["ofw"], dma="ofw")
                P.op("sp", lambda e, t=t: e.dma_start(out=obw[:, :], in_=oscr[0, (NT - 1 - t) * 128:(NT - t) * 128, :]), reads=["oscr0_%d_%d" % (NT - 1 - t, hh) for hh in range(NH)], writes=["obw"], dma="obw")
                for i, (c0, c1) in enumerate(blocks(DM)):
                    bk = "A" if i % 2 == 0 else "B"
                    P.op("pe", lambda e, bk=bk, c0=c0, c1=c1: e.matmul(out=pb[bk][:, 0:c1 - c0], lhsT=cf[:, C_J, :], rhs=obw[:, c0:c1], start=True, stop=True),
                         reads=["cf", "obw"], writes=["pb" + bk])
                    P.op("dve", lambda e, bk=bk, c0=c0, c1=c1: e.tensor_tensor(out=osum[:, c0:c1], in0=pb[bk][:, 0:c1 - c0], in1=ofw[:, c0:c1], op=ALU.add),
                         reads=["pb" + bk, "ofw"], writes=["osum"])
                P.op("act", lambda e: e.activation(out=sq[:, :], in_=osum[:, :], func=AF.Square), reads=["osum"], writes=["sq"])
                P.op("dve", lambda e: e.tensor_reduce(out=ssh[:, :], in_=sq[:, :].rearrange("p (h f) -> p h f", f=128), axis=AX.X, op=ALU.add),
                     reads=["sq"], writes=["ssh"])
                P.op("dve", lambda e: e.tensor_scalar(out=ssh[:, :], in0=ssh[:, :], scalar1=1.0 / 128, scalar2=1e-6, op0=ALU.mult, op1=ALU.add),
                     reads=["ssh"], writes=["ssh"])
                P.op("act", lambda e: e.sqrt(out=ssh[:, :], in_=ssh[:, :]), reads=["ssh"], writes=["ssh"])
                P.op("dve", lambda e: e.reciprocal(out=ssh[:, :], in_=ssh[:, :]), reads=["ssh"], writes=["ssh"])
                o3 = osum[:, :].rearrange("p (h f) -> p h f", f=128)
                P.op("dve", lambda e: e.tensor_tensor(out=o3, in0=o3, in1=ssh[:, :].unsqueeze(2).to_broadcast([128, NH, 128]), op=ALU.mult),
                     reads=["osum", "ssh"], writes=["osum"])
                P.op("dve", lambda e: e.tensor_tensor(out=o3, in0=o3, in1=gnw[:, :].unsqueeze(1).to_broadcast([128, NH, 128]), op=ALU.mult),
                     reads=["osum", "gnw"], writes=["osum"])
                for i, (c0, c1) in enumerate(blocks(DM)):
                    bk = "K" if i % 2 == 0 else "Y"
                    for d in range(KD):
                        P.op("pe", lambda e, bk=bk, d=d, c0=c0, c1=c1: e.matmul(out=pb[bk][:, 0:c1 - c0], lhsT=hs(d, t * 128, (t + 1) * 128), rhs=wres[:, d, c0:c1],
                                                                               start=(d == 0), stop=(d == KD - 1)), reads=["hT", "wres"], writes=["pb" + bk])
                    P.op("act", lambda e, bk=bk, c0=c0, c1=c1: e.activation(out=tf2[:, 0:c1 - c0], in_=pb[bk][:, 0:c1 - c0], func=AF.Silu), reads=["pb" + bk], writes=["tf2"])
                    P.op("dve", lambda e, c0=c0, c1=c1: e.tensor_tensor(out=ybp[:, c0:c1], in0=osum[:, c0:c1], in1=tf2[:, 0:c1 - c0], op=ALU.mult),
                         reads=["osum", "tf2"], writes=["ybp"])
                for fc in range(KD):
                    P.op("pe", lambda e, fc=fc: e.transpose(out=pbT[:, fc, :], in_=ybp[:, fc * 128:(fc + 1) * 128], identity=ident_bf),
                         reads=["ybp", "cb"], writes=["pbT"])
                P.op("act", lambda e, t=t: e.copy(out=stash1[:, :, t * 128:(t + 1) * 128], in_=pbT[:, 0:KD, :]), reads=["pbT"], writes=["stash1"])

            branch_out(w_pb, cfg.C_GB, True)

            load_wres(w_o[:, :])
            fin_ops = []
            for t in range(NT):
                P.op("sp", lambda e, t=t: e.dma_start(out=xr[:, :], in_=xs[4, 2 + t * 128:2 + (t + 1) * 128, :]), writes=["xr"], dma="xr")
                for i, (c0, c1) in enumerate(blocks(DM)):
                    bk = "A" if i % 2 == 0 else "B"
                    for fc in range(KD):
                        P.op("pe", lambda e, bk=bk, fc=fc, c0=c0, c1=c1: e.matmul(out=pb[bk][:, 0:c1 - c0], lhsT=stash2[:, fc, t * 128:(t + 1) * 128], rhs=wres[:, fc, c0:c1],
                                                                                 start=(fc == 0), stop=(fc == KD - 1)), reads=["stash2", "wres"], writes=["pb" + bk])
                    P.op("dve", lambda e, bk=bk, c0=c0, c1=c1: e.tensor_tensor(out=osum[:, c0:c1], in0=pb[bk][:, 0:c1 - c0], in1=gate_bc[:, c0:c1], op=ALU.mult),
                         reads=["pb" + bk, "gate_bc"], writes=["osum"])
                P.op("dve", lambda e: e.tensor_tensor(out=osum[:, :], in0=osum[:, :], in1=xr[:, :], op=ALU.add), reads=["osum", "xr"], writes=["osum"])
                P.op("act", lambda e: e.activation(out=sq[:, :], in_=osum[:, :], func=AF.Square, accum_out=ss[:, :]), reads=["osum"], writes=["sq", "ss"])
                P.op("dve", lambda e: e.tensor_scalar(out=rstd[:, :], in0=ss[:, :], scalar1=1.0 / DM, scalar2=1e-6, op0=ALU.mult, op1=ALU.add),
                     reads=["ss"], writes=["rstd"])
                P.op("act", lambda e: e.sqrt(out=rstd[:, :], in_=rstd[:, :]), reads=["rstd"], writes=["rstd"])
                P.op("dve", lambda e: e.reciprocal(out=rstd[:, :], in_=rstd[:, :]), reads=["rstd"], writes=["rstd"])
                P.op("dve", lambda e: e.scalar_tensor_tensor(out=yo[:, :], in0=osum[:, :], scalar=rstd[:, 0:1], in1=fnw[:, :], op0=ALU.mult, op1=ALU.mult),
                     reads=["osum", "rstd", "fnw"], writes=["ofw"])
                fin_ops.append(P.op("sp", lambda e, t=t: e.dma_start(out=y[t * 128:(t + 1) * 128, :], in_=yo[:, :]), reads=["ofw"], writes=["y%d" % t], dma="yo"))
            P.emit(final_wait_ops=fin_ops)
            if debug:
                print("scheduler estimate (us):", P.est_ns / 1e3)
    return nc


def make_consts():
    c = np.zeros((NCONST, 128, 128), np.float32)
    p = np.arange(128)[:, None]
    f = np.arange(128)[None, :]
    c[C_ID] = (p == f)
    c[C_ONE] = 1.0
    c[C_MLOW] = (p <= f)
    c[C_NEG] = np.where(p >= f, 0.0, NEG)
    c[C_STRICT] = (p > f)
    c[C_J] = (p + f == 127)
    for li, s in enumerate(LEVELS):
        m = ((p // (2 * s)) == (f // (2 * s))) & ((p % (2 * s)) >= s) & ((f % (2 * s)) < s)
        c[C_MM + li] = m
        c[C_MMT + li] = m.T
    return np.ascontiguousarray(c.transpose(1, 0, 2).reshape(128, NCONST * 128))


def fm(v, k):
    return np.ascontiguousarray(np.asarray(v, np.float32).reshape(k, 128).T)


def rep(v):
    v = np.asarray(v, np.float32).reshape(1, -1)
    return np.ascontiguousarray(np.broadcast_to(v, (128, v.shape[1])))


def host_prep(cfg, inputs, cores=None):
    DM, NH, KD, SEG, SEQ = cfg.DM, cfg.NH, cfg.KD, cfg.SEG, cfg.SEQ
    x = np.asarray(inputs["x"], np.float32)
    w_in = np.ascontiguousarray(np.asarray(inputs["w_in"], np.float32)[0])
    conv_qkv = np.asarray(inputs["conv_qkv_w"], np.float32)[0]
    a_log = np.asarray(inputs["a_log"], np.float32)[0]
    dt_bias = np.asarray(inputs["dt_bias"], np.float32)[0]
    consts = make_consts()
    common = {
        "normw_fm": fm(inputs["norm_w"][0], KD),
        "w_ada": np.ascontiguousarray(np.asarray(inputs["w_ada"], np.float32)[0]),
        "b_ada": np.ascontiguousarray(np.asarray(inputs["b_ada"], np.float32)[0].reshape(1, -1)),
        "w_in": w_in,
        "cwa_fm": np.ascontiguousarray(np.asarray(inputs["conv_a_w"], np.float32)[0].T.reshape(KD, 128, 3).transpose(1, 0, 2).reshape(128, KD * 3)),
        "constm": consts,
        "gnw_bc": rep(inputs["gdn_norm_w"][0]),
        "fnw_bc": rep(inputs["final_norm_w"]),
        "w_pa": np.ascontiguousarray(np.asarray(inputs["w_pa"], np.float32)[0]),
        "w_pb": np.ascontiguousarray(np.asarray(inputs["w_pb"], np.float32)[0]),
        "w_o": np.ascontiguousarray(np.asarray(inputs["w_o"], np.float32)[0]),
    }

    def seg_halo(b, i, rev):
        lo, hi = i * SEG - 2, (i + 1) * SEG + 2
        out = np.zeros((SEG + 4, DM), np.float32)
        a, bnd = max(lo, 0), min(hi, SEQ)
        out[a - lo:a - lo + (bnd - a)] = x[b, a:bnd]
        flo, fhi = float(i > 0), float(i < 3)
        if rev:
            out = out[::-1]
            flo, fhi = fhi, flo
        return out, flo, fhi

    in_maps = []
    cores = range(8) if cores is None else cores
    for core in cores:
        b, j = divmod(core, 4)
        plist = [(i, 0) for i in range(j)] + [(i, 1) for i in range(3, j, -1)] + [(j, 1), (j, 0)]
        xs = np.zeros((5, SEG + 4, DM), np.float32)
        flags = np.zeros((32,), np.float32)
        wab = np.zeros((5, DM, 2 * NH), np.float32)
        cw = np.zeros((5, 3 * NH, 128, 5), np.float32)
        alog = np.zeros((5, NH), np.float32)
        dtb = np.zeros((5, NH), np.float32)
        for p, (i, dr) in enumerate(plist):
            xs[p], flags[2 * p], flags[2 * p + 1] = seg_halo(b, i, dr == 1)
            wab[p, :, :NH] = w_in[:, cfg.C_A + dr * NH:cfg.C_A + (dr + 1) * NH]
            wab[p, :, NH:] = w_in[:, cfg.C_B + dr * NH:cfg.C_B + (dr + 1) * NH]
            taps = conv_qkv[::-1] if dr == 1 else conv_qkv
            cw[p] = taps.T.reshape(3 * NH, 128, 5)
            alog[p] = a_log[dr]
            dtb[p] = dt_bias[dr]
        for s in range(1, 3):
            flags[10 + s] = float(plist[s][1] == plist[s - 1][1])
        for s in range(3):
            nxt_dir = plist[s + 1][1] if s < 2 else None
            if plist[s][1] == 0 and (s == 2 or nxt_dir == 1):
                flags[13 + s] = 1.0
            if plist[s][1] == 1 and s == 2:
                flags[16 + s] = 1.0
        m = dict(common)
        m.update({
            "xs": xs,
            "c_fm": fm(inputs["c"][b], KD),
            "w_ab": wab,
            "cw_fm": np.ascontiguousarray(cw.transpose(2, 0, 1, 3).reshape(128, 5 * 3 * NH * 5)),
            "alog_bc": rep(alog.reshape(-1)),
            "dtb_bc": rep(dtb.reshape(-1)),
            "flags_bc": np.ascontiguousarray(np.broadcast_to(flags.reshape(1, 32), (128, 32))),
        })
        in_maps.append(m)
    return in_maps


_CACHE = {}


def kernel(**inputs):
    cfg = Cfg()
    if "nc" not in _CACHE:
        _CACHE["nc"] = build_program(cfg)
    nc = _CACHE["nc"]
    in_maps = host_prep(cfg, inputs)
    res = run_bass_kernel_spmd(nc, in_maps, core_ids=list(range(8)))
    out = np.zeros((2, cfg.SEQ, cfg.DM), np.float32)
    for core in range(8):
        b, j = divmod(core, 4)
        out[b, j * cfg.SEG:(j + 1) * cfg.SEG] = np.asarray(res.results[core]["y"], np.float32)
    return out
```

```python
import bisect
import contextlib

import numpy as np

import concourse.bass as bass
import concourse.mybir as mybir
from concourse.bass_utils import run_bass_kernel_spmd

F32 = mybir.dt.float32
BF16 = mybir.dt.bfloat16
ALU = mybir.AluOpType
AF = mybir.ActivationFunctionType
AX = mybir.AxisListType

ENGS = ("pe", "dve", "act", "pool", "sp")
SEM_EPOCH = 30000
NEG = -1.0e9
NCONST = 20
C_ID, C_ONE, C_MLOW, C_NEG, C_STRICT, C_J, C_MM, C_MMT = 0, 1, 2, 3, 4, 5, 6, 13
LEVELS = (1, 2, 4, 8, 16, 32, 64)


class Cfg:
    def __init__(self, DM=1024, SEG=2048):
        self.DM = DM
        self.NH = DM // 128
        self.KD = DM // 128
        self.SEG = SEG
        self.NT = SEG // 128
        self.NG = self.NT // 4
        self.SEQ = SEG * 4
        self.NIN = 8 * DM + 4 * self.NH + 2 * DM
        self.C_BG, self.C_CG, self.C_AX, self.C_AZ = 0, DM, 2 * DM, 3 * DM
        self.C_Q, self.C_K, self.C_V, self.C_Z = 4 * DM, 5 * DM, 6 * DM, 7 * DM
        self.C_A = 8 * DM
        self.C_B = 8 * DM + 2 * self.NH
        self.C_GA = 8 * DM + 4 * self.NH
        self.C_GB = self.C_GA + DM


class _Rec:
    def __getattr__(self, name):
        return lambda *a, **k: (name, a, k)


class Prog:
    def __init__(self, nc):
        self.nc = nc
        self.ops = []
        self.lastw = {}
        self.rd_eng = {}
        self.rd_dma = {}
        self.bar_idx = None

    def op(self, eng, fn, reads=(), writes=(), dma=None):
        idx = len(self.ops)
        pr = [k for k in reads if k.startswith("pb")]
        if pr:
            reads = [k for k in reads if not k.startswith("pb")]
            writes = list(writes) + [k for k in pr if k not in writes]
        deps = set()
        for k in reads:
            if k in self.lastw:
                deps.add(self.lastw[k])
        for k in writes:
            if k in self.lastw:
                deps.add(self.lastw[k])
            deps.update(self.rd_eng.get(k, ()))
            deps.update(self.rd_dma.get(k, ()))
        for k in writes:
            self.lastw[k] = idx
            self.rd_eng[k] = []
            self.rd_dma[k] = []
        for k in reads:
            if k in writes:
                continue
            if dma is not None:
                self.rd_dma.setdefault(k, []).append(idx)
            else:
                self.rd_eng.setdefault(k, []).append(idx)
        if self.bar_idx is not None:
            deps.add(self.bar_idx)
        deps.discard(idx)
        self.ops.append(dict(eng=eng, call=fn(_Rec()), deps=deps, dma=dma, needed=False))
        return idx

    def barrier(self, bar_t, cf, pbank, dram_src):
        allk = [k for k in self.lastw.keys()]
        self.bar_idx = None
        self.bar_idx = self.op("dve", lambda e: e.memset(bar_t[:, 0:1], 0.0), writes=allk + ["bar_dve"])
        self.op("act", lambda e: e.memzero(bar_t[:, 1:2]), reads=["bar_dve"], writes=["bar_act"])
        self.op("pool", lambda e: e.memset(bar_t[:, 2:3], 0.0), reads=["bar_dve"], writes=["bar_pool"])
        self.op("pe", lambda e: e.matmul(out=pbank[0:1, 0:1], lhsT=cf[0:1, 1, 0:1], rhs=cf[0:1, 1, 0:1], start=True, stop=True),
                reads=["bar_dve", "cf"], writes=["pbR2"])
        self.op("sp", lambda e: e.dma_start(out=bar_t[0:1, 4:8], in_=dram_src), reads=["bar_dve"], writes=["bar_sp"], dma="bar")

    def schedule(self, final_wait_ops):
        import heapq
        ops = self.ops
        n = len(ops)

        def cost(o):
            name, a, k = o["call"]
            out = k.get("out", a[0] if a else None)
            try:
                shp = out.shape
                free = 1
                for d in shp[1:]:
                    free *= d
                nbytes = free * shp[0] * (2 if out.dtype == BF16 else 4)
            except Exception:
                free, nbytes = 128, 65536
            if o["dma"] is not None:
                return 2000.0 + nbytes / 150.0
            e = o["eng"]
            aps = [v for v in list(a) + list(k.values()) if hasattr(v, "dtype") and hasattr(v, "shape")]
            in_psum = lambda v: str(v.name).startswith("pb")
            if e == "pe":
                f32 = any(v.dtype == F32 for v in aps if not in_psum(v))
                base = 91.0 if free <= 128 else 64.0 + 0.40 * free
                return base * (2.0 if f32 else 1.0)
            if e == "dve":
                slow = any((v.dtype == F32) or in_psum(v) for v in aps)
                return (100.0 + 1.0 * free) if slow else (80.0 + 0.5 * free)
            if e == "act":
                return 200.0 + 0.7 * free
            return 150.0 + 2.0 * free

        children = [[] for _ in range(n)]
        indeg = [0] * n
        for i, o in enumerate(ops):
            indeg[i] = len(o["deps"])
            for d in o["deps"]:
                children[d].append(i)
        LAT = 120.0
        costs = [cost(o) for o in ops]
        bl = [0.0] * n
        for i in range(n - 1, -1, -1):
            m = 0.0
            for ch in children[i]:
                if bl[ch] > m:
                    m = bl[ch]
            bl[i] = costs[i] + m
        prio = [(-bl[i], i) for i in range(n)]
        ready_t = [0.0] * n
        finish = [0.0] * n
        start = [0.0] * n
        free_at = {e: 0.0 for e in ENGS}
        pending = {e: [] for e in ENGS}
        avail = {e: [] for e in ENGS}
        for i, o in enumerate(ops):
            if indeg[i] == 0:
                heapq.heappush(pending[o["eng"]], (0.0, i))
        done = 0
        while done < n:
            best = None
            for e in ENGS:
                pe_, av_ = pending[e], avail[e]
                while pe_ and pe_[0][0] <= free_at[e]:
                    heapq.heappush(av_, prio[heapq.heappop(pe_)[1]])
                if av_:
                    cand = (free_at[e], av_[0][1], e, True)
                elif pe_:
                    cand = (pe_[0][0], pe_[0][1], e, False)
                else:
                    continue
                if best is None or cand[:2] < best[:2]:
                    best = cand
            assert best is not None, "scheduler deadlock"
            t0, i, e, from_av = best
            if from_av:
                heapq.heappop(avail[e])
            else:
                heapq.heappop(pending[e])
            o = ops[i]
            c = costs[i]
            start[i] = t0
            if o["dma"] is not None:
                free_at[e] = t0 + 70.0
            else:
                free_at[e] = t0 + c
            finish[i] = t0 + c
            done += 1
            for ch in children[i]:
                rt = finish[i] + (0.0 if (ops[ch]["eng"] == e and o["dma"] is None) else LAT)
                if rt > ready_t[ch]:
                    ready_t[ch] = rt
                indeg[ch] -= 1
                if indeg[ch] == 0:
                    heapq.heappush(pending[ops[ch]["eng"]], (ready_t[ch], ch))
        order = sorted(range(n), key=lambda i: (start[i], i))
        remap = {old: new for new, old in enumerate(order)}
        new_ops = []
        for old in order:
            o = ops[old]
            o["deps"] = set(remap[d] for d in o["deps"])
            new_ops.append(o)
        self.ops = new_ops
        self.est_ns = max(finish) if n else 0.0
        return [remap[d] for d in final_wait_ops]

    def emit(self, final_wait_ops=(), reorder=True):
        nc = self.nc
        if reorder:
            final_wait_ops = self.schedule(list(final_wait_ops))
        ops = self.ops
        for o in ops:
            best = {}
            keep = set()
            for d in o["deps"]:
                od = ops[d]
                if od["dma"] is not None:
                    keep.add(d)
                elif d > best.get(od["eng"], -1):
                    best[od["eng"]] = d
            keep.update(best.values())
            o["deps"] = keep

        def pe_pe(o, od):
            return o["eng"] == "pe" and od["eng"] == "pe" and od["dma"] is None and o["dma"] is None

        for o in ops:
            for d in o["deps"]:
                if not pe_pe(o, ops[d]):
                    ops[d]["needed"] = True
        for d in final_wait_ops:
            ops[d]["needed"] = True
        with contextlib.ExitStack() as st:
            eng_sems, eng_cnt = {}, {}
            dma_sems, dma_cnt, dma_order = {}, {}, {}
            for i, o in enumerate(ops):
                if o["dma"] is not None:
                    tag = o["dma"]
                    if tag not in dma_sems:
                        dma_sems[tag] = st.enter_context(nc.semaphore("d_" + tag))
                        dma_cnt[tag] = 0
                        dma_order[tag] = []
                    dma_cnt[tag] += 1
                    dma_order[tag].append(i)
                    o["ev"] = (dma_sems[tag], 16 * dma_cnt[tag], tag)
                elif o["needed"]:
                    e = o["eng"]
                    if e not in eng_sems or eng_cnt[e] >= SEM_EPOCH:
                        eng_sems[e] = st.enter_context(nc.semaphore("e_%s_%d" % (e, i)))
                        eng_cnt[e] = 0
                    eng_cnt[e] += 1
                    o["ev"] = (eng_sems[e], eng_cnt[e], None)
                else:
                    o["ev"] = None
            per_eng = {e: [] for e in ENGS}
            for i, o in enumerate(ops):
                per_eng[o["eng"]].append(i)

            def dep_event(i_waiter, d):
                sem, val, tag = ops[d]["ev"]
                return sem, val

            def run_engine(ename, engobj, extra_final=None):
                waited = {}
                for i in per_eng[ename]:
                    o = ops[i]
                    for d in sorted(o["deps"]):
                        if pe_pe(o, ops[d]):
                            continue
                        sem, val = dep_event(i, d)
                        if waited.get(id(sem), 0) >= val:
                            continue
                        waited[id(sem)] = val
                        engobj.wait_ge(sem, val)
                    name, a, k = o["call"]
                    inst = getattr(engobj, name)(*a, **k)
                    if o["ev"] is not None:
                        sem, val, tag = o["ev"]
                        inst.then_inc(sem, 16 if tag is not None else 1)
                if extra_final:
                    for d in extra_final:
                        sem, val = dep_event(len(ops), d)
                        engobj.wait_ge(sem, val)

            with nc.Block() as block:
                @block.tensor
                def _(e):
                    run_engine("pe", e)

                @block.vector
                def _(e):
                    run_engine("dve", e)

                @block.scalar
                def _(e):
                    run_engine("act", e)

                @block.gpsimd
                def _(e):
                    run_engine("pool", e)

                @block.sync
                def _(e):
                    run_engine("sp", e, extra_final=final_wait_ops)


def blocks(n, step=512):
    return [(a, min(a + step, n)) for a in range(0, n, step)]


def interleave(gens):
    gens = list(gens)
    while gens:
        nxt = []
        for g in gens:
            try:
                next(g)
                nxt.append(g)
            except StopIteration:
                pass
        gens = nxt


def build_program(cfg, debug=False):
    DM, NH, KD, SEG, NT, NG = cfg.DM, cfg.NH, cfg.KD, cfg.SEG, cfg.NT, cfg.NG
    SP4 = SEG + 4
    nc = bass.Bass("TRN2", target_bir_lowering=False)

    def din(name, shape, dt=F32):
        return nc.dram_tensor(name, list(shape), dt, kind="ExternalInput").ap()

    xs = din("xs", [5, SP4, DM])
    c_fm = din("c_fm", [128, KD])
    normw_fm = din("normw_fm", [128, KD])
    w_ada = din("w_ada", [DM, 3 * DM])
    b_ada = din("b_ada", [1, 3 * DM])
    w_in = din("w_in", [DM, cfg.NIN])
    w_ab = din("w_ab", [5, DM, 2 * NH])
    cw_fm = din("cw_fm", [128, 5 * 3 * NH * 5])
    cwa_fm = din("cwa_fm", [128, KD * 3])
    alog_bc = din("alog_bc", [128, 5 * NH])
    dtb_bc = din("dtb_bc", [128, 5 * NH])
    flags_bc = din("flags_bc", [128, 32])
    constm = din("constm", [128, NCONST * 128])
    gnw_bc = din("gnw_bc", [128, 128])
    fnw_bc = din("fnw_bc", [128, DM])
    w_pa = din("w_pa", [DM, DM])
    w_pb = din("w_pb", [DM, DM])
    w_o = din("w_o", [DM, DM])
    y = nc.dram_tensor("y", [SEG, DM], F32, kind="ExternalOutput").ap()
    oscr = nc.dram_tensor("oscr", [2, SEG, DM], F32).ap()
    dbg = None
    if debug:
        dbg = nc.dram_tensor("dbg", [128, 4 * 128], F32, kind="ExternalOutput").ap()

    P = Prog(nc)
    F_LO = lambda p: 2 * p
    F_HI = lambda p: 2 * p + 1
    F_CARRY, F_F, F_B = 10, 13, 16

    with contextlib.ExitStack() as st0:
        def sb(st, name, shape, dt):
            return st.enter_context(nc.sbuf_tensor(name, list(shape), dt))

        pb = {}
        for nm in ("A", "B", "G", "K", "Y", "R1", "R2"):
            pb[nm] = st0.enter_context(nc.psum_tensor("pb" + nm, [128, 512], F32))
        pbT = st0.enter_context(nc.psum_tensor("pbT", [128, 8, 128], BF16))

        cf = sb(st0, "cf", [128, 6, 128], F32)
        cb = sb(st0, "cb", [128, NCONST, 128], BF16)
        flg = sb(st0, "flg", [128, 32], F32)
        epsc = sb(st0, "epsc", [128, 1], F32)
        bar_t = sb(st0, "bar_t", [128, 8], F32)
        hT = sb(st0, "hT", [128, KD, SP4], BF16)
        shsc = sb(st0, "shsc", [128, 2 * KD], F32)
        scfm = sb(st0, "scfm", [128, KD], F32)
        gate_bc = sb(st0, "gate_bc", [128, DM], F32)
        xt = sb(st0, "xt", [128, DM], F32)
        xn = sb(st0, "xn", [128, DM], BF16)
        junk = xn
        ss = sb(st0, "ss", [128, 1], F32)
        rstd = sb(st0, "rstd", [128, 1], F32)
        xt_b = sb(st0, "xt_b", [128, DM], F32)
        xn_b = sb(st0, "xn_b", [128, DM], BF16)
        ss_b = sb(st0, "ss_b", [128, 1], F32)
        rstd_b = sb(st0, "rstd_b", [128, 1], F32)
        wst = [sb(st0, "wst%d" % i, [128, KD, 128], F32) for i in range(2)]
        wbf = [sb(st0, "wbf%d" % i, [128, KD, 128], BF16) for i in range(4)]

        ident_bf = cb[:, C_ID, :]
        ones_bf = cb[:, C_ONE, :]
        ones_f = cf[:, C_ONE, :]

        def bc4(ap2d):
            return ap2d.unsqueeze(1).to_broadcast([128, 4, 128])

        P.op("sp", lambda e: e.dma_start(out=cf[:, :, :].rearrange("p a b -> p (a b)"), in_=constm[:, 0:6 * 128]),
             writes=["cf"], dma="cf")
        P.op("sp", lambda e: e.dma_start(out=flg[:, :], in_=flags_bc[:, :]), writes=["flg"], dma="flg")
        P.op("dve", lambda e: e.memset(epsc[:, :], 1e-6), writes=["epsc"])

        with contextlib.ExitStack() as st:
            cfull = sb(st, "cfull", [128, NCONST, 128], F32)
            P.op("sp", lambda e: e.dma_start(out=cfull[:, :, :].rearrange("p a b -> p (a b)"), in_=constm[:, :]),
                 writes=["cfull"], dma="cfull")
            P.op("dve", lambda e: e.tensor_copy(out=cb[:, :, :], in_=cfull[:, :, :]), reads=["cfull"], writes=["cb"])
            cT = sb(st, "cT", [128, KD], F32)
            scs = sb(st, "scs", [128, KD], F32)
            nwf = sb(st, "nwf", [128, KD], F32)
            wa = sb(st, "wa", [128, 3 * DM], F32)
            bada = sb(st, "bada", [1, 3 * DM], F32)
            modrow = sb(st, "modrow", [1, 3 * DM], F32)
            P.op("sp", lambda e: e.dma_start(out=cT[:, :], in_=c_fm[:, :]), writes=["cT"], dma="cT")
            P.op("sp", lambda e: e.dma_start(out=nwf[:, :], in_=normw_fm[:, :]), writes=["nwf"], dma="nwf")
            P.op("sp", lambda e: e.dma_start(out=bada[:, :], in_=b_ada[:, :]), writes=["bada"], dma="bada")
            P.op("act", lambda e: e.activation(out=scs[:, :], in_=cT[:, :], func=AF.Silu), reads=["cT"], writes=["scs"])
            cblocks = blocks(3 * DM)
            banks = ["A", "B", "G", "K", "Y", "R1"]
            assert len(cblocks) <= len(banks)
            for d in range(KD):
                P.op("sp", lambda e, d=d: e.dma_start(out=wa[:, :], in_=w_ada[d * 128:(d + 1) * 128, :]),
                     writes=["wa"], dma="wa")
                for i, (c0, c1) in enumerate(cblocks):
                    P.op("pe", lambda e, d=d, i=i, c0=c0, c1=c1: e.matmul(
                        out=pb[banks[i]][0:1, 0:c1 - c0], lhsT=scs[:, d:d + 1], rhs=wa[:, c0:c1],
                        start=(d == 0), stop=(d == KD - 1)), reads=["scs", "wa"], writes=["pb" + banks[i]])
            for i, (c0, c1) in enumerate(cblocks):
                P.op("dve", lambda e, i=i, c0=c0, c1=c1: e.tensor_tensor(
                    out=modrow[0:1, c0:c1], in0=pb[banks[i]][0:1, 0:c1 - c0], in1=bada[0:1, c0:c1], op=ALU.add),
                    reads=["pb" + banks[i], "bada"], writes=["modrow"])
            for i in range(2 * KD):
                P.op("pe", lambda e, i=i: e.matmul(out=pb["A"][:, i:i + 1], lhsT=modrow[0:1, i * 128:(i + 1) * 128],
                                                   rhs=cf[0:1, C_ONE, 0:1], start=True, stop=True),
                     reads=["modrow", "cf"], writes=["pbA"])
            P.op("dve", lambda e: e.tensor_copy(out=shsc[:, :], in_=pb["A"][:, 0:2 * KD]), reads=["pbA"], writes=["shsc"])
            P.op("dve", lambda e: e.scalar_tensor_tensor(out=scfm[:, :], in0=shsc[:, KD:2 * KD], scalar=1.0, in1=nwf[:, :],
                                                         op0=ALU.add, op1=ALU.mult),
                 reads=["shsc", "nwf"], writes=["scfm"])
            for i, (c0, c1) in enumerate(blocks(DM)):
                bk = banks[1 + (i % 2)]
                P.op("pe", lambda e, bk=bk, c0=c0, c1=c1: e.matmul(
                    out=pb[bk][:, 0:c1 - c0], lhsT=cf[0:1, C_ONE, :], rhs=modrow[0:1, 2 * DM + c0:2 * DM + c1],
                    start=True, stop=True), reads=["modrow", "cf"], writes=["pb" + bk])
                P.op("act", lambda e, bk=bk, c0=c0, c1=c1: e.copy(out=gate_bc[:, c0:c1], in_=pb[bk][:, 0:c1 - c0]),
                     reads=["pb" + bk], writes=["gate_bc"])
            P.barrier(bar_t, cf, pb["R2"], flags_bc[0:1, 0:4])

        def load_w_chunk(slot, src_ap, q="sp", cast_eng="dve"):
            ws_ = slot % 2
            P.op(q, lambda e: e.dma_start(out=wst[ws_][:, :, :], in_=src_ap.rearrange("(k p) c -> p k c", p=128)),
                 writes=["wst%d" % ws_], dma="wst%d" % ws_)
            P.op(cast_eng, lambda e: (e.copy(out=wbf[slot][:, :, :], in_=wst[ws_][:, :, :]) if cast_eng == "act"
                                      else e.tensor_copy(out=wbf[slot][:, :, :], in_=wst[ws_][:, :, :])),
                 reads=["wst%d" % ws_], writes=["wbf%d" % slot])

        def stage_a(p):
            for t in range(NT + 1):
                rows = 128 if t < NT else 4
                q = t % 2
                xt_, xn_, ss_, rstd_ = (xt, xn, ss, rstd) if q == 0 else (xt_b, xn_b, ss_b, rstd_b)
                kxt_, kxn_, kss_, krs_ = ("xt", "xn", "ss", "rstd") if q == 0 else ("xt_b", "xn_b", "ss_b", "rstd_b")
                P.op("sp", lambda e, t=t, rows=rows: e.dma_start(out=xt_[0:rows, :], in_=xs[p, t * 128:t * 128 + rows, :]),
                     writes=[kxt_], dma=kxt_)
                P.op("act", lambda e, rows=rows: e.activation(out=xn_[0:rows, :], in_=xt_[0:rows, :], func=AF.Square,
                                                              accum_out=ss_[0:rows, :]), reads=[kxt_], writes=[kxn_, kss_])
                P.op("dve", lambda e, rows=rows: e.tensor_scalar(out=rstd_[0:rows, :], in0=ss_[0:rows, :], scalar1=1.0 / DM,
                                                                 scalar2=1e-6, op0=ALU.mult, op1=ALU.add),
                     reads=[kss_], writes=[krs_])
                P.op("act", lambda e, rows=rows: e.sqrt(out=rstd_[0:rows, :], in_=rstd_[0:rows, :]), reads=[krs_], writes=[krs_])
                P.op("dve", lambda e, rows=rows: e.reciprocal(out=rstd_[0:rows, :], in_=rstd_[0:rows, :]), reads=[krs_], writes=[krs_])
                P.op("dve", lambda e, rows=rows: e.tensor_scalar(out=xn_[0:rows, :], in0=xt_[0:rows, :], scalar1=rstd_[0:rows, 0:1],
                                                                 scalar2=None, op0=ALU.mult), reads=[kxt_, krs_], writes=[kxn_])
                for d in range(KD):
                    P.op("pe", lambda e, d=d, rows=rows: e.transpose(out=pbT[:, d, 0:rows], in_=xn_[0:rows, d * 128:(d + 1) * 128],
                                                                     identity=ident_bf[0:rows, 0:rows]),
                         reads=[kxn_, "cb"], writes=["pbT"])
                for d in range(KD):
                    P.op("act", lambda e, d=d, t=t, rows=rows: e.activation(
                        out=hT[:, d, t * 128:t * 128 + rows], in_=pbT[:, d, 0:rows], func=AF.Identity,
                        scale=scfm[:, d:d + 1], bias=shsc[:, d:d + 1]), reads=["pbT", "scfm", "shsc"], writes=["hT"])
                yield

        with contextlib.ExitStack() as st:
            cwt = sb(st, "cwt", [128, 5, 3 * NH, 5], F32)
            alg = sb(st, "alg", [128, 5, NH], F32)
            dtb = sb(st, "dtb", [128, 5, NH], F32)
            wabf = sb(st, "wabf", [128, KD, 2 * NH], F32)
            wabb = sb(st, "wabb", [128, KD, 2 * NH], BF16)
            xa = sb(st, "xa", [128, NT, NH], F32)
            aexp = sb(st, "aexp", [128, NH], F32)
            gall2 = [sb(st, "gall%d" % i, [128, NT, NH], F32) for i in range(2)]
            ball2 = [sb(st, "ball%d" % i, [128, NT, NH], F32) for i in range(2)]
            Sst = sb(st, "Sst", [128, NH, 128], F32)
            Sf = sb(st, "Sf", [128, NH, 128], F32)
            Sb = sb(st, "Sb", [128, NH, 128], F32)
            Sbf = [sb(st, "Sbf%d" % i, [128, 128], BF16) for i in range(2)]
            pre_a = sb(st, "pre", [128, SP4], BF16)
            pre_b = sb(st, "pre_b", [128, SP4], BF16)
            cTk2 = [sb(st, "cTk%d" % i, [128, SEG], BF16) for i in range(2)]
            cTv2 = [sb(st, "cTv%d" % i, [128, SEG], BF16) for i in range(2)]
            cTq2 = [sb(st, "cTq%d" % i, [128, SEG], BF16) for i in range(2)]
            dg = sb(st, "dg", [128, 3, 5, 128], BF16)
            sqt = sb(st, "sqt", [128, 512], BF16)
            rkT = sb(st, "rkT", [128, 512], F32)
            g4 = lambda n, dt: sb(st, n, [128, 4, 128], dt)
            kn_tok, v_tok = g4("kn_tok", BF16), g4("v_tok", BF16)
            rhs_g, GcM = g4("rhs_g", F32), g4("GcM", F32)
            dm = rhs_g
            Dm, eGR, bM, Dp, Am = g4("Dm", BF16), g4("eGR", BF16), g4("bM", BF16), g4("Dp", BF16), g4("Am", BF16)
            Rv, Rk = g4("Rv", BF16), g4("Rk", BF16)
            Lo = [g4("Lo%d" % G, BF16) for G in range(NG)]
            Lm = [g4("Lm%d" % G, BF16) for G in range(NG)]
            Ynp = [g4("Ynp%d" % G, BF16) for G in range(NG)]
            Xb = [[g4("X%d_0" % G, BF16)] * 2 for G in range(NG)]
            XTb = [[g4("XT%d_0" % G, BF16)] * 2 for G in range(NG)]
            Gc = [sb(st, "Gc%d" % G, [128, 4], F32) for G in range(NG)]
            eGc = [sb(st, "eGc%d" % G, [128, 4], F32) for G in range(NG)]
            kdf = [sb(st, "kdf%d" % G, [128, 4], F32) for G in range(NG)]
            bE = [sb(st, "bE%d" % G, [128, 4], F32) for G in range(NG)]
            hp = lambda n, dt: [sb(st, "%s%d" % (n, i), [128, NT, 128], dt) for i in range(2)]
            wT_all, u_all, kd_all, qdT_all, AT_all = hp("wTa", BF16), hp("ua", BF16), hp("kda", BF16), hp("qdTa", BF16), hp("ATa", BF16)
            gl_all = [sb(st, "gla%d" % i, [128, NT], F32) for i in range(2)]
            vnew = sb(st, "vnew", [128, 128], BF16)
            osb = [sb(st, "osb%d" % i, [128, 128], F32) for i in range(2)]

            if debug:
                print("GDN-phase SBUF bytes remaining/partition:", nc.sbuf_bytes_remaining)
            P.op("sp", lambda e: e.dma_start(out=cwt[:, :, :, :].rearrange("p a b c -> p (a b c)"), in_=cw_fm[:, :]),
                 writes=["cwt"], dma="cwt")
            P.op("sp", lambda e: e.dma_start(out=alg[:, :, :].rearrange("p a b -> p (a b)"), in_=alog_bc[:, :]),
                 writes=["alg"], dma="alg")
            P.op("sp", lambda e: e.dma_start(out=dtb[:, :, :].rearrange("p a b -> p (a b)"), in_=dtb_bc[:, :]),
                 writes=["dtb"], dma="dtb")
            P.op("dve", lambda e: e.memset(Sst[:, :, :], 0.0), writes=["S%d" % h for h in range(NH)])
            P.op("dve", lambda e: e.memset(Sf[:, :, :], 0.0), writes=["Sf%d" % h for h in range(NH)])
            P.op("dve", lambda e: e.memset(Sb[:, :, :], 0.0), writes=["Sb%d" % h for h in range(NH)])

            def pass_setup(p):
                gall, ball, kga, kba = gall2[p % 2], ball2[p % 2], "gall%d" % (p % 2), "ball%d" % (p % 2)
                yield from stage_a(p)
                P.op("sp", lambda e: e.dma_start(out=wabf[:, :, :], in_=w_ab[p].rearrange("(k p) c -> p k c", p=128)),
                     writes=["wabf"], dma="wabf")
                P.op("dve", lambda e: e.tensor_copy(out=wabb[:, :, :], in_=wabf[:, :, :]), reads=["wabf"], writes=["wabb"])
                for t in range(NT):
                    for d in range(KD):
                        P.op("pe", lambda e, t=t, d=d: e.matmul(
                            out=pb["K"][:, t * 2 * NH:(t + 1) * 2 * NH], lhsT=hT[:, d, 2 + 128 * t:2 + 128 * t + 128],
                            rhs=wabb[:, d, :], start=(d == 0), stop=(d == KD - 1)), reads=["hT", "wabb"], writes=["pbK"])
                abv = pb["K"][:, 0:NT * 2 * NH].rearrange("p (t c) -> p t c", c=2 * NH)
                P.op("dve", lambda e: e.tensor_tensor(out=xa[:, :, :], in0=abv[:, :, 0:NH],
                                                      in1=dtb[:, p:p + 1, :].to_broadcast([128, NT, NH]), op=ALU.add),
                     reads=["pbK", "dtb"], writes=["xa"])
                P.op("act", lambda e: e.activation(out=ball[:, :, :], in_=abv[:, :, NH:2 * NH], func=AF.Sigmoid),
                     reads=["pbK"], writes=[kba])
                P.op("dve", lambda e: e.tensor_scalar_min(out=xa[:, :, :], in0=xa[:, :, :], scalar1=30.0), reads=["xa"], writes=["xa"])
                P.op("act", lambda e: e.activation(out=xa[:, :, :], in_=xa[:, :, :], func=AF.Exp), reads=["xa"], writes=["xa"])
                P.op("act", lambda e: e.activation(out=xa[:, :, :], in_=xa[:, :, :], func=AF.Ln, bias=cf[:, C_ONE, 0:1], scale=1.0),
                     reads=["xa", "cf"], writes=["xa"])
                P.op("act", lambda e: e.activation(out=aexp[:, :], in_=alg[:, p, :], func=AF.Exp), reads=["alg"], writes=["aexp"])
                P.op("dve", lambda e: e.scalar_tensor_tensor(out=gall[:, :, :], in0=xa[:, :, :], scalar=-1.0,
                                                             in1=aexp[:, :].unsqueeze(1).to_broadcast([128, NT, NH]),
                                                             op0=ALU.mult, op1=ALU.mult), reads=["xa", "aexp"], writes=[kga])
                yield

            def prepA(p, h, par, out):
                cTk, cTv, cTq = cTk2[par], cTv2[par], cTq2[par]
                names = [("k", cfg.C_K, cTk), ("v", cfg.C_V, cTv)] + ([("q", cfg.C_Q, cTq)] if out else [])
                for wi, (nm, coff, cT_) in enumerate(names):
                    load_w_chunk(wi, w_in[:, coff + h * 128:coff + (h + 1) * 128])
                    chunk = {"k": NH + h, "v": 2 * NH + h, "q": h}[nm]
                    for tap in range(5):
                        P.op("dve", lambda e, wi=wi, tap=tap, chunk=chunk: e.tensor_scalar(
                            out=dg[:, wi, tap, :], in0=ident_bf, scalar1=cwt[:, p, chunk, tap:tap + 1], scalar2=None,
                            op0=ALU.mult), reads=["cb", "cwt"], writes=["dg%d" % wi])
                yield
                for wi, (nm, coff, cT_) in enumerate(names):
                    pre, kpre = (pre_a, "pre") if wi % 2 == 0 else (pre_b, "pre_b")
                    for bi, (c0, c1) in enumerate(blocks(SP4)):
                        for d in range(KD):
                            P.op("pe", lambda e, wi=wi, d=d, c0=c0, c1=c1: e.matmul(
                                out=pb["A"][:, 0:c1 - c0], lhsT=wbf[wi][:, d, :], rhs=hT[:, d, c0:c1],
                                start=(d == 0), stop=(d == KD - 1)), reads=["wbf%d" % wi, "hT"], writes=["pbA"])
                        if bi % 2 == 0:
                            P.op("act", lambda e, c0=c0, c1=c1: e.copy(out=pre[:, c0:c1], in_=pb["A"][:, 0:c1 - c0]),
                                 reads=["pbA"], writes=[kpre])
                        else:
                            P.op("dve", lambda e, c0=c0, c1=c1: e.tensor_copy(out=pre[:, c0:c1], in_=pb["A"][:, 0:c1 - c0]),
                                 reads=["pbA"], writes=[kpre])
                        yield
                    P.op("dve", lambda e: e.tensor_scalar(out=pre[:, 0:2], in0=pre[:, 0:2], scalar1=flg[:, F_LO(p):F_LO(p) + 1],
                                                          scalar2=None, op0=ALU.mult), reads=["pre", "flg"], writes=[kpre])
                    P.op("dve", lambda e: e.tensor_scalar(out=pre[:, SEG + 2:SP4], in0=pre[:, SEG + 2:SP4],
                                                          scalar1=flg[:, F_HI(p):F_HI(p) + 1], scalar2=None, op0=ALU.mult),
                         reads=["pre", "flg"], writes=[kpre])
                    for (o0, o1) in blocks(SEG):
                        for tap in range(5):
                            P.op("pe", lambda e, wi=wi, tap=tap, o0=o0, o1=o1: e.matmul(
                                out=pb["R2"][:, 0:o1 - o0], lhsT=dg[:, wi, tap, :], rhs=pre[:, o0 + tap:o1 + tap],
                                start=(tap == 0), stop=(tap == 4)), reads=["dg%d" % wi, kpre], writes=["pbR2"])
                        P.op("act", lambda e, cT_=cT_, o0=o0, o1=o1: e.activation(out=cT_[:, o0:o1], in_=pb["R2"][:, 0:o1 - o0],
                                                                                   func=AF.Silu), reads=["pbR2"], writes=["cT%s%d" % (nm, par)])
                        yield
                for nm, cT_ in ([("k", cTk)] + ([("q", cTq)] if out else [])):
                    for (o0, o1) in blocks(SEG):
                        n = o1 - o0
                        P.op("dve", lambda e, cT_=cT_, o0=o0, o1=o1, n=n: e.tensor_tensor(out=sqt[:, 0:n], in0=cT_[:, o0:o1], in1=cT_[:, o0:o1], op=ALU.mult),
                             reads=["cT%s%d" % (nm, par)], writes=["sqt"])
                        P.op("pe", lambda e, n=n: e.matmul(out=pb["R2"][:, 0:n], lhsT=ones_bf, rhs=sqt[:, 0:n], start=True, stop=True),
                             reads=["sqt", "cb"], writes=["pbR2"])
                        P.op("act", lambda e, n=n: e.activation(out=rkT[:, 0:n], in_=pb["R2"][:, 0:n], func=AF.Ln,
                                                                bias=epsc[:, 0:1], scale=1.0), reads=["pbR2", "epsc"], writes=["rkT"])
                        P.op("act", lambda e, n=n: e.activation(out=rkT[:, 0:n], in_=rkT[:, 0:n], func=AF.Exp, scale=-0.5),
                             reads=["rkT"], writes=["rkT"])
                        if nm == "k":
                            P.op("dve", lambda e, cT_=cT_, o0=o0, o1=o1, n=n: e.tensor_tensor(
                                out=cT_[:, o0:o1], in0=cT_[:, o0:o1], in1=rkT[:, 0:n], op=ALU.mult),
                                reads=["cT%s%d" % (nm, par), "rkT"], writes=["cT%s%d" % (nm, par)])
                        else:
                            P.op("dve", lambda e, cT_=cT_, o0=o0, o1=o1, n=n: e.scalar_tensor_tensor(
                                out=cT_[:, o0:o1], in0=cT_[:, o0:o1], scalar=128.0 ** -0.5, in1=rkT[:, 0:n],
                                op0=ALU.mult, op1=ALU.mult), reads=["cT%s%d" % (nm, par), "rkT"], writes=["cT%s%d" % (nm, par)])
                        yield
            def prepB(p, h, par, out):
                cTk, cTv, cTq = cTk2[par], cTv2[par], cTq2[par]
                kck, kcv, kcq = "cTk%d" % par, "cTv%d" % par, "cTq%d" % par
                gall, ball, kga, kba = gall2[p % 2], ball2[p % 2], "gall%d" % (p % 2), "ball%d" % (p % 2)
                v4 = lambda ps: ps[:, :].rearrange("p (g f) -> p g f", f=128)
                PK, PG = pb["K"], pb["G"]

                def chain(G):
                    T0 = 4 * G
                    tc = lambda g: slice((T0 + g) * 128, (T0 + g + 1) * 128)
                    kG = "_%d" % G
                    gsl = gall[:, T0:T0 + 4, h:h + 1].to_broadcast([128, 4, 128])
                    bsl = ball[:, T0:T0 + 4, h:h + 1].to_broadcast([128, 4, 128])
                    P.op("dve", lambda e: e.tensor_tensor(out=rhs_g[:, :, :], in0=bc4(cf[:, C_MLOW, :]), in1=gsl, op=ALU.mult),
                         reads=["cf", kga], writes=["rhs_g"])
                    for g in range(4):
                        P.op("pe", lambda e, g=g: e.matmul(out=PG[:, g * 128:(g + 1) * 128], lhsT=ones_f, rhs=rhs_g[:, g, :],
                                                           start=True, stop=True), reads=["cf", "rhs_g"], writes=["pbG"])
                    for g in range(4):
                        P.op("pe", lambda e, g=g: e.matmul(out=PK[:, g:g + 1], lhsT=rhs_g[:, g, :], rhs=cf[:, C_ONE, 0:1],
                                                           start=True, stop=True), reads=["cf", "rhs_g"], writes=["pbK"])
                    P.op("dve", lambda e: e.tensor_copy(out=Gc[G][:, :], in_=PK[:, 0:4]), reads=["pbK"], writes=["Gc" + kG])
                    P.op("dve", lambda e: e.tensor_tensor(out=GcM[:, :, :], in0=bc4(cf[:, C_NEG, :]),
                                                          in1=Gc[G][:, :].unsqueeze(2).to_broadcast([128, 4, 128]), op=ALU.add),
                         reads=["cf", "Gc" + kG], writes=["GcM"])
                    P.op("dve", lambda e: e.scalar_tensor_tensor(out=dm[:, :, :], in0=v4(PG), scalar=-1.0, in1=GcM[:, :, :],
                                                                 op0=ALU.mult, op1=ALU.add), reads=["pbG", "GcM"], writes=["rhs_g"])
                    P.op("act", lambda e: e.activation(out=Dm[:, :, :], in_=dm[:, :, :], func=AF.Exp), reads=["rhs_g"], writes=["Dm"])
                    if out:
                        P.op("act", lambda e: e.activation(out=eGR[:, :, :], in_=v4(PG), func=AF.Exp), reads=["pbG"], writes=["eGR"])
                    P.op("act", lambda e: e.activation(out=eGc[G][:, :], in_=Gc[G][:, :], func=AF.Exp), reads=["Gc" + kG], writes=["eGc" + kG])
                    P.op("dve", lambda e: e.tensor_tensor(out=kdf[G][:, :], in0=v4(PG)[:, :, 127], in1=Gc[G][:, :], op=ALU.subtract),
                         reads=["pbG", "Gc" + kG], writes=["kdf" + kG])
                    P.op("act", lambda e: e.activation(out=kdf[G][:, :], in_=kdf[G][:, :], func=AF.Exp), reads=["kdf" + kG], writes=["kdf" + kG])
                    P.op("act", lambda e: e.activation(out=gl_all[par][:, T0:T0 + 4], in_=v4(PG)[:, :, 127], func=AF.Exp),
                         reads=["pbG"], writes=["gl%d" % par])
                    P.op("dve", lambda e: e.tensor_tensor(out=bM[:, :, :], in0=bc4(cb[:, C_STRICT, :]), in1=bsl, op=ALU.mult),
                         reads=["cb", kba], writes=["bM"])
                    P.op("dve", lambda e: e.tensor_tensor(out=Dp[:, :, :], in0=Dm[:, :, :], in1=bM[:, :, :], op=ALU.mult),
                         reads=["Dm", "bM"], writes=["Dp"])
                    for g in range(4):
                        P.op("pe", lambda e, g=g: e.matmul(out=PK[:, g * 128:(g + 1) * 128], lhsT=cTk[:, tc(g)], rhs=cTk[:, tc(g)],
                                                           start=True, stop=True), reads=[kck], writes=["pbK"])
                    P.op("dve", lambda e: e.tensor_tensor(out=Lm[G][:, :, :], in0=v4(PK), in1=Dp[:, :, :], op=ALU.mult),
                         reads=["pbK", "Dp"], writes=["Lm" + kG])
                    if out:
                        for g in range(4):
                            P.op("pe", lambda e, g=g: e.matmul(out=PK[:, g * 128:(g + 1) * 128], lhsT=cTq[:, tc(g)], rhs=cTk[:, tc(g)],
                                                               start=True, stop=True), reads=[kcq, kck], writes=["pbK"])
                        P.op("dve", lambda e: e.tensor_tensor(out=Am[:, :, :], in0=v4(PK), in1=Dm[:, :, :], op=ALU.mult),
                             reads=["pbK", "Dm"], writes=["Am"])
                        for g in range(4):
                            P.op("pe", lambda e, g=g: e.transpose(out=pbT[:, 4 + g, :], in_=Am[:, g, :], identity=ident_bf),
                                 reads=["Am", "cb"], writes=["pbT"])
                        P.op("act", lambda e: e.copy(out=AT_all[par][:, T0:T0 + 4, :], in_=pbT[:, 4:8, :]),
                             reads=["pbT"], writes=["AT%d" % par])
                        P.op("dve", lambda e: e.tensor_tensor(
                            out=qdT_all[par][:, T0:T0 + 4, :], in0=cTq[:, T0 * 128:(T0 + 4) * 128].rearrange("p (g f) -> p g f", f=128),
                            in1=eGR[:, :, :], op=ALU.mult), reads=[kcq, "eGR"], writes=["qdT%d" % par])
                        yield
                    b1, b2 = (pb["Y"], pb["B"]) if G % 2 == 0 else (pb["G"], pb["K"])
                    n1, n2 = ("pbY", "pbB") if G % 2 == 0 else ("pbG", "pbK")
                    X, XT = Xb[G][0], XTb[G][0]
                    kx, kxt = "X%d_0" % G, "XT%d_0" % G
                    for li, s in enumerate(LEVELS):
                        P.op("dve", lambda e, li=li: e.tensor_tensor(out=Lo[G][:, :, :], in0=Lm[G][:, :, :], in1=bc4(cb[:, C_MM + li, :]), op=ALU.mult),
                             reads=["Lm" + kG, "cb"], writes=["Lo" + kG])
                        if s == 1:
                            P.op("dve", lambda e: e.scalar_tensor_tensor(out=X[:, :, :], in0=Lo[G][:, :, :], scalar=-1.0, in1=bc4(ident_bf),
                                                                         op0=ALU.mult, op1=ALU.add), reads=["Lo" + kG, "cb"], writes=[kx])
                            h0 = 4 * (G % 2)
                            for g in range(4):
                                P.op("pe", lambda e, g=g: e.transpose(out=pbT[:, h0 + g, :], in_=X[:, g, :], identity=ident_bf),
                                     reads=[kx, "cb"], writes=["pbT"])
                            P.op("act", lambda e: e.copy(out=XT[:, :, :], in_=pbT[:, h0:h0 + 4, :]), reads=["pbT"], writes=[kxt])
                            yield
                            continue
                        for g in range(4):
                            P.op("pe", lambda e, g=g: e.matmul(out=b2[:, g * 128:(g + 1) * 128], lhsT=Lo[G][:, g, :], rhs=XT[:, g, :],
                                                               start=True, stop=True), reads=["Lo" + kG, kxt], writes=[n2])
                        P.op("act", lambda e: e.mul(out=Ynp[G][:, :, :], in_=v4(b2), mul=-1.0), reads=[n2], writes=["Ynp" + kG])
                        yield
                        for g in range(4):
                            P.op("pe", lambda e, g=g: e.matmul(out=b1[:, g * 128:(g + 1) * 128], lhsT=X[:, g, :], rhs=Ynp[G][:, g, :],
                                                               start=True, stop=True), reads=[kx, "Ynp" + kG], writes=[n1])
                        P.op("dve", lambda e: e.tensor_tensor(out=XT[:, :, :], in0=v4(b1), in1=XT[:, :, :], op=ALU.add), reads=[n1, kxt], writes=[kxt])
                        if s != LEVELS[-1]:
                            hh = 4 * (G % 2)
                            for g in range(4):
                                P.op("pe", lambda e, g=g: e.transpose(out=pbT[:, hh + g, :], in_=XT[:, g, :], identity=ident_bf),
                                     reads=[kxt, "cb"], writes=["pbT"])
                            P.op("act", lambda e: e.copy(out=X[:, :, :], in_=pbT[:, hh:hh + 4, :]), reads=["pbT"], writes=[kx])
                        yield
                    for g in range(4):
                        P.op("pe", lambda e, g=g: e.transpose(out=pbT[:, g, :], in_=cTk[:, tc(g)], identity=ident_bf),
                             reads=[kck, "cb"], writes=["pbT"])
                    P.op("act", lambda e: e.copy(out=kn_tok[:, :, :], in_=pbT[:, 0:4, :]), reads=["pbT"], writes=["kn_tok"])
                    for g in range(4):
                        P.op("pe", lambda e, g=g: e.transpose(out=pbT[:, 4 + g, :], in_=cTv[:, tc(g)], identity=ident_bf),
                             reads=[kcv, "cb"], writes=["pbT"])
                    P.op("act", lambda e: e.copy(out=v_tok[:, :, :], in_=pbT[:, 4:8, :]), reads=["pbT"], writes=["v_tok"])
                    P.op("dve", lambda e: e.tensor_tensor(out=Rv[:, :, :], in0=v_tok[:, :, :], in1=bsl, op=ALU.mult),
                         reads=["v_tok", kba], writes=["Rv"])
                    P.op("dve", lambda e: e.tensor_tensor(out=bE[G][:, :], in0=ball[:, T0:T0 + 4, h], in1=eGc[G][:, :], op=ALU.mult),
                         reads=[kba, "eGc" + kG], writes=["bE" + kG])
                    P.op("dve", lambda e: e.tensor_tensor(out=Rk[:, :, :], in0=kn_tok[:, :, :],
                                                          in1=bE[G][:, :].unsqueeze(2).to_broadcast([128, 4, 128]), op=ALU.mult),
                         reads=["kn_tok", "bE" + kG], writes=["Rk"])
                    for g in range(4):
                        P.op("pe", lambda e, g=g, XT=XT: e.matmul(out=b1[:, g * 128:(g + 1) * 128], lhsT=XT[:, g, :], rhs=Rv[:, g, :],
                                                                  start=True, stop=True), reads=[kxt, "Rv"], writes=[n1])
                    P.op("act", lambda e: e.copy(out=u_all[par][:, T0:T0 + 4, :], in_=v4(b1)), reads=[n1], writes=["u%d" % par])
                    for g in range(4):
                        P.op("pe", lambda e, g=g, XT=XT: e.matmul(out=b2[:, g * 128:(g + 1) * 128], lhsT=Rk[:, g, :], rhs=XT[:, g, :],
                                                                  start=True, stop=True), reads=[kxt, "Rk"], writes=[n2])
                    P.op("act", lambda e: e.copy(out=wT_all[par][:, T0:T0 + 4, :], in_=v4(b2)), reads=[n2], writes=["wT%d" % par])
                    P.op("dve", lambda e: e.tensor_tensor(out=kd_all[par][:, T0:T0 + 4, :], in0=kn_tok[:, :, :],
                                                           in1=kdf[G][:, :].unsqueeze(2).to_broadcast([128, 4, 128]), op=ALU.mult),
                         reads=["kn_tok", "kdf" + kG], writes=["kd%d" % par])

                    yield

                chains = [chain(G) for G in range(NG)]
                while chains:
                    alive = []
                    for c in chains:
                        try:
                            next(c)
                            alive.append(c)
                        except StopIteration:
                            pass
                    chains = alive
                    yield

            def rec(p, h, par, out):
                S_h, Sf_h, Sb_h = Sst[:, h, :], Sf[:, h, :], Sb[:, h, :]
                kS, kSf, kSb = "S%d" % h, "Sf%d" % h, "Sb%d" % h
                sbf = Sbf[par]
                ksbf = "Sbf%d" % par
                if p in (1, 2):
                    P.op("dve", lambda e: e.tensor_scalar(out=S_h, in0=S_h, scalar1=flg[:, F_CARRY + p:F_CARRY + p + 1], scalar2=None,
                                                          op0=ALU.mult), reads=[kS, "flg"], writes=[kS])
                elif p == 3:
                    P.op("dve", lambda e: e.tensor_copy(out=S_h, in_=Sb_h), reads=[kSb], writes=[kS])
                elif p == 4:
                    P.op("dve", lambda e: e.tensor_copy(out=S_h, in_=Sf_h), reads=[kSf], writes=[kS])
                P.op("act", lambda e: e.copy(out=sbf[:, :], in_=S_h), reads=[kS], writes=[ksbf])
                for t in range(NT):
                    P.op("pe", lambda e, t=t: e.matmul(out=pb["R1"][:, 0:128], lhsT=wT_all[par][:, t, :], rhs=sbf[:, :], start=True, stop=True),
                         reads=["wT%d" % par, ksbf], writes=["pbR1"])
                    P.op("dve", lambda e, t=t: e.tensor_tensor(out=vnew[:, :], in0=u_all[par][:, t, :], in1=pb["R1"][:, 0:128], op=ALU.subtract),
                         reads=["u%d" % par, "pbR1"], writes=["vnew"])
                    if out:
                        ob = osb[t % 2]
                        P.op("pe", lambda e, t=t: e.matmul(out=pb["R1"][:, 256:384], lhsT=qdT_all[par][:, t, :], rhs=sbf[:, :], start=True, stop=False),
                             reads=["qdT%d" % par, ksbf], writes=["pbR1"])
                        P.op("pe", lambda e, t=t: e.matmul(out=pb["R1"][:, 256:384], lhsT=AT_all[par][:, t, :], rhs=vnew[:, :], start=False, stop=True),
                             reads=["AT%d" % par, "vnew"], writes=["pbR1"])
                        P.op("act", lambda e, ob=ob: e.copy(out=ob[:, :], in_=pb["R1"][:, 256:384]), reads=["pbR1"], writes=["osb%d" % (t % 2)])
                        P.op("sp", lambda e, ob=ob, t=t: e.dma_start(out=oscr[p - 3, t * 128:(t + 1) * 128, h * 128:(h + 1) * 128], in_=ob[:, :]),
                             reads=["osb%d" % (t % 2)], writes=["oscr%d_%d_%d" % (p - 3, t, h)], dma="osb%d" % (t % 2))
                    P.op("pe", lambda e, t=t: e.matmul(out=pb["R1"][:, 128:256], lhsT=kd_all[par][:, t, :], rhs=vnew[:, :], start=True, stop=True),
                         reads=["kd%d" % par, "vnew"], writes=["pbR1"])
                    P.op("dve", lambda e, t=t: e.scalar_tensor_tensor(out=S_h, in0=S_h, scalar=gl_all[par][:, t:t + 1], in1=pb["R1"][:, 128:256],
                                                                      op0=ALU.mult, op1=ALU.add), reads=[kS, "gl%d" % par, "pbR1"], writes=[kS])
                    P.op("act", lambda e: e.copy(out=sbf[:, :], in_=S_h), reads=[kS], writes=[ksbf])
                    yield
                if p < 3:
                    P.op("dve", lambda e: e.scalar_tensor_tensor(out=Sf_h, in0=S_h, scalar=flg[:, F_F + p:F_F + p + 1], in1=Sf_h,
                                                                 op0=ALU.mult, op1=ALU.add), reads=[kS, kSf, "flg"], writes=[kSf])
                    P.op("dve", lambda e: e.scalar_tensor_tensor(out=Sb_h, in0=S_h, scalar=flg[:, F_B + p:F_B + p + 1], in1=Sb_h,
                                                                 op0=ALU.mult, op1=ALU.add), reads=[kS, kSb, "flg"], writes=[kSb])
                yield

            def prepA_task(p, h, par, out):
                if h == 0:
                    yield from pass_setup(p)
                yield from prepA(p, h, par, out)

            seq = [(p, h, k % 2, p >= 3) for k, (p, h) in enumerate((p, h) for p in range(5) for h in range(NH))]
            n = len(seq)
            for step in range(n + 2):
                gens = []
                if step < n:
                    gens.append(prepA_task(*seq[step]))
                if 0 <= step - 1 < n:
                    gens.append(prepB(*seq[step - 1]))
                if 0 <= step - 2 < n:
                    gens.append(rec(*seq[step - 2]))
                interleave(gens)

            P.barrier(bar_t, cf, pb["R2"], flags_bc[0:1, 0:4])

        with contextlib.ExitStack() as st:
            stash1 = sb(st, "stash1", [128, KD, SEG], BF16)
            stash2 = sb(st, "stash2", [128, KD, SEG], BF16)
            wres_f = sb(st, "wres_f", [128, DM], F32)
            wres = sb(st, "wres", [128, KD, DM], BF16)
            cwa = sb(st, "cwa", [128, KD, 3], F32)
            dgA = sb(st, "dgA", [128, 3, 128], BF16)
            pA = sb(st, "pA", [128, SEG + 2], BF16)
            tf1 = sb(st, "tf1", [128, 512], F32)
            tf2 = sb(st, "tf2", [128, 512], F32)
            tf3 = sb(st, "tf3", [128, 512], F32)
            gnw = sb(st, "gnw", [128, 128], F32)
            fnw = sb(st, "fnw", [128, DM], F32)
            ofw = sb(st, "ofw", [128, DM], F32)
            obw = sb(st, "obw", [128, DM], F32)
            osum = sb(st, "osum", [128, DM], F32)
            sq = sb(st, "sq", [128, DM], F32)
            ssh = sb(st, "ssh", [128, NH], F32)
            ybp = sb(st, "ybp", [128, DM], BF16)
            xr = sb(st, "xr", [128, DM], F32)
            yo = ofw
            first_reads = []
            if debug:
                print("main-phase SBUF bytes remaining/partition:", nc.sbuf_bytes_remaining)
            P.op("sp", lambda e: e.dma_start(out=cwa[:, :, :].rearrange("p a b -> p (a b)"), in_=cwa_fm[:, :]), reads=first_reads, writes=["cwa"], dma="cwa")
            P.op("sp", lambda e: e.dma_start(out=gnw[:, :], in_=gnw_bc[:, :]), reads=first_reads, writes=["gnw"], dma="gnw")
            P.op("sp", lambda e: e.dma_start(out=fnw[:, :], in_=fnw_bc[:, :]), reads=first_reads, writes=["fnw"], dma="fnw")
            hs = lambda d, o0, o1: hT[:, d, 2 + o0:2 + o1]

            def proj(bank, slot, rhs_fn, n):
                for d in range(KD):
                    P.op("pe", lambda e, d=d: e.matmul(out=pb[bank][:, 0:n], lhsT=wbf[slot][:, d, :], rhs=rhs_fn(d),
                                                       start=(d == 0), stop=(d == KD - 1)), reads=["wbf%d" % slot, "hT"], writes=["pb" + bank])

            for fc in range(KD):
                for slot, coff in enumerate((cfg.C_BG, cfg.C_AX, cfg.C_CG, cfg.C_AZ)):
                    load_w_chunk(slot, w_in[:, coff + fc * 128:coff + (fc + 1) * 128], cast_eng="act" if slot % 2 else "dve")
                for tap in range(3):
                    P.op("dve", lambda e, tap=tap: e.tensor_scalar(out=dgA[:, tap, :], in0=ident_bf, scalar1=cwa[:, fc, tap:tap + 1], scalar2=None,
                                                                   op0=ALU.mult), reads=["cb", "cwa"], writes=["dgA"])
                for (c0, c1) in blocks(SEG + 2):
                    n = c1 - c0
                    proj("A", 0, lambda d, c0=c0, c1=c1: hT[:, d, 1 + c0:1 + c1], n)
                    proj("B", 1, lambda d, c0=c0, c1=c1: hT[:, d, 1 + c0:1 + c1], n)
                    P.op("act", lambda e, n=n: e.copy(out=tf1[:, 0:n], in_=pb["A"][:, 0:n]), reads=["pbA"], writes=["tf1"])
                    P.op("dve", lambda e, c0=c0, c1=c1, n=n: e.tensor_tensor(out=pA[:, c0:c1], in0=tf1[:, 0:n], in1=pb["B"][:, 0:n], op=ALU.mult),
                         reads=["tf1", "pbB"], writes=["pA"])
                P.op("dve", lambda e: e.tensor_scalar(out=pA[:, 0:1], in0=pA[:, 0:1], scalar1=flg[:, F_LO(4):F_LO(4) + 1], scalar2=None, op0=ALU.mult),
                     reads=["pA", "flg"], writes=["pA"])
                P.op("dve", lambda e: e.tensor_scalar(out=pA[:, SEG + 1:SEG + 2], in0=pA[:, SEG + 1:SEG + 2], scalar1=flg[:, F_HI(4):F_HI(4) + 1],
                                                      scalar2=None, op0=ALU.mult), reads=["pA", "flg"], writes=["pA"])
                for (o0, o1) in blocks(SEG):
                    n = o1 - o0
                    for tap in range(3):
                        P.op("pe", lambda e, tap=tap, o0=o0, o1=o1, n=n: e.matmul(out=pb["K"][:, 0:n], lhsT=dgA[:, tap, :], rhs=pA[:, o0 + tap:o1 + tap],
                                                                                 start=(tap == 0), stop=(tap == 2)), reads=["dgA", "pA"], writes=["pbK"])
                    proj("A", 2, lambda d, o0=o0, o1=o1: hs(d, o0, o1), n)
                    proj("B", 3, lambda d, o0=o0, o1=o1: hs(d, o0, o1), n)
                    P.op("act", lambda e, n=n: e.activation(out=tf2[:, 0:n], in_=pb["B"][:, 0:n], func=AF.Silu), reads=["pbB"], writes=["tf2"])
                    P.op("act", lambda e, n=n: e.copy(out=tf1[:, 0:n], in_=pb["A"][:, 0:n]), reads=["pbA"], writes=["tf1"])
                    P.op("dve", lambda e, n=n: e.tensor_tensor(out=tf3[:, 0:n], in0=tf1[:, 0:n], in1=pb["K"][:, 0:n], op=ALU.mult),
                         reads=["tf1", "pbK"], writes=["tf3"])
                    P.op("dve", lambda e, o0=o0, o1=o1, n=n: e.tensor_tensor(out=stash1[:, fc, o0:o1], in0=tf3[:, 0:n], in1=tf2[:, 0:n], op=ALU.mult),
                         reads=["tf3", "tf2"], writes=["stash1"])

            def branch_out(wmat, gate_off, accumulate):
                for oc in range(KD):
                    load_w_chunk(0, wmat[:, oc * 128:(oc + 1) * 128])
                    load_w_chunk(1, w_in[:, gate_off + oc * 128:gate_off + (oc + 1) * 128], cast_eng="act")
                    for (o0, o1) in blocks(SEG):
                        n = o1 - o0
                        for fc in range(KD):
                            P.op("pe", lambda e, fc=fc, o0=o0, o1=o1, n=n: e.matmul(out=pb["A"][:, 0:n], lhsT=wbf[0][:, fc, :], rhs=stash1[:, fc, o0:o1],
                                                                                   start=(fc == 0), stop=(fc == KD - 1)), reads=["wbf0", "stash1"], writes=["pbA"])
                        proj("B", 1, lambda d, o0=o0, o1=o1: hs(d, o0, o1), n)
                        P.op("act", lambda e, n=n: e.activation(out=tf2[:, 0:n], in_=pb["B"][:, 0:n], func=AF.Sigmoid), reads=["pbB"], writes=["tf2"])
                        if not accumulate:
                            P.op("dve", lambda e, oc=oc, o0=o0, o1=o1, n=n: e.tensor_tensor(out=stash2[:, oc, o0:o1], in0=tf2[:, 0:n], in1=pb["A"][:, 0:n], op=ALU.mult),
                                 reads=["tf2", "pbA"], writes=["stash2"])
                        else:
                            P.op("dve", lambda e, n=n: e.tensor_tensor(out=tf3[:, 0:n], in0=tf2[:, 0:n], in1=pb["A"][:, 0:n], op=ALU.mult),
                                 reads=["tf2", "pbA"], writes=["tf3"])
                            P.op("dve", lambda e, oc=oc, o0=o0, o1=o1, n=n: e.tensor_tensor(out=stash2[:, oc, o0:o1], in0=stash2[:, oc, o0:o1], in1=tf3[:, 0:n], op=ALU.add),
                                 reads=["tf3", "stash2"], writes=["stash2"])

            branch_out(w_pa, cfg.C_GA, False)

            def load_wres(src_ap):
                for d in range(KD):
                    P.op("sp", lambda e, d=d: e.dma_start(out=wres_f[:, :], in_=src_ap[d * 128:(d + 1) * 128, :]), writes=["wres_f"], dma="wres_f")
                    P.op("act" if d % 2 else "dve", lambda e, d=d: (e.copy(out=wres[:, d, :], in_=wres_f[:, :]) if d % 2 else e.tensor_copy(out=wres[:, d, :], in_=wres_f[:, :])), reads=["wres_f"], writes=["wres"])

            load_wres(w_in[:, cfg.C_Z:cfg.C_Z + DM])
            for t in range(NT):
                P.op("sp", lambda e, t=t: e.dma_start(out=ofw[:, :], in_=oscr[1, t * 128:(t + 1) * 128, :]), reads=["oscr1_%d_%d" % (t, hh) for hh in range(NH)], writes=["ofw"], dma="ofw")
                P.op("sp", lambda e, t=t: e.dma_start(out=obw[:, :], in_=oscr[0, (NT - 1 - t) * 128:(NT - t) * 128, :]), reads=["oscr0_%d_%d" % (NT - 1 - t, hh) for hh in range(NH)], writes=["obw"], dma="obw")
                for i, (c0, c1) in enumerate(blocks(DM)):
                    bk = "A" if i % 2 == 0 else "B"
                    P.op("pe", lambda e, bk=bk, c0=c0, c1=c1: e.matmul(out=pb[bk][:, 0:c1 - c0], lhsT=cf[:, C_J, :], rhs=obw[:, c0:c1], start=True, stop=True),
                         reads=["cf", "obw"], writes=["pb" + bk])
                    P.op("dve", lambda e, bk=bk, c0=c0, c1=c1: e.tensor_tensor(out=osum[:, c0:c1], in0=pb[bk][:, 0:c1 - c0], in1=ofw[:, c0:c1], op=ALU.add),
                         reads=["pb" + bk, "ofw"], writes=["osum"])
                P.op("act", lambda e: e.activation(out=sq[:, :], in_=osum[:, :], func=AF.Square), reads=["osum"], writes=["sq"])
                P.op("dve", lambda e: e.tensor_reduce(out=ssh[:, :], in_=sq[:, :].rearrange("p (h f) -> p h f", f=128), axis=AX.X, op=ALU.add),
                     reads=["sq"], writes=["ssh"])
                P.op("dve", lambda e: e.tensor_scalar(out=ssh[:, :], in0=ssh[:, :], scalar1=1.0 / 128, scalar2=1e-6, op0=ALU.mult, op1=ALU.add),
                     reads=["ssh"], writes=["ssh"])
                P.op("act", lambda e: e.sqrt(out=ssh[:, :], in_=ssh[:, :]), reads=["ssh"], writes=["ssh"])
                P.op("dve", lambda e: e.reciprocal(out=ssh[:, :], in_=ssh[:, :]), reads=["ssh"], writes=["ssh"])
                o3 = osum[:, :].rearrange("p (h f) -> p h f", f=128)
                P.op("dve", lambda e: e.tensor_tensor(out=o3, in0=o3, in1=ssh[:, :].unsqueeze(2).to_broadcast([128, NH, 128]), op=ALU.mult),
                     reads=["osum", "ssh"], writes=["osum"])
                P.op("dve", lambda e: e.tensor_tensor(out=o3, in0=o3, in1=gnw[:, :].unsqueeze(1).to_broadcast([128, NH, 128]), op=ALU.mult),
                     reads=["osum", "gnw"], writes=["osum"])
                for i, (c0, c1) in enumerate(blocks(DM)):
                    bk = "K" if i % 2 == 0 else "Y"
                    for d in range(KD):
                        P.op("pe", lambda e, bk=bk, d=d, c0=c0, c1=c1: e.matmul(out=pb[bk][:, 0:c1 - c0], lhsT=hs(d, t * 128, (t + 1) * 128), rhs=wres[:, d, c0:c1],
                                                                               start=(d == 0), stop=(d == KD - 1)), reads=["hT", "wres"], writes=["pb" + bk])
                    P.op("act", lambda e, bk=bk, c0=c0, c1=c1: e.activation(out=tf2[:, 0:c1 - c0], in_=pb[bk][:, 0:c1 - c0], func=AF.Silu), reads=["pb" + bk], writes=["tf2"])
                    P.op("dve", lambda e, c0=c0, c1=c1: e.tensor_tensor(out=ybp[:, c0:c1], in0=osum[:, c0:c1], in1=tf2[:, 0:c1 - c0], op=ALU.mult),
                         reads=["osum", "tf2"], writes=["ybp"])
                for fc in range(KD):
                    P.op("pe", lambda e, fc=fc: e.transpose(out=pbT[:, fc, :], in_=ybp[:, fc * 128:(fc + 1) * 128], identity=ident_bf),
                         reads=["ybp", "cb"], writes=["pbT"])
                P.op("act", lambda e, t=t: e.copy(out=stash1[:, :, t * 128:(t + 1) * 128], in_=pbT[:, 0:KD, :]), reads=["pbT"], writes=["stash1"])

            branch_out(w_pb, cfg.C_GB, True)

            load_wres(w_o[:, :])
            fin_ops = []
            for t in range(NT):
                P.op("sp", lambda e, t=t: e.dma_start(out=xr[:, :], in_=xs[4, 2 + t * 128:2 + (t + 1) * 128, :]), writes=["xr"], dma="xr")
                for i, (c0, c1) in enumerate(blocks(DM)):
                    bk = "A" if i % 2 == 0 else "B"
                    for fc in range(KD):
                        P.op("pe", lambda e, bk=bk, fc=fc, c0=c0, c1=c1: e.matmul(out=pb[bk][:, 0:c1 - c0], lhsT=stash2[:, fc, t * 128:(t + 1) * 128], rhs=wres[:, fc, c0:c1],
                                                                                 start=(fc == 0), stop=(fc == KD - 1)), reads=["stash2", "wres"], writes=["pb" + bk])
                    P.op("dve", lambda e, bk=bk, c0=c0, c1=c1: e.tensor_tensor(out=osum[:, c0:c1], in0=pb[bk][:, 0:c1 - c0], in1=gate_bc[:, c0:c1], op=ALU.mult),
                         reads=["pb" + bk, "gate_bc"], writes=["osum"])
                P.op("dve", lambda e: e.tensor_tensor(out=osum[:, :], in0=osum[:, :], in1=xr[:, :], op=ALU.add), reads=["osum", "xr"], writes=["osum"])
                P.op("act", lambda e: e.activation(out=sq[:, :], in_=osum[:, :], func=AF.Square, accum_out=ss[:, :]), reads=["osum"], writes=["sq", "ss"])
                P.op("dve", lambda e: e.tensor_scalar(out=rstd[:, :], in0=ss[:, :], scalar1=1.0 / DM, scalar2=1e-6, op0=ALU.mult, op1=ALU.add),
                     reads=["ss"], writes=["rstd"])
                P.op("act", lambda e: e.sqrt(out=rstd[:, :], in_=rstd[:, :]), reads=["rstd"], writes=["rstd"])
                P.op("dve", lambda e: e.reciprocal(out=rstd[:, :], in_=rstd[:, :]), reads=["rstd"], writes=["rstd"])
                P.op("dve", lambda e: e.scalar_tensor_tensor(out=yo[:, :], in0=osum[:, :], scalar=rstd[:, 0:1], in1=fnw[:, :], op0=ALU.mult, op1=ALU.mult),
                     reads=["osum", "rstd", "fnw"], writes=["ofw"])
                fin_ops.append(P.op("sp", lambda e, t=t: e.dma_start(out=y[t * 128:(t + 1) * 128, :], in_=yo[:, :]), reads=["ofw"], writes=["y%d" % t], dma="yo"))
            P.emit(final_wait_ops=fin_ops)
            if debug:
                print("scheduler estimate (us):", P.est_ns / 1e3)
    return nc


def make_consts():
    c = np.zeros((NCONST, 128, 128), np.float32)
    p = np.arange(128)[:, None]
    f = np.arange(128)[None, :]
    c[C_ID] = (p == f)
    c[C_ONE] = 1.0
    c[C_MLOW] = (p <= f)
    c[C_NEG] = np.where(p >= f, 0.0, NEG)
    c[C_STRICT] = (p > f)
    c[C_J] = (p + f == 127)
    for li, s in enumerate(LEVELS):
        m = ((p // (2 * s)) == (f // (2 * s))) & ((p % (2 * s)) >= s) & ((f % (2 * s)) < s)
        c[C_MM + li] = m
        c[C_MMT + li] = m.T
    return np.ascontiguousarray(c.transpose(1, 0, 2).reshape(128, NCONST * 128))


def fm(v, k):
    return np.ascontiguousarray(np.asarray(v, np.float32).reshape(k, 128).T)


def rep(v):
    v = np.asarray(v, np.float32).reshape(1, -1)
    return np.ascontiguousarray(np.broadcast_to(v, (128, v.shape[1])))


def host_prep(cfg, inputs, cores=None):
    DM, NH, KD, SEG, SEQ = cfg.DM, cfg.NH, cfg.KD, cfg.SEG, cfg.SEQ
    x = np.asarray(inputs["x"], np.float32)
    w_in = np.ascontiguousarray(np.asarray(inputs["w_in"], np.float32)[0])
    conv_qkv = np.asarray(inputs["conv_qkv_w"], np.float32)[0]
    a_log = np.asarray(inputs["a_log"], np.float32)[0]
    dt_bias = np.asarray(inputs["dt_bias"], np.float32)[0]
    consts = make_consts()
    common = {
        "normw_fm": fm(inputs["norm_w"][0], KD),
        "w_ada": np.ascontiguousarray(np.asarray(inputs["w_ada"], np.float32)[0]),
        "b_ada": np.ascontiguousarray(np.asarray(inputs["b_ada"], np.float32)[0].reshape(1, -1)),
        "w_in": w_in,
        "cwa_fm": np.ascontiguousarray(np.asarray(inputs["conv_a_w"], np.float32)[0].T.reshape(KD, 128, 3).transpose(1, 0, 2).reshape(128, KD * 3)),
        "constm": consts,
        "gnw_bc": rep(inputs["gdn_norm_w"][0]),
        "fnw_bc": rep(inputs["final_norm_w"]),
        "w_pa": np.ascontiguousarray(np.asarray(inputs["w_pa"], np.float32)[0]),
        "w_pb": np.ascontiguousarray(np.asarray(inputs["w_pb"], np.float32)[0]),
        "w_o": np.ascontiguousarray(np.asarray(inputs["w_o"], np.float32)[0]),
    }

    def seg_halo(b, i, rev):
        lo, hi = i * SEG - 2, (i + 1) * SEG + 2
        out = np.zeros((SEG + 4, DM), np.float32)
        a, bnd = max(lo, 0), min(hi, SEQ)
        out[a - lo:a - lo + (bnd - a)] = x[b, a:bnd]
        flo, fhi = float(i > 0), float(i < 3)
        if rev:
            out = out[::-1]
            flo, fhi = fhi, flo
        return out, flo, fhi

    in_maps = []
    cores = range(8) if cores is None else cores
    for core in cores:
        b, j = divmod(core, 4)
        plist = [(i, 0) for i in range(j)] + [(i, 1) for i in range(3, j, -1)] + [(j, 1), (j, 0)]
        xs = np.zeros((5, SEG + 4, DM), np.float32)
        flags = np.zeros((32,), np.float32)
        wab = np.zeros((5, DM, 2 * NH), np.float32)
        cw = np.zeros((5, 3 * NH, 128, 5), np.float32)
        alog = np.zeros((5, NH), np.float32)
        dtb = np.zeros((5, NH), np.float32)
        for p, (i, dr) in enumerate(plist):
            xs[p], flags[2 * p], flags[2 * p + 1] = seg_halo(b, i, dr == 1)
            wab[p, :, :NH] = w_in[:, cfg.C_A + dr * NH:cfg.C_A + (dr + 1) * NH]
            wab[p, :, NH:] = w_in[:, cfg.C_B + dr * NH:cfg.C_B + (dr + 1) * NH]
            taps = conv_qkv[::-1] if dr == 1 else conv_qkv
            cw[p] = taps.T.reshape(3 * NH, 128, 5)
            alog[p] = a_log[dr]
            dtb[p] = dt_bias[dr]
        for s in range(1, 3):
            flags[10 + s] = float(plist[s][1] == plist[s - 1][1])
        for s in range(3):
            nxt_dir = plist[s + 1][1] if s < 2 else None
            if plist[s][1] == 0 and (s == 2 or nxt_dir == 1):
                flags[13 + s] = 1.0
            if plist[s][1] == 1 and s == 2:
                flags[16 + s] = 1.0
        m = dict(common)
        m.update({
            "xs": xs,
            "c_fm": fm(inputs["c"][b], KD),
            "w_ab": wab,
            "cw_fm": np.ascontiguousarray(cw.transpose(2, 0, 1, 3).reshape(128, 5 * 3 * NH * 5)),
            "alog_bc": rep(alog.reshape(-1)),
            "dtb_bc": rep(dtb.reshape(-1)),
            "flags_bc": np.ascontiguousarray(np.broadcast_to(flags.reshape(1, 32), (128, 32))),
        })
        in_maps.append(m)
    return in_maps


_CACHE = {}


def kernel(**inputs):
    cfg = Cfg()
    if "nc" not in _CACHE:
        _CACHE["nc"] = build_program(cfg)
    nc = _CACHE["nc"]
    in_maps = host_prep(cfg, inputs)
    res = run_bass_kernel_spmd(nc, in_maps, core_ids=list(range(8)))
    out = np.zeros((2, cfg.SEQ, cfg.DM), np.float32)
    for core in range(8):
        b, j = divmod(core, 4)
        out[b, j * cfg.SEG:(j + 1) * cfg.SEG] = np.asarray(res.results[core]["y"], np.float32)
    return out
```
